# Optimizing a Trainium2 kernel written in Bass

```python
import math
import jax
import jax.numpy as jnp
from jax import lax
import numpy as np


D_MODEL = 1024
BATCH = 4
SEQ = 4096
DEPTH = 2
DEC_BATCH = 32
DEC_SEQ = 1
PAST_LEN = 8192
PAGE_SIZE = 128

D_FF = 2816
CONV_CH = D_MODEL // 2
CONV_A_WIDTH = 31
LRU_CH = D_MODEL // 2
LRU_BLOCKS = 8
LRU_BLOCK = LRU_CH // LRU_BLOCKS
CONV_B_WIDTH = 4
LRU_C = 8.0
ATT_HEADS = 8
ATT_HD = 64
ATT_VD = 2 * ATT_HD
N_MEM = 256
XATT_HEADS = 4
XATT_HD = D_MODEL // XATT_HEADS
Q_BLOCK = 128
N_EVEN = (DEPTH + 1) // 2
N_ODD = DEPTH // 2
EPS = 1e-6
NEG_INF = -1e30

kernel_name = 'hybrid_conformer_rglru_diffattn_step'


def rms_norm(x, g):
    xf = x.astype(jnp.float32)
    y = xf * lax.rsqrt(jnp.mean(xf * xf, axis=-1, keepdims=True) + EPS)
    return (y * g.astype(jnp.float32)).astype(x.dtype)


def layer_norm(x, g, b):
    xf = x.astype(jnp.float32)
    mu = jnp.mean(xf, axis=-1, keepdims=True)
    var = jnp.mean(jnp.square(xf - mu), axis=-1, keepdims=True)
    y = (xf - mu) * lax.rsqrt(var + EPS) * g.astype(jnp.float32) + b.astype(jnp.float32)
    return y.astype(x.dtype)


def swiglu(x, w_in, w_out):
    g, u = jnp.split(x @ w_in, 2, axis=-1)
    return (jax.nn.silu(g) * u) @ w_out


def dwconv_valid(x_ext, w, b):
    y = lax.conv_general_dilated(x_ext, w[:, None, :].astype(x_ext.dtype), window_strides=(1,), padding='VALID', dimension_numbers=('NWC', 'WIO', 'NWC'), feature_group_count=x_ext.shape[-1])
    return y + b


def block_diag(x, w, b):
    bsz, t, c = x.shape
    y = jnp.einsum('btki,kij->btkj', x.reshape(bsz, t, LRU_BLOCKS, LRU_BLOCK), w)
    return y.reshape(bsz, t, c) + b


def rg_lru(x, h0, w_a, b_a, w_x, b_x, lam):
    r = jax.nn.sigmoid(block_diag(x, w_a, b_a).astype(jnp.float32))
    i = jax.nn.sigmoid(block_diag(x, w_x, b_x).astype(jnp.float32))
    log_a = -LRU_C * r * jax.nn.softplus(-lam.astype(jnp.float32))
    a = jnp.exp(log_a)
    beta = jnp.sqrt(jnp.maximum(-jnp.expm1(2.0 * log_a), 0.0))
    u = beta * i * x.astype(jnp.float32)
    u = u.at[:, 0].add(a[:, 0] * h0.astype(jnp.float32))

    def combine(c1, c2):
        return c1[0] * c2[0], c2[0] * c1[1] + c2[1]

    _, h = lax.associative_scan(combine, (a, u), axis=1)
    return h, h[:, -1]


def even_mixer(xn, hist_a, hist_b, h0, p, e):
    z = xn @ p['even_w_in'][e]
    a_val, a_gate, b_rec, b_gate = jnp.split(z, [CONV_CH, 2 * CONV_CH, 2 * CONV_CH + LRU_CH], axis=-1)
    u = a_val * jax.nn.sigmoid(a_gate)
    u_ext = jnp.concatenate([hist_a.astype(u.dtype), u], axis=1)
    ya = dwconv_valid(u_ext, p['conv_a_w'][e], p['conv_a_b'][e])
    ya = jax.nn.silu(layer_norm(ya, p['conv_a_ln_g'][e], p['conv_a_ln_b'][e]))
    r_ext = jnp.concatenate([hist_b.astype(b_rec.dtype), b_rec], axis=1)
    xr = dwconv_valid(r_ext, p['conv_b_w'][e], p['conv_b_b'][e])
    h, h_last = rg_lru(xr, h0, p['lru_w_a'][e], p['lru_b_a'][e], p['lru_w_x'][e], p['lru_b_x'][e], p['lru_lambda'][e])
    yb = h.astype(xn.dtype) * jax.nn.gelu(b_gate)
    out = jnp.concatenate([ya, yb], axis=-1) @ p['even_w_out'][e]
    return out, u_ext[:, -(CONV_A_WIDTH - 1):], r_ext[:, -(CONV_B_WIDTH - 1):], h_last


def diff_attn_block(q, k, v, q_pos, k_pos, lam):
    s = jnp.einsum('bqhmd,bkhmd->bhmqk', q, k).astype(jnp.float32) * (ATT_HD ** -0.5)
    s = jnp.where(k_pos[None, :] <= q_pos[:, None], s, NEG_INF)
    pr = jax.nn.softmax(s, axis=-1)
    w = pr[:, :, 0] - lam * pr[:, :, 1]
    return jnp.einsum('bhqk,bkhe->bqhe', w.astype(v.dtype), v)


def odd_mixer(xn, k_past, v_past, p, o, layer_idx):
    bsz, t, _ = xn.shape
    past = k_past.shape[1]
    q, k, v = jnp.split(xn @ p['attn_w_in'][o], 3, axis=-1)
    q = q.reshape(bsz, t, ATT_HEADS, 2, ATT_HD)
    k_new = k.reshape(bsz, t, ATT_HEADS, 2 * ATT_HD)
    v_new = v.reshape(bsz, t, ATT_HEADS, ATT_VD)
    k_all = jnp.concatenate([k_past.astype(k_new.dtype), k_new], axis=1).reshape(bsz, past + t, ATT_HEADS, 2, ATT_HD)
    v_all = jnp.concatenate([v_past.astype(v_new.dtype), v_new], axis=1)
    lam_init = 0.8 - 0.6 * math.exp(-0.3 * layer_idx)
    f32 = jnp.float32
    lam = (jnp.exp(jnp.sum(p['lam_q1'][o].astype(f32) * p['lam_k1'][o].astype(f32)))
           - jnp.exp(jnp.sum(p['lam_q2'][o].astype(f32) * p['lam_k2'][o].astype(f32))) + lam_init)
    q_pos = past + jnp.arange(t)
    k_pos = jnp.arange(past + t)
    if t > Q_BLOCK and t % Q_BLOCK == 0:
        nb = t // Q_BLOCK
        qb = jnp.moveaxis(q.reshape(bsz, nb, Q_BLOCK, ATT_HEADS, 2, ATT_HD), 1, 0)
        pb = q_pos.reshape(nb, Q_BLOCK)
        ob = lax.map(lambda qp: diff_attn_block(qp[0], k_all, v_all, qp[1], k_pos, lam), (qb, pb))
        att = jnp.moveaxis(ob, 0, 1).reshape(bsz, t, ATT_HEADS, ATT_VD)
    else:
        att = diff_attn_block(q, k_all, v_all, q_pos, k_pos, lam)
    att = rms_norm(att, p['attn_subln_g'][o]) * (1.0 - lam_init)
    out = att.reshape(bsz, t, ATT_HEADS * ATT_VD) @ p['attn_w_out'][o]
    return out, k_new, v_new


def cross_attn(xn, mem_k, mem_v, w_q, w_out):
    bsz, t, _ = xn.shape
    q = (xn @ w_q).reshape(bsz, t, XATT_HEADS, XATT_HD)
    s = jnp.einsum('bqhd,bkhd->bhqk', q, mem_k.astype(q.dtype)).astype(jnp.float32) * (XATT_HD ** -0.5)
    pr = jax.nn.softmax(s, axis=-1)
    o = jnp.einsum('bhqk,bkhd->bqhd', pr.astype(xn.dtype), mem_v.astype(xn.dtype))
    return o.reshape(bsz, t, D_MODEL) @ w_out


def trunk(x, mem_k, mem_v, hist_a, hist_b, lru_h0, k_past, v_past, p):
    conv_a, conv_b, lru_h, k_rows, v_rows = [], [], [], [], []
    for l in range(DEPTH):
        x = x + 0.5 * swiglu(rms_norm(x, p['ffn1_g'][l]), p['ffn1_w_in'][l], p['ffn1_w_out'][l])
        xn = rms_norm(x, p['mix_g'][l])
        if l % 2 == 0:
            e = l // 2
            m, ca, cb, hl = even_mixer(xn, hist_a[e], hist_b[e], lru_h0[e], p, e)
            conv_a.append(ca)
            conv_b.append(cb)
            lru_h.append(hl)
        else:
            o = l // 2
            m, kn, vn = odd_mixer(xn, k_past[o], v_past[o], p, o, l)
            k_rows.append(kn)
            v_rows.append(vn)
        x = x + m
        x = x + cross_attn(rms_norm(x, p['xattn_g'][l]), mem_k[l], mem_v[l], p['xattn_w_q'][l], p['xattn_w_out'][l])
        x = x + 0.5 * swiglu(rms_norm(x, p['ffn2_g'][l]), p['ffn2_w_in'][l], p['ffn2_w_out'][l])
    y = rms_norm(x, p['final_g'])
    return y, jnp.stack(conv_a), jnp.stack(conv_b), jnp.stack(lru_h), jnp.stack(k_rows), jnp.stack(v_rows)


def setup_inputs(seed: int = 0) -> dict:
    key = jax.random.key(seed)
    counter = [0]

    def nk():
        counter[0] += 1
        return jax.random.fold_in(key, counter[0])

    def nrm(shape, scale=1.0):
        return jax.random.normal(nk(), shape, jnp.float32) * scale

    def gain(shape):
        return 1.0 + nrm(shape, 0.05)

    n_pages = PAST_LEN // PAGE_SIZE
    n_used = DEC_BATCH * n_pages
    n_pool = n_used + n_used // 4
    page_table = jax.random.permutation(nk(), n_pool)[:n_used].reshape(DEC_BATCH, n_pages).astype(jnp.int32)
    a_base = jax.random.uniform(nk(), (N_EVEN, LRU_CH), jnp.float32, 0.9, 0.999) ** (1.0 / LRU_C)
    lru_lambda = jnp.log(a_base) - jnp.log1p(-a_base)
    d_mix = CONV_CH + LRU_CH
    return {
        'x_prompt': nrm((BATCH, SEQ, D_MODEL)),
        'x_sample': nrm((DEC_BATCH, DEC_SEQ, D_MODEL)),
        'state_conv_a': nrm((N_EVEN, DEC_BATCH, CONV_A_WIDTH - 1, CONV_CH), 0.5),
        'state_conv_b': nrm((N_EVEN, DEC_BATCH, CONV_B_WIDTH - 1, LRU_CH)),
        'state_lru': nrm((N_EVEN, DEC_BATCH, LRU_CH), 0.5),
        'cache_k': nrm((N_ODD, n_pool, PAGE_SIZE, ATT_HEADS, 2 * ATT_HD)),
        'cache_v': nrm((N_ODD, n_pool, PAGE_SIZE, ATT_HEADS, ATT_VD)),
        'cache_mem_k': nrm((DEPTH, DEC_BATCH, N_MEM, XATT_HEADS, XATT_HD)),
        'cache_mem_v': nrm((DEPTH, DEC_BATCH, N_MEM, XATT_HEADS, XATT_HD)),
        'page_table': page_table,
        'mem_prompt': nrm((BATCH, N_MEM, D_MODEL)),
        'ffn1_g': gain((DEPTH, D_MODEL)),
        'ffn1_w_in': nrm((DEPTH, D_MODEL, 2 * D_FF), D_MODEL ** -0.5),
        'ffn1_w_out': nrm((DEPTH, D_FF, D_MODEL), D_FF ** -0.5),
        'mix_g': gain((DEPTH, D_MODEL)),
        'even_w_in': nrm((N_EVEN, D_MODEL, 2 * d_mix), D_MODEL ** -0.5),
        'conv_a_w': nrm((N_EVEN, CONV_A_WIDTH, CONV_CH), CONV_A_WIDTH ** -0.5),
        'conv_a_b': nrm((N_EVEN, CONV_CH), 0.02),
        'conv_a_ln_g': gain((N_EVEN, CONV_CH)),
        'conv_a_ln_b': nrm((N_EVEN, CONV_CH), 0.02),
        'conv_b_w': nrm((N_EVEN, CONV_B_WIDTH, LRU_CH), CONV_B_WIDTH ** -0.5),
        'conv_b_b': nrm((N_EVEN, LRU_CH), 0.02),
        'lru_w_a': nrm((N_EVEN, LRU_BLOCKS, LRU_BLOCK, LRU_BLOCK), LRU_BLOCK ** -0.5),
        'lru_b_a': nrm((N_EVEN, LRU_CH), 0.02),
        'lru_w_x': nrm((N_EVEN, LRU_BLOCKS, LRU_BLOCK, LRU_BLOCK), LRU_BLOCK ** -0.5),
        'lru_b_x': nrm((N_EVEN, LRU_CH), 0.02),
        'lru_lambda': lru_lambda,
        'even_w_out': nrm((N_EVEN, d_mix, D_MODEL), d_mix ** -0.5),
        'attn_w_in': nrm((N_ODD, D_MODEL, 3 * ATT_HEADS * ATT_VD), D_MODEL ** -0.5),
        'lam_q1': nrm((N_ODD, ATT_HD), 0.1),
        'lam_k1': nrm((N_ODD, ATT_HD), 0.1),
        'lam_q2': nrm((N_ODD, ATT_HD), 0.1),
        'lam_k2': nrm((N_ODD, ATT_HD), 0.1),
        'attn_subln_g': gain((N_ODD, ATT_VD)),
        'attn_w_out': nrm((N_ODD, ATT_HEADS * ATT_VD, D_MODEL), (ATT_HEADS * ATT_VD) ** -0.5),
        'xattn_g': gain((DEPTH, D_MODEL)),
        'xattn_w_q': nrm((DEPTH, D_MODEL, D_MODEL), D_MODEL ** -0.5),
        'xattn_w_kv': nrm((DEPTH, D_MODEL, 2 * D_MODEL), D_MODEL ** -0.5),
        'xattn_w_out': nrm((DEPTH, D_MODEL, D_MODEL), D_MODEL ** -0.5),
        'ffn2_g': gain((DEPTH, D_MODEL)),
        'ffn2_w_in': nrm((DEPTH, D_MODEL, 2 * D_FF), D_MODEL ** -0.5),
        'ffn2_w_out': nrm((DEPTH, D_FF, D_MODEL), D_FF ** -0.5),
        'final_g': gain((D_MODEL,)),
    }


def reference(x_prompt, x_sample, state_conv_a, state_conv_b, state_lru, cache_k, cache_v, cache_mem_k, cache_mem_v, page_table, mem_prompt, ffn1_g, ffn1_w_in, ffn1_w_out, mix_g, even_w_in, conv_a_w, conv_a_b, conv_a_ln_g, conv_a_ln_b, conv_b_w, conv_b_b, lru_w_a, lru_b_a, lru_w_x, lru_b_x, lru_lambda, even_w_out, attn_w_in, lam_q1, lam_k1, lam_q2, lam_k2, attn_subln_g, attn_w_out, xattn_g, xattn_w_q, xattn_w_kv, xattn_w_out, ffn2_g, ffn2_w_in, ffn2_w_out, final_g):
    p = {
        'ffn1_g': ffn1_g, 'ffn1_w_in': ffn1_w_in, 'ffn1_w_out': ffn1_w_out, 'mix_g': mix_g,
        'even_w_in': even_w_in, 'conv_a_w': conv_a_w, 'conv_a_b': conv_a_b,
        'conv_a_ln_g': conv_a_ln_g, 'conv_a_ln_b': conv_a_ln_b, 'conv_b_w': conv_b_w, 'conv_b_b': conv_b_b,
        'lru_w_a': lru_w_a, 'lru_b_a': lru_b_a, 'lru_w_x': lru_w_x, 'lru_b_x': lru_b_x,
        'lru_lambda': lru_lambda, 'even_w_out': even_w_out, 'attn_w_in': attn_w_in,
        'lam_q1': lam_q1, 'lam_k1': lam_k1, 'lam_q2': lam_q2, 'lam_k2': lam_k2,
        'attn_subln_g': attn_subln_g, 'attn_w_out': attn_w_out, 'xattn_g': xattn_g,
        'xattn_w_q': xattn_w_q, 'xattn_w_out': xattn_w_out, 'ffn2_g': ffn2_g,
        'ffn2_w_in': ffn2_w_in, 'ffn2_w_out': ffn2_w_out, 'final_g': final_g,
    }
    bsz = x_prompt.shape[0]
    dt = x_prompt.dtype
    mem_kv = jnp.einsum('bmd,ldf->lbmf', mem_prompt, xattn_w_kv)
    p_mem_k, p_mem_v = jnp.split(mem_kv, 2, axis=-1)
    p_mem_k = p_mem_k.reshape(DEPTH, bsz, N_MEM, XATT_HEADS, XATT_HD)
    p_mem_v = p_mem_v.reshape(DEPTH, bsz, N_MEM, XATT_HEADS, XATT_HD)
    y_prompt, p_conv_a, p_conv_b, p_lru, p_k, p_v = trunk(
        x_prompt, p_mem_k, p_mem_v,
        jnp.zeros((N_EVEN, bsz, CONV_A_WIDTH - 1, CONV_CH), dt),
        jnp.zeros((N_EVEN, bsz, CONV_B_WIDTH - 1, LRU_CH), dt),
        jnp.zeros((N_EVEN, bsz, LRU_CH), jnp.float32),
        jnp.zeros((N_ODD, bsz, 0, ATT_HEADS, 2 * ATT_HD), dt),
        jnp.zeros((N_ODD, bsz, 0, ATT_HEADS, ATT_VD), dt), p)
    dbsz, n_pages = page_table.shape
    past_len = n_pages * cache_k.shape[2]
    k_past = cache_k[:, page_table].reshape(N_ODD, dbsz, past_len, ATT_HEADS, 2 * ATT_HD)
    v_past = cache_v[:, page_table].reshape(N_ODD, dbsz, past_len, ATT_HEADS, ATT_VD)
    y_sample, s_conv_a, s_conv_b, s_lru, s_k, s_v = trunk(
        x_sample, cache_mem_k, cache_mem_v, state_conv_a, state_conv_b, state_lru, k_past, v_past, p)
    return (y_prompt, y_sample, p_conv_a, p_conv_b, p_lru, p_k, p_v, p_mem_k, p_mem_v, s_conv_a, s_conv_b, s_lru, s_k, s_v)
```

```python
import math
import os
from contextlib import ExitStack

import numpy as np
import concourse.bass as bass
import concourse.mybir as mybir
from concourse.bass_utils import run_bass_kernel_spmd

F32 = mybir.dt.float32
BF16 = mybir.dt.bfloat16
I32 = mybir.dt.int32
U32 = mybir.dt.uint32
AF = mybir.ActivationFunctionType
ALU = mybir.AluOpType
AX = mybir.AxisListType

D = 1024
DFF = 2816
NHC = 22
TB = 512
NBLK = 8
NS = 4
NCOL = TB + NS
EPS = 1e-6
SLOT = 2192
NSLOT = 9
LAM_INIT = 0.8 - 0.6 * math.exp(-0.3 * 1)
NPAGE = 64
SEQ_FULL = NBLK * TB


class Res:
    __slots__ = ("name", "w", "r", "dsem", "dcnt", "excl", "nowaw")

    def __init__(self, name, excl=False, nowaw=False):
        self.name = name
        self.excl = excl
        self.nowaw = nowaw
        self.w = {}
        self.r = {}
        self.dsem = None
        self.dcnt = 0


class Prog:
    def __init__(self, nc, es):
        self.nc = nc
        self.es = es
        self.sems = {}
        self.eng = {}
        for name, h in (("pe", nc.tensor), ("act", nc.scalar), ("dve", nc.vector),
                        ("pool", nc.gpsimd), ("sp", nc.sync)):
            sem = self.new_sem("e_" + name)
            self.eng[name] = dict(h=h, sem=sem, n=0, ops=[], seen={}, pend=False)
        self.all_dma = []
        self.dry = False

    def new_sem(self, name):
        s = self.es.enter_context(self.nc.semaphore(name))
        self.sems[name] = s
        return name

    def res_sem(self, r, sw=False):
        if r.dsem is None:
            r.dsem = {}
            r.dcnt = {}
        if sw not in r.dsem:
            r.dsem[sw] = self.new_sem(("q_" if sw else "d_") + r.name)
            r.dcnt[sw] = 0
        return r.dsem[sw]

    @staticmethod
    def flat(xs):
        out = []
        for x in xs:
            if isinstance(x, (list, tuple)):
                out.extend(Prog.flat(x))
            else:
                out.append(x)
        return out

    def emit(self, e, fn, reads=(), writes=(), inc=True, dma=None):
        if self.dry:
            return None
        reads = self.flat(reads)
        writes = self.flat(writes)
        E = self.eng[e]
        deps = {}
        sw = (e == "pool")
        own_dma = self.res_sem(dma, sw) if dma is not None else None
        for r in reads:
            for s, v in r.w.items():
                deps[s] = max(deps.get(s, 0), v)
            if r.excl:
                for s, v in r.r.items():
                    if s != E["sem"]:
                        deps[s] = max(deps.get(s, 0), v)
        for w in writes:
            for s, v in w.w.items():
                if s == own_dma or w.nowaw:
                    continue
                deps[s] = max(deps.get(s, 0), v)
            for s, v in w.r.items():
                deps[s] = max(deps.get(s, 0), v)
        for s, v in deps.items():
            if e == "pe" and s == E["sem"]:
                continue
            if E["seen"].get(s, 0) >= v:
                continue
            E["ops"].append(("w", s, v))
            E["seen"][s] = v
        if dma is not None:
            dma.dcnt[sw] += 16
            tick = (own_dma, dma.dcnt[sw])
            E["ops"].append(("d", fn, own_dma))
        else:
            tick = (E["sem"], E["n"] + 1)
            E["ops"].append(("o", fn, inc))
            if inc:
                E["n"] += 1
                E["pend"] = False
            else:
                E["pend"] = True
        for r in reads:
            r.r[tick[0]] = max(r.r.get(tick[0], 0), tick[1])
        for w in writes:
            w.w[tick[0]] = max(w.w.get(tick[0], 0), tick[1])
        return tick

    def replay(self, e, h):
        E = self.eng[e]
        assert not E["pend"], e
        for op in E["ops"]:
            if op[0] == "w":
                h.wait_ge(self.sems[op[1]], op[2])
            elif op[0] == "d":
                op[1](h).then_inc(self.sems[op[2]], 16)
            else:
                ins = op[1](h)
                if op[2]:
                    ins.then_inc(self.sems[E["sem"]], 1)


def build_program(nblk=NBLK, npage=NPAGE, npool=2560):
    SEQ = nblk * TB
    NBLK = nblk
    NPAGE = npage
    nc = bass.Bass("TRN2", target_bir_lowering=False)
    es = ExitStack()
    P = Prog(nc, es)

    def din(name, shape, dt=F32):
        return nc.dram_tensor(name, list(shape), dt, kind="ExternalInput").ap()

    def dout(name, shape):
        return nc.dram_tensor(name, list(shape), F32, kind="ExternalOutput").ap()

    xp = din("xp", (SEQ, D))
    xs = din("xs", (NS, D))
    sca = din("sca", (NS, 30, 512))
    scb = din("scb", (NS, 3, 512))
    slru = din("slru", (NS, 512))
    ck = din("ck", (npool * 128, D))
    cv = din("cv", (npool * 128, D))
    cmk = din("cmk", (2, NS, 256, D))
    cmv = din("cmv", (2, NS, 256, D))
    pt = din("pt", (NS, NPAGE), I32)
    mp = din("mp", (256, D))
    ffn_g = [din("ffn1_g", (2, D)), din("ffn2_g", (2, D))]
    ffn_wi = [din("ffn1_w_in", (2, D, 2 * DFF)), din("ffn2_w_in", (2, D, 2 * DFF))]
    ffn_wo = [din("ffn1_w_out", (2, DFF, D)), din("ffn2_w_out", (2, DFF, D))]
    mix_g = din("mix_g", (2, D))
    even_w_in = din("even_w_in", (1, D, 2048))
    conv_a_w = din("conv_a_w", (1, 31, 512))
    conv_a_b = din("conv_a_b", (1, 512))
    conv_a_ln_g = din("conv_a_ln_g", (1, 512))
    conv_a_ln_b = din("conv_a_ln_b", (1, 512))
    conv_b_w = din("conv_b_w", (1, 4, 512))
    conv_b_b = din("conv_b_b", (1, 512))
    lru_w_a = din("lru_w_a", (1, 8, 64, 64))
    lru_b_a = din("lru_b_a", (1, 512))
    lru_w_x = din("lru_w_x", (1, 8, 64, 64))
    lru_b_x = din("lru_b_x", (1, 512))
    lru_lambda = din("lru_lambda", (1, 512))
    even_w_out = din("even_w_out", (1, D, D))
    attn_w_in = din("attn_w_in", (1, D, 3072))
    lam_q1 = din("lam_q1", (1, 64))
    lam_k1 = din("lam_k1", (1, 64))
    lam_q2 = din("lam_q2", (1, 64))
    lam_k2 = din("lam_k2", (1, 64))
    attn_subln_g = din("attn_subln_g", (1, 128))
    attn_w_out = din("attn_w_out", (1, D, D))
    xattn_g = din("xattn_g", (2, D))
    xattn_w_q = din("xattn_w_q", (2, D, D))
    xattn_w_kv = din("xattn_w_kv", (2, D, 2 * D))
    xattn_w_out = din("xattn_w_out", (2, D, D))
    final_g = din("final_g", (1, D))
    c_ident = din("c_ident", (128, 128))
    c_mask = din("c_mask", (128, 4 * 512))
    c_sel = din("c_sel", (NS, NS * 128))
    c_e0 = din("c_e0", (128, 1))
    c_bmask = din("c_bmask", (8, D))
    c_sel2 = din("c_sel2", (8, NS * NS))
    c_iota = din("c_iota", (128, 1))

    o_y_p = dout("o_y_p", (SEQ, D))
    o_y_s = dout("o_y_s", (NS, D))
    o_p_conv_a = dout("o_p_conv_a", (30, 512))
    o_p_conv_b = dout("o_p_conv_b", (3, 512))
    o_p_lru = dout("o_p_lru", (1, 512))
    o_p_k = dout("o_p_k", (SEQ, D))
    o_p_v = dout("o_p_v", (SEQ, D))
    o_p_mem_k = dout("o_p_mem_k", (2, 256, D))
    o_p_mem_v = dout("o_p_mem_v", (2, 256, D))
    o_s_conv_a = dout("o_s_conv_a", (NS, 30, 512))
    o_s_conv_b = dout("o_s_conv_b", (NS, 3, 512))
    o_s_lru = dout("o_s_lru", (NS, 512))
    o_s_k = dout("o_s_k", (NS, D))
    o_s_v = dout("o_s_v", (NS, D))

    ktscr = nc.dram_tensor("ktscr", [8, 128, SEQ], BF16).ap()
    vscr = nc.dram_tensor("vscr", [SEQ, D], BF16).ap()
    R_ktscr = Res("ktscr")
    R_vscr = Res("vscr")
    R_out = Res("outs", nowaw=True)

    def sb(name, shape, dt=F32):
        return es.enter_context(nc.sbuf_tensor(name, list(shape), dt))

    x_t = sb("x", [128, 8, NCOL]);          R_x = Res("x")
    xn_t = sb("xn", [128, 8, NCOL], BF16);  R_xn = Res("xn")
    rs_t = sb("rs", [128, NCOL]);           R_rs = Res("rs")
    arena = sb("arena", [128, NSLOT * SLOT])
    R_sl = [Res("sl%d" % i) for i in range(NSLOT)]
    wst = [sb("wst%d" % i, [128, 4096]) for i in range(2)]
    R_wst = [Res("wst%d" % i) for i in range(2)]
    wbf = [sb("wbf%d" % i, [128, 4096], BF16) for i in range(2)]
    R_wbf = [[Res("wbf%da" % i), Res("wbf%db" % i)] for i in range(2)]
    ts_t = sb("ts", [128, 4, D]);           R_ts = Res("ts")
    memKT = sb("memKT", [128, 2, 8, 256], BF16); R_memKT = Res("memKT")
    memV = sb("memV", [128, 2, 2, D], BF16);     R_memV = Res("memV")
    ident = sb("ident", [128, 128]);        R_c = Res("consts")
    ones_bf = sb("ones_bf", [128, 128], BF16)
    ident_bf = sb("ident_bf", [128, 128], BF16)
    R_dg = [Res("dg%d" % i) for i in range(4)]
    ones_f = sb("ones_f", [128, 128])
    mask_bf = sb("mask_bf", [128, 4, 512], BF16)
    gains = sb("gains", [128, 9, 8])
    cw_a = sb("cw_a", [128, 4, 32])
    cw_b = sb("cw_b", [128, 4, 4])
    vecs = sb("vecs", [128, 10, 4])
    wabd = sb("wabd", [128, 4, 128], BF16)
    wxbd = sb("wxbd", [128, 4, 128], BF16)
    ucarry = sb("ucarry", [128, 4, 30]);    R_carry = Res("carry")
    rcarry = sb("rcarry", [128, 4, 3])
    hcarry = sb("hcarry", [128, 4])
    tmp_t = [sb("tmp%d" % i, [128, 512]) for i in range(2)]
    R_tmp = [Res("tmp%d" % i) for i in range(2)]
    small = sb("small", [128, 64]);         R_small = Res("small")
    lamc = sb("lamc", [128, 4]);            R_lam = Res("lam")
    subg = sb("subg", [128, 1])
    sel_t = sb("sel", [NS, NS * 128])
    e0_t = sb("e0", [128, 1])
    bmask_t = sb("bmask", [8, D])
    sel2_t = sb("sel2", [8, NS * NS])
    iota_t = sb("iota", [128, 1])
    idxf_t = sb("idxf", [128, NS * NPAGE])
    idx_t = sb("idx", [128, NS * NPAGE], I32); R_idx = Res("idx")
    samp = sb("samp", [128, 4, NS, 32]);    R_samp = Res("samp")
    sampb = sb("sampb", [128, 4, NS, 4])
    samph = sb("samph", [128, 4, NS])

    banks = [es.enter_context(nc.psum_tensor("bank%d" % i, [128, 512], F32)) for i in range(8)]
    R_bank = [Res("bank%d" % i, excl=True) for i in range(8)]
    bank_ptr = [0]

    def bank():
        i = bank_ptr[0] % 8
        bank_ptr[0] += 1
        return i

    def slot(i, n=1):
        return arena[:, i * SLOT:(i + n) * SLOT]

    def slot_bf(i, n=1):
        return slot(i, n).bitcast(BF16)

    def v3(ap, c, t):
        return ap[:, 0:c * t].rearrange("p (c t) -> p c t", c=c)

    tmp_ptr = [0]

    def tmp():
        i = tmp_ptr[0] % 2
        tmp_ptr[0] += 1
        return tmp_t[i], R_tmp[i]

    def mm(out, lhsT, rhs, start, stop, reads, writes, inc=None):
        if inc is None:
            inc = stop
        P.emit("pe", lambda h: h.matmul(out, lhsT, rhs, start=start, stop=stop),
               reads=reads, writes=writes, inc=inc)

    def tr(out, in_, reads, writes, n=128):
        P.emit("pe", lambda h: h.transpose(out, in_, ident[0:n, 0:n]), reads=list(reads) + [R_c], writes=writes)

    def act(out, in_, func, reads, writes, bias=0.0, scale=1.0):
        P.emit("act", lambda h: h.activation(out, in_, func, bias=bias, scale=scale), reads=reads, writes=writes)

    def tt(out, a, b, op, reads, writes, e="dve"):
        P.emit(e, lambda h: h.tensor_tensor(out, a, b, op), reads=reads, writes=writes)

    def tsc(out, a, s1, s2, op0, op1, reads, writes, e="dve"):
        if s2 is None:
            P.emit(e, lambda h: h.tensor_scalar(out, a, s1, None, op0), reads=reads, writes=writes)
        else:
            P.emit(e, lambda h: h.tensor_scalar(out, a, s1, s2, op0, op1), reads=reads, writes=writes)

    def stt(out, a, s, b, op0, op1, reads, writes):
        P.emit("dve", lambda h: h.scalar_tensor_tensor(out, a, s, b, op0, op1), reads=reads, writes=writes)

    def cp(out, in_, reads, writes, e="act"):
        if e == "act":
            P.emit("act", lambda h: h.copy(out, in_), reads=reads, writes=writes)
        else:
            P.emit(e, lambda h: h.tensor_copy(out, in_), reads=reads, writes=writes)

    def dma(out, in_, reads, writes, sem_res, e="sp"):
        P.emit(e, lambda h: h.dma_start(out=out, in_=in_), reads=reads, writes=writes, dma=sem_res)

    def store(out, in_, src_res):
        srcs = src_res if isinstance(src_res, (list, tuple)) else [src_res]
        dma(out, in_, list(srcs), [R_out], srcs[0], e="pool")

    wcnt = [0]
    wlist = []
    wdma = [0]
    wcast = [0]
    NST = len(wst)
    NBF = len(wbf)

    def w_issue_dma(upto):
        while wdma[0] <= upto and wdma[0] < len(wlist):
            i = wdma[0]
            wdma[0] += 1
            for dst_fn, src in wlist[i][0]:
                dma(dst_fn(wst[i % NST]), src, [], [R_wst[i % NST]], R_wst[i % NST])

    cast_act_only = [False]

    def w_issue_cast(upto):
        while wcast[0] <= upto and wcast[0] < len(wlist):
            i = wcast[0]
            wcast[0] += 1
            n_el = wlist[i][1]
            hsz = n_el if cast_act_only[0] else (n_el // 2 + 127) // 128 * 128
            s_, b_ = i % NST, i % NBF
            P.emit("act", lambda h, s_=s_, b_=b_, hsz=hsz: h.copy(wbf[b_][:, 0:hsz], wst[s_][:, 0:hsz]),
                   reads=[R_wst[s_]], writes=[R_wbf[b_][0]])
            if hsz < n_el:
                P.emit("dve", lambda h, s_=s_, b_=b_, hsz=hsz, n_el=n_el: h.tensor_copy(wbf[b_][:, hsz:n_el], wst[s_][:, hsz:n_el]),
                       reads=[R_wst[s_]], writes=[R_wbf[b_][1]])

    def wload(pieces, n_el):
        k = wcnt[0]
        wcnt[0] += 1
        if P.dry:
            wlist.append((pieces, n_el))
            return wbf[0], R_wbf[0]
        w_issue_dma(k + 1)
        w_issue_cast(k + 1)
        w_issue_dma(k + NST)
        return wbf[k % NBF], R_wbf[k % NBF]

    def wview(w2d):
        return w2d.rearrange("(kc p) n -> p kc n", p=128)

    def std_tile(w2d, c0, nk=8, ncol=512):
        src = wview(w2d)[:, 0:nk, c0:c0 + ncol]
        return [(lambda t: t[:, 0:nk * ncol].rearrange("p (k n) -> p k n", k=nk), src)], nk * ncol

    def coltiles(j):
        return [(0, TB)] + ([(TB, NCOL)] if j == 0 else [])

    def ncols(j):
        return NCOL if j == 0 else TB

    def rmsnorm(j, gi):
        nc_ = ncols(j)
        sq = v3(slot_bf(8), 8, NCOL)
        act(sq[:, :, 0:nc_], x_t[:, :, 0:nc_], AF.Square, [R_x], [R_sl[8]])
        for (c0, c1) in coltiles(j):
            b = bank()
            for kc in range(8):
                mm(banks[b][:, 0:c1 - c0], ones_bf[:, :], sq[:, kc, c0:c1], kc == 0, kc == 7,
                   [R_sl[8], R_c], [R_bank[b]])
            act(rs_t[:, c0:c1], banks[b][:, 0:c1 - c0], AF.Ln, [R_bank[b]], [R_rs], bias=EPS, scale=1.0 / D)
            act(rs_t[:, c0:c1], rs_t[:, c0:c1], AF.Exp, [R_rs], [R_rs], scale=-0.5)
            for kc in range(8):
                stt(xn_t[:, kc, c0:c1], x_t[:, kc, c0:c1], gains[:, gi, kc:kc + 1], rs_t[:, c0:c1],
                    ALU.mult, ALU.mult, [R_x, R_rs, R_c], [R_xn])

    def linear_fm(j, w2d, n_out_chunks, rhs3, rhs_res, evac, nk=8, oc0=0):
        ntile = (n_out_chunks + 3) // 4
        for t in range(ntile):
            wt, rw = wload(*std_tile(w2d, oc0 * 128 + t * 512, nk))
            w3 = wt[:, 0:nk * 512].rearrange("p (k n) -> p k n", k=nk)
            for oo in range(4):
                oc = oc0 + t * 4 + oo
                if oc >= oc0 + n_out_chunks:
                    break
                for (c0, c1) in coltiles(j):
                    b = bank()
                    for kc in range(nk):
                        mm(banks[b][:, 0:c1 - c0], w3[:, kc, oo * 128:(oo + 1) * 128], rhs3[:, kc, c0:c1],
                           kc == 0, kc == nk - 1, [rw, rhs_res], [R_bank[b]])
                    evac(oc, c0, c1, b)

    def ffn(j, l, which):
        gi = (0 if which == 0 else 2) + l
        rmsnorm(j, gi)
        wi = ffn_wi[which][l]
        wo = ffn_wo[which][l]
        hb = v3(slot_bf(0, 3), NHC, NCOL)
        R_h = R_sl[0:3]

        def in_tile(t):
            wv = wview(wi)
            pieces = [
                (lambda tl: tl[:, 0:4096].rearrange("p (k n) -> p k n", k=8)[:, :, 0:256], wv[:, :, t * 256:t * 256 + 256]),
                (lambda tl: tl[:, 0:4096].rearrange("p (k n) -> p k n", k=8)[:, :, 256:512],
                 wv[:, :, DFF + t * 256:DFF + t * 256 + 256]),
            ]
            return pieces, 4096

        for t in range(11):
            wt, rw = wload(*in_tile(t))
            w3 = wt[:, 0:4096].rearrange("p (k n) -> p k n", k=8)
            for hh in range(2):
                hc = 2 * t + hh
                for (c0, c1) in coltiles(j):
                    n = c1 - c0
                    bg = bank()
                    bu = bank()
                    for kc in range(8):
                        mm(banks[bg][:, 0:n], w3[:, kc, hh * 128:(hh + 1) * 128], xn_t[:, kc, c0:c1],
                           kc == 0, kc == 7, [rw, R_xn], [R_bank[bg]])
                    for kc in range(8):
                        mm(banks[bu][:, 0:n], w3[:, kc, 256 + hh * 128:256 + (hh + 1) * 128], xn_t[:, kc, c0:c1],
                           kc == 0, kc == 7, [rw, R_xn], [R_bank[bu]])
                    tm, rt = tmp()
                    act(tm[:, 0:n], banks[bg][:, 0:n], AF.Silu, [R_bank[bg]], [rt])
                    tt(hb[:, hc, c0:c1], tm[:, 0:n], banks[bu][:, 0:n], ALU.mult, [rt, R_bank[bu]], R_h)

        def out_tile(p, half):
            src = wview(wo)[:, half * 11:(half + 1) * 11, p * 256:(p + 1) * 256]
            return [(lambda tl: tl[:, 0:11 * 256].rearrange("p (k n) -> p k n", k=11), src)], 11 * 256

        seq = [(p, half) for p in range(4) for half in range(2)]
        bs = {}
        for i, (p, half) in enumerate(seq):
            wt, rw = wload(*out_tile(p, half))
            w3 = wt[:, 0:11 * 256].rearrange("p (k n) -> p k n", k=11)
            for fo in range(2):
                for (c0, c1) in coltiles(j):
                    n = c1 - c0
                    if half == 0:
                        bs[(fo, c0)] = bank()
                    b = bs[(fo, c0)]
                    for kk in range(11):
                        mm(banks[b][:, 0:n], w3[:, kk, fo * 128:(fo + 1) * 128], hb[:, half * 11 + kk, c0:c1],
                           half == 0 and kk == 0, half == 1 and kk == 10, [rw] + R_h, [R_bank[b]],
                           inc=(kk == 10))
                    if half == 1:
                        oc = 2 * p + fo
                        stt(x_t[:, oc, c0:c1], banks[b][:, 0:n], 0.5, x_t[:, oc, c0:c1], ALU.mult, ALU.add,
                            [R_bank[b], R_x], [R_x])

    def add_to_x(oc, c0, c1, b):
        tt(x_t[:, oc, c0:c1], banks[b][:, 0:c1 - c0], x_t[:, oc, c0:c1], ALU.add, [R_bank[b], R_x], [R_x])

    def store_T(src_fn, n, nchunk, dram2d, reads, stage_ap, stage_res):
        for g in range((nchunk + 3) // 4):
            b = bank()
            cs = range(g * 4, min(nchunk, g * 4 + 4))
            for c in cs:
                tr(banks[b][0:n, (c - g * 4) * 128:(c - g * 4 + 1) * 128], src_fn(c), reads, [R_bank[b]])
            w = len(cs) * 128
            cp(stage_ap[0:n, g * 512:g * 512 + w], banks[b][0:n, 0:w], [R_bank[b]], [stage_res])
        store(dram2d, stage_ap[0:n, 0:nchunk * 128], stage_res)

    def load_T(dst_fn, n, nchunk, dram2d, dst_res, stage_ap, stage_res):
        dma(stage_ap[0:n, 0:nchunk * 128], dram2d, [], [stage_res], stage_res)
        for c in range(nchunk):
            b = bank()
            tr(banks[b][:, 0:n], stage_ap[0:n, c * 128:(c + 1) * 128], [stage_res], [R_bank[b]], n=n)
            cp(dst_fn(c), banks[b][:, 0:n], [R_bank[b]], [dst_res])

    def setup():
        dma(ident[:, :], c_ident, [], [R_c], R_c)
        P.emit("pool", lambda h: h.memset(ones_f[:, :], 1.0), reads=[], writes=[R_c])
        cp(ident_bf[:, :], ident[:, :], [R_c], [R_c], e="dve")
        P.emit("pool", lambda h: h.memset(ones_bf[:, :], 1.0), reads=[], writes=[R_c])
        dma(ts_t[:, 0, :], c_mask[:, 0:1024], [], [R_ts], R_ts)
        dma(ts_t[:, 1, :], c_mask[:, 1024:2048], [], [R_ts], R_ts)
        cp(mask_bf[:, 0:2, :], ts_t[:, 0, :].rearrange("p (a b) -> p a b", a=2), [R_ts], [R_c], e="dve")
        cp(mask_bf[:, 2:4, :], ts_t[:, 1, :].rearrange("p (a b) -> p a b", a=2), [R_ts], [R_c], e="dve")
        dma(sel_t[:, :], c_sel, [], [R_c], R_c)
        dma(e0_t[:, :], c_e0, [], [R_c], R_c)
        dma(bmask_t[:, :], c_bmask, [], [R_c], R_c)
        dma(sel2_t[:, :], c_sel2, [], [R_c], R_c)
        dma(iota_t[:, :], c_iota, [], [R_c], R_c)
        st = slot(0)
        glist = [ffn_g[0][0:1, :], ffn_g[0][1:2, :], ffn_g[1][0:1, :], ffn_g[1][1:2, :], mix_g[0:1, :], mix_g[1:2, :],
                 xattn_g[0:1, :], xattn_g[1:2, :], final_g[0:1, :]]
        for i, g in enumerate(glist):
            dma(st[i:i + 1, 0:D], g, [], [R_sl[0]], R_sl[0], e="pool")
        for c in range(8):
            b = bank()
            tr(banks[b][:, 0:9], st[0:9, c * 128:(c + 1) * 128], [R_sl[0]], [R_bank[b]], n=9)
            cp(gains[:, :, c], banks[b][:, 0:9], [R_bank[b]], [R_c])
        st1 = slot(1)
        vl = [conv_a_b, conv_a_ln_g, conv_a_ln_b, conv_b_b, lru_b_a, lru_b_x, lru_lambda]
        for i, v in enumerate(vl):
            dma(st1[i:i + 1, 0:512], v, [], [R_sl[1]], R_sl[1], e="pool")
        for c in range(4):
            b = bank()
            tr(banks[b][:, 0:7], st1[0:7, c * 128:(c + 1) * 128], [R_sl[1]], [R_bank[b]], n=7)
            cp(vecs[:, 0:7, c], banks[b][:, 0:7], [R_bank[b]], [R_c])
        st2 = slot(2)
        dma(st2[0:31, 0:512], conv_a_w[0], [], [R_sl[2]], R_sl[2])
        for c in range(4):
            b = bank()
            tr(banks[b][:, 0:31], st2[0:31, c * 128:(c + 1) * 128], [R_sl[2]], [R_bank[b]], n=31)
            cp(cw_a[:, c, 0:31], banks[b][:, 0:31], [R_bank[b]], [R_c])
        st3 = slot(3)
        dma(st3[0:4, 0:512], conv_b_w[0], [], [R_sl[3]], R_sl[3])
        for c in range(4):
            b = bank()
            tr(banks[b][:, 0:4], st3[0:4, c * 128:(c + 1) * 128], [R_sl[3]], [R_bank[b]], n=4)
            cp(cw_b[:, c, 0:4], banks[b][:, 0:4], [R_bank[b]], [R_c])
        st4 = v3(slot(4), 4, 128)
        st5 = v3(slot(5), 4, 128)
        P.emit("pool", lambda h: h.memset(slot(4)[:, 0:512], 0.0), reads=[], writes=[R_sl[4]])
        P.emit("pool", lambda h: h.memset(slot(5)[:, 0:512], 0.0), reads=[], writes=[R_sl[5]])
        for c in range(4):
            for hh in range(2):
                dma(st4[hh * 64:(hh + 1) * 64, c, hh * 64:(hh + 1) * 64], lru_w_a[0, 2 * c + hh], [], [R_sl[4]], R_sl[4])
                dma(st5[hh * 64:(hh + 1) * 64, c, hh * 64:(hh + 1) * 64], lru_w_x[0, 2 * c + hh], [], [R_sl[5]], R_sl[5])
        cp(wabd[:, :, :], st4, [R_sl[4]], [R_c], e="dve")
        cp(wxbd[:, :, :], st5, [R_sl[5]], [R_c], e="dve")
        lam_ = vecs[:, 6, :]
        a_ = small[:, 0:4]; e_ = small[:, 4:8]; z_ = small[:, 8:12]; z2 = small[:, 12:16]
        pl = small[:, 16:20]; t_ = small[:, 20:24]
        RS_ = [R_small, R_c]
        act(a_, lam_, AF.Abs, RS_, [R_small])
        act(e_, a_, AF.Exp, RS_, [R_small], scale=-1.0)
        tsc(t_, e_, 2.0, None, ALU.add, None, RS_, [R_small])
        P.emit("dve", lambda h: h.reciprocal(t_, t_), reads=RS_, writes=[R_small])
        tt(z_, e_, t_, ALU.mult, RS_, [R_small])
        tt(z2, z_, z_, ALU.mult, RS_, [R_small])
        tsc(pl, z2, 1.0 / 11.0, 1.0 / 9.0, ALU.mult, ALU.add, RS_, [R_small])
        for cf in (1.0 / 7.0, 1.0 / 5.0, 1.0 / 3.0, 1.0):
            tt(pl, pl, z2, ALU.mult, RS_, [R_small])
            tsc(pl, pl, cf, None, ALU.add, None, RS_, [R_small])
        tt(pl, pl, z_, ALU.mult, RS_, [R_small])
        tsc(t_, lam_, -1.0, 0.0, ALU.mult, ALU.max, RS_, [R_small])
        stt(pl, pl, 2.0, t_, ALU.mult, ALU.add, RS_, [R_small])
        tsc(vecs[:, 7, :], pl, -8.0, None, ALU.mult, None, RS_, [R_c])
        for i, v in enumerate((lam_q1, lam_k1, lam_q2, lam_k2)):
            dma(st[32:33, i * 64:(i + 1) * 64], v, [], [R_sl[0]], R_sl[0], e="pool")
        tt(st[32:33, 256:320], st[32:33, 0:64], st[32:33, 64:128], ALU.mult, [R_sl[0]], [R_sl[0]])
        tt(st[32:33, 320:384], st[32:33, 128:192], st[32:33, 192:256], ALU.mult, [R_sl[0]], [R_sl[0]])
        P.emit("dve", lambda h: h.reduce_sum(st[32:33, 384:386], st[32:33, 256:384].rearrange("p (a b) -> p a b", a=2), AX.X),
               reads=[R_sl[0]], writes=[R_sl[0]])
        act(st[32:33, 384:386], st[32:33, 384:386], AF.Exp, [R_sl[0]], [R_sl[0]])
        tt(st[32:33, 386:387], st[32:33, 384:385], st[32:33, 385:386], ALU.subtract, [R_sl[0]], [R_sl[0]])
        tsc(st[32:33, 387:388], st[32:33, 386:387], LAM_INIT, None, ALU.add, None, [R_sl[0]], [R_sl[0]])
        tsc(st[32:33, 388:389], st[32:33, 387:388], -1.0, None, ALU.mult, None, [R_sl[0]], [R_sl[0]])
        b = bank()
        mm(banks[b][:, 0:2], ones_f[32:33, :], st[32:33, 387:389], True, True, [R_sl[0], R_c], [R_bank[b]])
        cp(lamc[:, 0:2], banks[b][:, 0:2], [R_bank[b]], [R_lam])
        dma(st[64:65, 0:128], attn_subln_g, [], [R_sl[0]], R_sl[0], e="pool")
        b = bank()
        mm(banks[b][:, 0:1], st[64:65, 0:128], ones_f[64:65, 0:1], True, True, [R_sl[0], R_c], [R_bank[b]])
        act(subg[:, :], banks[b][:, 0:1], AF.Copy, [R_bank[b]], [R_lam], scale=1.0 - LAM_INIT)
        P.emit("pool", lambda h: h.memset(ucarry[:, :, :], 0.0), reads=[], writes=[R_carry])
        P.emit("pool", lambda h: h.memset(rcarry[:, :, :], 0.0), reads=[], writes=[R_carry])
        P.emit("pool", lambda h: h.memset(hcarry[:, :], 0.0), reads=[], writes=[R_carry])
        dma(idx_t[:, :], pt.rearrange("s j -> (s j)").partition_broadcast(128), [], [R_idx], R_idx)
        cp(idxf_t[:, :], idx_t[:, :], [R_idx], [R_idx], e="dve")
        tsc(idxf_t[:, :], idxf_t[:, :], 128.0, iota_t[:, 0:1], ALU.mult, ALU.add, [R_idx, R_c], [R_idx])
        cp(idx_t[:, :], idxf_t[:, :], [R_idx], [R_idx], e="dve")

    def mem_setup():
        mt = v3(slot(0), 8, 256)
        mtb = v3(slot_bf(1), 8, 256)
        for r in range(2):
            load_T(lambda c, r=r: mt[:, c, r * 128:(r + 1) * 128], 128, 8, mp[r * 128:(r + 1) * 128, :], R_sl[0],
                   ts_t[:, r, :], R_ts)
        cp(mtb, mt, [R_sl[0]], [R_sl[1]], e="dve")
        ksub = int(os.environ.get("KSUB", "99"))
        if ksub < 1:
            return
        for l in range(2):
            w2d = xattn_w_kv[l]
            for t in range(4):
                wt, rw = wload(*std_tile(w2d, t * 512))
                if ksub < 2:
                    continue
                w3 = wt[:, 0:4096].rearrange("p (k n) -> p k n", k=8)
                if t < 2:
                    for oo in range(4):
                        b = bank()
                        for kc in range(8):
                            mm(banks[b][:, 0:256], w3[:, kc, oo * 128:(oo + 1) * 128], mtb[:, kc, :], kc == 0, kc == 7,
                               [rw, R_sl[1]], [R_bank[b]])
                        cp(memKT[:, l, t * 4 + oo, :], banks[b][:, 0:256], [R_bank[b]], [R_memKT])
                for r in range(2):
                    b = bank()
                    for kc in range(8):
                        mm(banks[b][:, :], mtb[:, kc, r * 128:(r + 1) * 128], w3[:, kc, :], kc == 0, kc == 7,
                           [rw, R_sl[1]], [R_bank[b]])
                    stg = v3(slot(2 + (t % 2)), 2, 512)
                    cp(stg[:, r, :], banks[b][:, :], [R_bank[b]], [R_sl[2 + (t % 2)]])
                    if ksub < 3:
                        continue
                    if t >= 2:
                        cp(memV[:, l, r, (t - 2) * 512:(t - 1) * 512], banks[b][:, :], [R_bank[b]], [R_memV])
                    dst = (o_p_mem_k if t < 2 else o_p_mem_v)[l, r * 128:(r + 1) * 128, (t % 2) * 512:(t % 2 + 1) * 512]
                    if ksub < 4:
                        continue
                    store(dst, stg[:, r, :], R_sl[2 + (t % 2)])

    def load_block(j):
        dma(ts_t[:, :, :], xp[j * TB:(j + 1) * TB, :].rearrange("(s p) f -> p s f", p=128), [], [R_ts], R_ts)
        for c in range(8):
            b = bank()
            for s in range(4):
                tr(banks[b][:, s * 128:(s + 1) * 128], ts_t[:, s, c * 128:(c + 1) * 128], [R_ts], [R_bank[b]])
            cp(x_t[:, c, 0:TB], banks[b][:, :], [R_bank[b]], [R_x], e=("act" if c % 2 else "dve"))
        if j == 0:
            load_T(lambda c: x_t[:, c, TB:NCOL], NS, 8, xs, R_x, slot(0), R_sl[0])

    def final_block(j):
        rmsnorm(j, 8)
        nc_ = ncols(j)
        yt = v3(slot(0, 2), 8, NCOL)
        for kc in range(8):
            stt(yt[:, kc, 0:nc_], x_t[:, kc, 0:nc_], gains[:, 8, kc:kc + 1], rs_t[:, 0:nc_], ALU.mult, ALU.mult,
                [R_x, R_rs, R_c], [R_sl[0], R_sl[1]])
        for s in range(4):
            for g in range(2):
                b = bank()
                for c in range(4):
                    tr(banks[b][:, c * 128:(c + 1) * 128], yt[:, g * 4 + c, s * 128:(s + 1) * 128], [R_sl[0], R_sl[1]],
                       [R_bank[b]])
                cp(ts_t[:, s, g * 512:(g + 1) * 512], banks[b][:, :], [R_bank[b]], [R_ts], e=("act" if g else "dve"))
        store(o_y_p[j * TB:(j + 1) * TB, :].rearrange("(s p) f -> p s f", p=128), ts_t[:, :, :], R_ts)
        if j == 0:
            store_T(lambda c: yt[:, c, TB:NCOL], NS, 8, o_y_s, [R_sl[0], R_sl[1]], slot(2), R_sl[2])

    def even_mixer(j):
        rmsnorm(j, 4)
        nc_ = ncols(j)
        W = TB + 30
        aval = v3(slot(0), 4, NCOL);  R_av = R_sl[0]
        uext = v3(slot(1), 4, W);     R_u = R_sl[1]
        rext = v3(slot(2), 4, TB + 3); R_r = R_sl[2]
        bg = v3(slot(3), 4, NCOL);    R_bg = R_sl[3]
        xr = v3(slot(4), 4, NCOL);    R_xr = R_sl[4]
        m5 = v3(slot(5), 4, NCOL);    R_5 = R_sl[5]
        m6 = v3(slot(6), 4, NCOL);    R_6 = R_sl[6]
        m7 = v3(slot(7), 4, NCOL);    R_7 = R_sl[7]
        mixin = v3(slot_bf(8), 8, NCOL); R_mi = R_sl[8]
        cp(uext[:, :, 0:30], ucarry[:, :, :], [R_carry], [R_u], e="pool")
        cp(rext[:, :, 0:3], rcarry[:, :, :], [R_carry], [R_r], e="pool")
        if j == 0:
            for s in range(NS):
                load_T(lambda c, s=s: samp[:, c, s, 0:30], 30, 4, sca[s], R_samp, slot(5), R_sl[5])
                load_T(lambda c, s=s: sampb[:, c, s, 0:3], 3, 4, scb[s], R_samp, slot(6), R_sl[6])
                if "d2d" not in os.environ.get("KSKIP", ""):
                    dma(o_s_conv_a[s, 0:29, :], sca[s, 1:30, :], [], [R_out], R_out)
                    dma(o_s_conv_b[s, 0:2, :], scb[s, 1:3, :], [], [R_out], R_out)
            load_T(lambda c: samph[:, c, :], NS, 4, slru, R_samp, slot(7), R_sl[7])

        def evac(oc, c0, c1, b):
            q, c = oc // 4, oc % 4
            n = c1 - c0
            samp_tile = c0 >= TB
            if q == 0:
                cp(aval[:, c, c0:c1], banks[b][:, 0:n], [R_bank[b]], [R_av])
            elif q == 1:
                tm, rt = tmp()
                act(tm[:, 0:n], banks[b][:, 0:n], AF.Sigmoid, [R_bank[b]], [rt])
                if samp_tile:
                    tt(samp[:, c, :, 30], tm[:, 0:n], aval[:, c, c0:c1], ALU.mult, [rt, R_av], [R_samp])
                else:
                    tt(uext[:, c, 30:30 + TB], tm[:, 0:n], aval[:, c, c0:c1], ALU.mult, [rt, R_av], [R_u])
            elif q == 2:
                if samp_tile:
                    cp(sampb[:, c, :, 3], banks[b][:, 0:n], [R_bank[b]], [R_samp])
                else:
                    cp(rext[:, c, 3:3 + TB], banks[b][:, 0:n], [R_bank[b]], [R_r])
            else:
                cp(bg[:, c, c0:c1], banks[b][:, 0:n], [R_bank[b]], [R_bg])

        linear_fm(j, even_w_in[0], 8, xn_t, R_xn, evac)
        acc = aval
        ubf = v3(slot_bf(5), 4, W)
        cp(ubf, uext, [R_u], [R_5])
        dgs = slot_bf(6)
        di = 0
        for c in range(4):
            b = bank()
            for w in range(31):
                dg = dgs[:, (di % 4) * 128:(di % 4 + 1) * 128]
                rd = R_dg[di % 4]
                di += 1
                tsc(dg, ident_bf[:, :], cw_a[:, c, w:w + 1], None, ALU.mult, None, [R_c], [rd, R_6])
                mm(banks[b][:, :], dg, ubf[:, c, w:w + TB], w == 0, w == 30, [rd, R_5], [R_bank[b]], inc=True)
            tsc(acc[:, c, 0:TB], banks[b][:, :], vecs[:, 0, c:c + 1], None, ALU.add, None, [R_bank[b], R_c], [R_av])
        if j == 0:
            for c in range(4):
                pr = m7[:, c, 0:NS * 31].rearrange("p (s w) -> p s w", s=NS)
                tt(pr, samp[:, c, :, 0:31], cw_a[:, c:c + 1, 0:31].to_broadcast([128, NS, 31]), ALU.mult,
                   [R_samp, R_c], [R_7])
                P.emit("dve", lambda h, c=c, pr=pr: h.reduce_sum(acc[:, c, TB:NCOL], pr, AX.X), reads=[R_7], writes=[R_av])
                tsc(acc[:, c, TB:NCOL], acc[:, c, TB:NCOL], vecs[:, 0, c:c + 1], None, ALU.add, None, [R_av, R_c], [R_av])
        cast_act_only[0] = True
        linear_fm(j, even_w_in[0], 8, xn_t, R_xn, evac, oc0=8)
        cast_act_only[0] = False
        sqf = m7
        act(sqf[:, :, 0:nc_], acc[:, :, 0:nc_], AF.Square, [R_av], [R_7])
        for (c0, c1) in coltiles(j):
            n = c1 - c0
            b1 = bank(); b2 = bank()
            for c in range(4):
                mm(banks[b1][:, 0:n], ones_f[:, :], acc[:, c, c0:c1], c == 0, c == 3, [R_av, R_c], [R_bank[b1]])
            for c in range(4):
                mm(banks[b2][:, 0:n], ones_f[:, :], sqf[:, c, c0:c1], c == 0, c == 3, [R_7, R_c], [R_bank[b2]])
            mu = m5[:, 0, c0:c1]; var = m5[:, 1, c0:c1]; t1 = m5[:, 2, c0:c1]
            act(mu, banks[b1][:, 0:n], AF.Copy, [R_bank[b1]], [R_5], scale=1.0 / 512)
            tt(t1, mu, mu, ALU.mult, [R_5], [R_5])
            stt(var, banks[b2][:, 0:n], 1.0 / 512, t1, ALU.mult, ALU.subtract, [R_bank[b2], R_5], [R_5])
            act(var, var, AF.Ln, [R_5], [R_5], bias=EPS)
            act(var, var, AF.Exp, [R_5], [R_5], scale=-0.5)
            for c in range(4):
                tt(m6[:, c, c0:c1], acc[:, c, c0:c1], mu, ALU.subtract, [R_av, R_5], [R_6])
                tt(m6[:, c, c0:c1], m6[:, c, c0:c1], var, ALU.mult, [R_6, R_5], [R_6])
                P.emit("act", lambda h, c=c, c0=c0, c1=c1: h.activation(
                    mixin[:, c, c0:c1], m6[:, c, c0:c1], AF.Silu, bias=vecs[:, 2, c:c + 1], scale=vecs[:, 1, c:c + 1]),
                    reads=[R_6, R_c], writes=[R_mi])
        for c in range(4):
            tsc(xr[:, c, 0:TB], rext[:, c, 0:TB], cw_b[:, c, 0:1], vecs[:, 3, c:c + 1], ALU.mult, ALU.add, [R_r, R_c], [R_xr])
            for w in range(1, 4):
                stt(xr[:, c, 0:TB], rext[:, c, w:w + TB], cw_b[:, c, w:w + 1], xr[:, c, 0:TB], ALU.mult, ALU.add,
                    [R_r, R_c, R_xr], [R_xr])
        if j == 0:
            for c in range(4):
                pr = m7[:, c, 0:NS * 4].rearrange("p (s w) -> p s w", s=NS)
                tt(pr, sampb[:, c, :, 0:4], cw_b[:, c:c + 1, 0:4].to_broadcast([128, NS, 4]), ALU.mult, [R_samp, R_c], [R_7])
                P.emit("dve", lambda h, c=c, pr=pr: h.reduce_sum(xr[:, c, TB:NCOL], pr, AX.X), reads=[R_7], writes=[R_xr])
                tsc(xr[:, c, TB:NCOL], xr[:, c, TB:NCOL], vecs[:, 3, c:c + 1], None, ALU.add, None, [R_xr, R_c], [R_xr])
        xrb = v3(slot_bf(7), 4, NCOL)
        cp(xrb[:, :, 0:nc_], xr[:, :, 0:nc_], [R_xr], [R_7])
        for c in range(4):
            for (c0, c1) in coltiles(j):
                n = c1 - c0
                b1 = bank(); b2 = bank()
                mm(banks[b1][:, 0:n], wabd[:, c, :], xrb[:, c, c0:c1], True, True, [R_7, R_c], [R_bank[b1]])
                mm(banks[b2][:, 0:n], wxbd[:, c, :], xrb[:, c, c0:c1], True, True, [R_7, R_c], [R_bank[b2]])
                P.emit("act", lambda h, c=c, c0=c0, c1=c1, b1=b1, n=n: h.activation(
                    m5[:, c, c0:c1], banks[b1][:, 0:n], AF.Sigmoid, bias=vecs[:, 4, c:c + 1]),
                    reads=[R_bank[b1], R_c], writes=[R_5])
                P.emit("act", lambda h, c=c, c0=c0, c1=c1, b2=b2, n=n: h.activation(
                    m6[:, c, c0:c1], banks[b2][:, 0:n], AF.Sigmoid, bias=vecs[:, 5, c:c + 1]),
                    reads=[R_bank[b2], R_c], writes=[R_6])
        for c in range(4):
            P.emit("act", lambda h, c=c: h.activation(m5[:, c, 0:nc_], m5[:, c, 0:nc_], AF.Exp, scale=vecs[:, 7, c:c + 1]),
                   reads=[R_5, R_c], writes=[R_5])
        bt = v3(slot(7), 4, NCOL)
        tt(bt[:, :, 0:nc_], m5[:, :, 0:nc_], m5[:, :, 0:nc_], ALU.mult, [R_5], [R_7])
        tsc(bt[:, :, 0:nc_], bt[:, :, 0:nc_], -1.0, 1.0, ALU.mult, ALU.add, [R_7], [R_7])
        tsc(bt[:, :, 0:nc_], bt[:, :, 0:nc_], 1e-30, None, ALU.max, None, [R_7], [R_7])
        act(bt[:, :, 0:nc_], bt[:, :, 0:nc_], AF.Ln, [R_7], [R_7])
        act(bt[:, :, 0:nc_], bt[:, :, 0:nc_], AF.Exp, [R_7], [R_7], scale=0.5)
        tt(m6[:, :, 0:nc_], m6[:, :, 0:nc_], bt[:, :, 0:nc_], ALU.mult, [R_6, R_7], [R_6])
        tt(m6[:, :, 0:nc_], m6[:, :, 0:nc_], xr[:, :, 0:nc_], ALU.mult, [R_6, R_xr], [R_6])
        hl = bt
        for c in range(4):
            P.emit("dve", lambda h, c=c: h.tensor_tensor_scan(hl[:, c, 0:TB], m5[:, c, 0:TB], m6[:, c, 0:TB],
                                                            hcarry[:, c:c + 1], ALU.mult, ALU.add),
                   reads=[R_5, R_6, R_carry], writes=[R_7])
        if j == 0:
            for c in range(4):
                tt(hl[:, c, TB:NCOL], m5[:, c, TB:NCOL], samph[:, c, :], ALU.mult, [R_5, R_samp], [R_7])
                tt(hl[:, c, TB:NCOL], hl[:, c, TB:NCOL], m6[:, c, TB:NCOL], ALU.add, [R_7, R_6], [R_7])
        cp(hcarry[:, :], hl[:, :, TB - 1], [R_7], [R_carry], e="pool")
        cp(ucarry[:, :, :], uext[:, :, TB:TB + 30], [R_u], [R_carry], e="pool")
        cp(rcarry[:, :, :], rext[:, :, TB:TB + 3], [R_r], [R_carry], e="pool")
        g5 = m5
        tt(g5[:, :, 0:nc_], bg[:, :, 0:nc_], bg[:, :, 0:nc_], ALU.mult, [R_bg], [R_5])
        tsc(g5[:, :, 0:nc_], g5[:, :, 0:nc_], 0.044715, 1.0, ALU.mult, ALU.add, [R_5], [R_5])
        tt(g5[:, :, 0:nc_], g5[:, :, 0:nc_], bg[:, :, 0:nc_], ALU.mult, [R_5, R_bg], [R_5])
        act(g5[:, :, 0:nc_], g5[:, :, 0:nc_], AF.Sigmoid, [R_5], [R_5], scale=1.5957691216057308)
        tt(g5[:, :, 0:nc_], g5[:, :, 0:nc_], bg[:, :, 0:nc_], ALU.mult, [R_5, R_bg], [R_5])
        tt(mixin[:, 4:8, 0:nc_], hl[:, :, 0:nc_], g5[:, :, 0:nc_], ALU.mult, [R_7, R_5], [R_mi])
        if j == NBLK - 1:
            store_T(lambda c: uext[:, c, TB:TB + 30], 30, 4, o_p_conv_a, [R_u], slot(3), R_sl[3])
            store_T(lambda c: rext[:, c, TB:TB + 3], 3, 4, o_p_conv_b, [R_r], slot(4), R_sl[4])
            store_T(lambda c: hl[:, c, TB - 1:TB], 1, 4, o_p_lru, [R_7], slot(6), R_sl[6])
        if j == 0:
            store_T(lambda c: samp[:, c, :, 30], NS, 4, o_s_conv_a[:, 29, :], [R_samp], slot(3), R_sl[3])
            store_T(lambda c: sampb[:, c, :, 3], NS, 4, o_s_conv_b[:, 2, :], [R_samp], slot(4), R_sl[4])
            store_T(lambda c: hl[:, c, TB:NCOL], NS, 4, o_s_lru, [R_7], slot(6), R_sl[6])
        linear_fm(j, even_w_out[0], 8, mixin, R_mi, add_to_x)

    def xattn_core(q3, R_q, c0, c1, kT, R_k, vv, R_v, o3, R_o, pslot):
        n = c1 - c0
        pT = slot_bf(pslot)
        for hd in range(4):
            R_p = R_xp[hd]
            pts = []
            bl = bank()
            for mt_ in range(2):
                b = bank()
                for dc in range(2):
                    mm(banks[b][:, 0:n], kT[:, hd * 2 + dc, mt_ * 128:(mt_ + 1) * 128], q3[:, hd * 2 + dc, c0:c1],
                       dc == 0, dc == 1, [R_k, R_q], [R_bank[b]])
                pt_ = pT[:, hd * 1024 + mt_ * 512:hd * 1024 + mt_ * 512 + n]
                act(pt_, banks[b][:, 0:n], AF.Exp, [R_bank[b]], [R_p, R_sl[pslot]], scale=1.0 / 16.0)
                pts.append(pt_)
            for mt_ in range(2):
                mm(banks[bl][:, 0:n], ones_bf[:, :], pts[mt_], mt_ == 0, mt_ == 1, [R_p, R_c], [R_bank[bl]])
            tm, rt = tmp()
            act(tm[:, 0:n], banks[bl][:, 0:n], AF.Ln, [R_bank[bl]], [rt])
            act(tm[:, 0:n], tm[:, 0:n], AF.Exp, [rt], [rt], scale=-1.0)
            for ec in range(2):
                b = bank()
                for mt_ in range(2):
                    mm(banks[b][:, 0:n], vv[:, mt_, hd * 256 + ec * 128:hd * 256 + (ec + 1) * 128], pts[mt_],
                       mt_ == 0, mt_ == 1, [R_v, R_p], [R_bank[b]])
                tt(o3[:, hd * 2 + ec, c0:c1], banks[b][:, 0:n], tm[:, 0:n], ALU.mult, [R_bank[b], rt], [R_o])

    def xattn(j, l):
        rmsnorm(j, 6 + l)
        q3 = v3(slot_bf(0), 8, NCOL); R_q = R_sl[0]
        o3 = v3(slot_bf(1), 8, NCOL); R_o = R_sl[1]

        def evq(oc, c0, c1, b):
            cp(q3[:, oc, c0:c1], banks[b][:, 0:c1 - c0], [R_bank[b]], [R_q])

        linear_fm(j, xattn_w_q[l], 8, xn_t, R_xn, evq)
        xattn_core(q3, R_q, 0, TB, memKT[:, l], R_memKT, memV[:, l], R_memV, o3, R_o, 2)
        if j == 0:
            for s in range(NS):
                kTs = v3(slot_bf(3), 8, 256)
                vvs = v3(slot_bf(4), 2, D)
                kst = v3(slot(5, 2), 2, D)
                vst = v3(slot(7), 2, D)
                dma(kst, cmk[l, s].rearrange("(r p) f -> p r f", p=128), [], [R_sl[5], R_sl[6]], R_sl[5])
                dma(vst, cmv[l, s].rearrange("(r p) f -> p r f", p=128), [], [R_sl[7]], R_sl[7])
                for c in range(8):
                    b = bank()
                    for r in range(2):
                        tr(banks[b][:, r * 128:(r + 1) * 128], kst[:, r, c * 128:(c + 1) * 128], [R_sl[5], R_sl[6]], [R_bank[b]])
                    cp(kTs[:, c, :], banks[b][:, 0:256], [R_bank[b]], [R_sl[3]], e=("act" if c % 2 else "dve"))
                cp(vvs, vst, [R_sl[7]], [R_sl[4]], e="pool")
                xattn_core(q3, R_q, TB + s, TB + s + 1, kTs, R_sl[3], vvs, R_sl[4], o3, R_o, 2)
        linear_fm(j, xattn_w_out[l], 8, o3, R_o, add_to_x)

    R_pT = [Res("pT%d" % i) for i in range(4)]
    R_xp = [Res("xp%d" % i) for i in range(4)]

    def attention(j):
        rmsnorm(j, 5)
        nc_ = ncols(j)
        qT = v3(slot_bf(0), 8, NCOL); R_q = R_sl[0]
        att = v3(slot_bf(1), 8, NCOL); R_att = R_sl[1]
        kTn = v3(slot_bf(2), 8, TB); R_kn = R_sl[2]
        wi = attn_w_in[0]
        nkt = 4 * (j + 1)

        def evq(oc, c0, c1, b):
            cp(qT[:, oc, c0:c1], banks[b][:, 0:c1 - c0], [R_bank[b]], [R_q])

        linear_fm(j, wi[:, 0:D], 8, xn_t, R_xn, evq)
        ktok = ts_t; vtok = v3(slot(5, 2), 4, D); vbf = v3(slot_bf(7), 4, D)
        def stok(col, n=512):
            if col < 2 * D:
                return slot(3)[0:NS, col:col + n], R_sl[3]
            return slot(4)[0:NS, col - 2 * D:col - 2 * D + n], R_sl[4]
        for part in range(2):
            for t in range(2):
                col = D + part * D + t * 512
                wt, rw = wload(*std_tile(wi, col))
                w3 = wt[:, 0:4096].rearrange("p (k n) -> p k n", k=8)
                if part == 0:
                    for oo in range(4):
                        b = bank()
                        for kc in range(8):
                            mm(banks[b][:, :], w3[:, kc, oo * 128:(oo + 1) * 128], xn_t[:, kc, 0:TB], kc == 0, kc == 7,
                               [rw, R_xn], [R_bank[b]])
                        cp(kTn[:, t * 4 + oo, :], banks[b][:, :], [R_bank[b]], [R_kn])
                for s in range(4):
                    b = bank()
                    for kc in range(8):
                        mm(banks[b][:, :], xn_t[:, kc, s * 128:(s + 1) * 128], w3[:, kc, :], kc == 0, kc == 7,
                           [rw, R_xn], [R_bank[b]])
                    if part == 0:
                        cp(ktok[:, s, t * 512:(t + 1) * 512], banks[b][:, :], [R_bank[b]], [R_ts])
                    else:
                        cp(vtok[:, s, t * 512:(t + 1) * 512], banks[b][:, :], [R_bank[b]], [R_sl[5], R_sl[6]])
                        cp(vbf[:, s, t * 512:(t + 1) * 512], banks[b][:, :], [R_bank[b]], [R_sl[7]], e="dve")
                if j == 0:
                    b = bank()
                    for kc in range(8):
                        mm(banks[b][0:NS, :], xn_t[:, kc, TB:NCOL], w3[:, kc, :], kc == 0, kc == 7, [rw, R_xn], [R_bank[b]])
                    sap, sres = stok(col)
                    cp(sap, banks[b][0:NS, :], [R_bank[b]], [sres])
        if j == 0:
            for t in range(2):
                wt, rw = wload(*std_tile(wi, t * 512))
                w3 = wt[:, 0:4096].rearrange("p (k n) -> p k n", k=8)
                b = bank()
                for kc in range(8):
                    mm(banks[b][0:NS, :], xn_t[:, kc, TB:NCOL], w3[:, kc, :], kc == 0, kc == 7, [rw, R_xn], [R_bank[b]])
                sap, sres = stok(t * 512)
                cp(sap, banks[b][0:NS, :], [R_bank[b]], [sres])
            store(o_s_k, stok(D, D)[0], R_sl[3])
            store(o_s_v, stok(2 * D, D)[0], R_sl[4])
        rows = slice(j * TB, (j + 1) * TB)
        store(o_p_k[rows, :].rearrange("(s p) f -> p s f", p=128), ktok[:, :, :], R_ts)
        store(o_p_v[rows, :].rearrange("(s p) f -> p s f", p=128), vtok, [R_sl[5], R_sl[6]])
        dma(vscr[rows, :].rearrange("(s p) f -> p s f", p=128), vbf, [R_sl[7]], [R_vscr], R_sl[7], e="pool")
        dma(ktscr[:, :, rows].rearrange("h p t -> p h t"), kTn, [R_kn], [R_ktscr], R_kn, e="pool")
        nk = nkt * 128

        def kv_bufs(hd):
            if j == 0:
                return (slot_bf(8)[:, 2048:2560], R_sl[8], v3(slot_bf(8), 4, 128), R_sl[8])
            if hd % 2 == 0:
                return (slot_bf(3)[:, 0:SEQ], R_sl[3], v3(slot_bf(8), 32, 128), R_sl[8])
            return (slot_bf(2)[:, 0:SEQ], R_sl[2], v3(slot_bf(4), 32, 128), R_sl[4])

        def kv_load(hd):
            kh, R_kh, vh, R_vh = kv_bufs(hd)
            dma(kh[:, 0:nk], ktscr[hd, :, 0:nk], [R_ktscr], [R_kh], R_kh)
            dma(vh[:, 0:nkt, :], vscr[0:nk, hd * 128:(hd + 1) * 128].rearrange("(t p) e -> p t e", p=128),
                [R_vscr], [R_vh], R_vh)

        kv_load(0)
        for hd in range(8):
            kh, R_kh, vh, R_vh = kv_bufs(hd)
            if j == 0 and hd > 0:
                kv_load(hd)
            if hd + 1 < 8 and j > 0:
                kv_load(hd + 1)
            A = [None, None]
            steps = [(m, kt) for m in range(2) for kt in range(nkt)]

            def s_issue(i):
                m, kt = steps[i]
                bsn = i % 4
                mm(banks[bsn][:, :], kh[m * 64:(m + 1) * 64, kt * 128:(kt + 1) * 128], qT[m * 64:(m + 1) * 64, hd, 0:TB],
                   True, True, [R_kh, R_q], [R_bank[bsn]])

            LOOK = 2
            for i in range(min(LOOK, len(steps))):
                s_issue(i)
            for i, (m, kt) in enumerate(steps):
                if i + LOOK < len(steps):
                    s_issue(i + LOOK)
                bo = 4 + 2 * m
                bl = 5 + 2 * m
                bsn = i % 4
                pT = slot_bf(6)[:, (i % 4) * 512:(i % 4 + 1) * 512]
                R_p = R_pT[i % 4]
                act(pT, banks[bsn][:, :], AF.Exp, [R_bank[bsn]], [R_p, R_sl[6]], scale=0.125)
                dk = kt - 4 * j
                if dk >= 0:
                    tt(pT, pT, mask_bf[:, dk, :], ALU.mult, [R_p, R_c], [R_p], e="pool")
                mm(banks[bo][:, :], vh[:, kt, :], pT, kt == 0, kt == nkt - 1, [R_vh, R_p], [R_bank[bo]])
                mm(banks[bl][:, :], ones_bf[:, :], pT, kt == 0, kt == nkt - 1, [R_p, R_c], [R_bank[bl]])
                if kt == nkt - 1:
                    tm, rt = tmp()
                    act(tm[:, :], banks[bl][:, :], AF.Ln, [R_bank[bl]], [rt])
                    act(tm[:, :], tm[:, :], AF.Exp, [rt], [rt], scale=-1.0)
                    am = slot(5)[:, m * 512:(m + 1) * 512]
                    tt(am, banks[bo][:, :], tm[:, :], ALU.mult, [R_bank[bo], rt], [R_sl[5]])
                    A[m] = am
            dd = slot(5)[:, 1024:1536]
            stt(dd, A[1], lamc[:, 1:2], A[0], ALU.mult, ALU.add, [R_sl[5], R_lam], [R_sl[5]])
            sqd = slot_bf(5)[:, 3072:3584]
            act(sqd, dd, AF.Square, [R_sl[5]], [R_sl[5]])
            b = bank_ptr[0] % 4
            bank_ptr[0] += 1
            mm(banks[b][:, :], ones_bf[:, :], sqd, True, True, [R_sl[5], R_c], [R_bank[b]])
            tm, rt = tmp()
            act(tm[:, :], banks[b][:, :], AF.Ln, [R_bank[b]], [rt], bias=EPS, scale=1.0 / 128)
            act(tm[:, :], tm[:, :], AF.Exp, [rt], [rt], scale=-0.5)
            stt(att[:, hd, 0:TB], dd, subg[:, 0:1], tm[:, :], ALU.mult, ALU.mult, [R_sl[5], R_lam, rt], [R_att])
        if j == 0 and "satt" not in os.environ.get("KSKIP", ""):
            sample_attention(stok, att, R_att)
        linear_fm(j, attn_w_out[0], 8, att, R_att, add_to_x)

    def sample_attention(stok, att, R_att):
        ck_rows = ck
        cv_rows = cv
        S_all = v3(slot(4)[:, 1100:2192], NPAGE + 1, 16)
        R_S = R_sl[4]
        attok = slot(2)
        R_at = R_sl[2]
        bat = [6, 7]
        for s in range(NS):
            qkv = slot(5, 2)
            R_qkv = [R_sl[5], R_sl[6]]
            for t in range(6):
                b = bank_ptr[0] % 4
                bank_ptr[0] += 1
                sap, sres = stok(t * 512)
                mm(banks[b][:, :], sel_t[0:NS, s * 128:(s + 1) * 128], sap, True, True,
                   [sres, R_c], [R_bank[b]])
                cp(qkv[:, t * 512:(t + 1) * 512], banks[b][:, :], [R_bank[b]], R_qkv, e=("act" if t % 2 else "dve"))
            qbc = qkv[:, 0:D]
            pgk = [slot(7)[:, 0:D], slot(8)[:, 0:D]]
            R_pgk = [R_sl[7], R_sl[8]]
            prod = slot(0)[:, 0:D]
            for pg in range(NPAGE + 1):
                if pg < NPAGE:
                    kb = pgk[pg % 2]
                    rk = R_pgk[pg % 2]
                    col = s * NPAGE + pg
                    P.emit("pool", lambda h, kb=kb, col=col: h.indirect_dma_start(
                        out=kb, out_offset=None, in_=ck_rows,
                        in_offset=bass.IndirectOffsetOnAxis(ap=idx_t[:, col:col + 1].bitcast(U32), axis=0)),
                        reads=[R_idx], writes=[rk], dma=rk)
                    src, rsrc = kb, [rk]
                else:
                    src, rsrc = qkv[:, D:2 * D], R_qkv
                tt(prod, src, qbc, ALU.mult, rsrc + R_qkv, [R_sl[0]])
                P.emit("dve", lambda h, pg=pg: h.reduce_sum(S_all[:, pg, :], prod.rearrange("p (a d) -> p a d", d=64), AX.X),
                       reads=[R_sl[0]], writes=[R_S])
            mx = small[:, 32:48]
            P.emit("dve", lambda h: h.reduce_max(mx, S_all[:, :, :].rearrange("p g a -> p a g"), AX.X),
                   reads=[R_S], writes=[R_small])
            b = bank_ptr[0] % 4
            bank_ptr[0] += 1
            tr(banks[b][0:16, 0:128], mx, [R_small], [R_bank[b]])
            m16 = small[0:16, 48:49]
            P.emit("dve", lambda h, b=b: h.reduce_max(m16, banks[b][0:16, 0:128], AX.X), reads=[R_bank[b]], writes=[R_small])
            dg = slot(0)[0:16, 0:16]
            tsc(dg, ident[0:16, 0:16], m16, None, ALU.mult, None, [R_small, R_c], [R_sl[0]])
            b = bank_ptr[0] % 4
            bank_ptr[0] += 1
            mm(banks[b][:, 0:16], ones_f[0:16, :], dg, True, True, [R_sl[0], R_c], [R_bank[b]])
            mbc = small[:, 16:32]
            cp(mbc, banks[b][:, 0:16], [R_bank[b]], [R_small])
            tt(S_all[:, :, :], S_all[:, :, :], mbc.unsqueeze(1).to_broadcast([128, NPAGE + 1, 16]), ALU.subtract,
               [R_S, R_small], [R_S])
            act(S_all[:, :, :], S_all[:, :, :], AF.Exp, [R_S], [R_S], scale=0.125)
            tsc(S_all[:, NPAGE, :], S_all[:, NPAGE, :], e0_t[:, 0:1], None, ALU.mult, None, [R_S, R_c], [R_S])
            ps = small[:, 0:16]
            P.emit("dve", lambda h: h.reduce_sum(ps, S_all[:, :, :].rearrange("p g a -> p a g"), AX.X),
                   reads=[R_S], writes=[R_small])
            lq = small[0:8, 50:52]
            for m in range(2):
                b = bank_ptr[0] % 4
                bank_ptr[0] += 1
                mm(banks[b][0:8, 0:1], ps.rearrange("p (h m) -> p m h", m=2)[:, m, :], ones_f[:, 0:1], True, True,
                   [R_small, R_c], [R_bank[b]])
                cp(lq[:, m:m + 1], banks[b][0:8, 0:1], [R_bank[b]], [R_small])
            P.emit("dve", lambda h: h.reciprocal(lq, lq), reads=[R_small], writes=[R_small])
            Sv = S_all[:, :, :].rearrange("p g (h m) -> p g m h", m=2)
            for pg in range(NPAGE + 1):
                if pg < NPAGE:
                    vb = pgk[pg % 2]
                    rv = R_pgk[pg % 2]
                    col = s * NPAGE + pg
                    P.emit("pool", lambda h, vb=vb, col=col: h.indirect_dma_start(
                        out=vb, out_offset=None, in_=cv_rows,
                        in_offset=bass.IndirectOffsetOnAxis(ap=idx_t[:, col:col + 1].bitcast(U32), axis=0)),
                        reads=[R_idx], writes=[rv], dma=rv)
                    src, rsrc = vb, [rv]
                else:
                    src, rsrc = qkv[:, 2 * D:3 * D], R_qkv
                for m in range(2):
                    for hf in range(2):
                        bb = 4 + 2 * m + hf
                        mm(banks[bb][0:8, :], Sv[:, pg, m, :], src[:, hf * 512:(hf + 1) * 512], pg == 0, pg == NPAGE,
                           [R_S] + rsrc, [R_bank[bb]], inc=(pg == NPAGE or (m == 1 and hf == 1)))
            dmat = slot(0)[0:8, 0:D]
            n2 = slot(0)[0:8, D:2 * D]
            for hf in range(2):
                cs = slice(hf * 512, (hf + 1) * 512)
                stt(dmat[:, cs], banks[4 + hf][0:8, :], lq[:, 0:1], bmask_t[:, cs], ALU.mult, ALU.mult,
                    [R_bank[4 + hf], R_small, R_c], [R_sl[0]])
                stt(n2[:, cs], banks[6 + hf][0:8, :], lq[:, 1:2], bmask_t[:, cs], ALU.mult, ALU.mult,
                    [R_bank[6 + hf], R_small, R_c], [R_sl[0]])
            stt(dmat, n2, lamc[0:8, 1:2], dmat, ALU.mult, ALU.add, [R_sl[0], R_lam], [R_sl[0]])
            for hf in range(2):
                b = bank_ptr[0] % 4
                bank_ptr[0] += 1
                mm(banks[b][0:NS, :], sel2_t[0:8, s * NS:(s + 1) * NS], dmat[:, hf * 512:(hf + 1) * 512], True, True,
                   [R_sl[0], R_c], [R_bank[b]])
                if s == 0:
                    cp(attok[0:NS, hf * 512:(hf + 1) * 512], banks[b][0:NS, :], [R_bank[b]], [R_at])
                else:
                    tt(attok[0:NS, hf * 512:(hf + 1) * 512], banks[b][0:NS, :], attok[0:NS, hf * 512:(hf + 1) * 512], ALU.add,
                       [R_bank[b], R_at], [R_at])
        a3 = attok[0:NS, 0:D].rearrange("p (h e) -> p h e", h=8)
        sq3 = slot(0)[0:NS, 0:D].rearrange("p (h e) -> p h e", h=8)
        tt(sq3, a3, a3, ALU.mult, [R_at], [R_sl[0]])
        ssum = small[0:NS, 52:60]
        P.emit("dve", lambda h: h.reduce_sum(ssum, sq3, AX.X), reads=[R_sl[0]], writes=[R_small])
        act(ssum, ssum, AF.Sqrt, [R_small], [R_small], bias=EPS, scale=1.0 / 128)
        P.emit("dve", lambda h: h.reciprocal(ssum, ssum), reads=[R_small], writes=[R_small])
        tt(a3, a3, ssum.unsqueeze(2).to_broadcast([NS, 8, 128]), ALU.mult, [R_at, R_small], [R_at])
        for hd in range(8):
            b = bank_ptr[0] % 4
            bank_ptr[0] += 1
            tr(banks[b][:, 0:NS], attok[0:NS, hd * 128:(hd + 1) * 128], [R_at], [R_bank[b]], n=NS)
            tsc(att[:, hd, TB:NCOL], banks[b][:, 0:NS], subg[:, 0:1], None, ALU.mult, None, [R_bank[b], R_lam], [R_att])


    kstop = int(os.environ.get("KSTOP", "1000000"))
    stages = [setup, mem_setup]
    for j in range(NBLK):
        stages += [lambda j=j: load_block(j), lambda j=j: ffn(j, 0, 0), lambda j=j: even_mixer(j), lambda j=j: xattn(j, 0),
                   lambda j=j: ffn(j, 0, 1), lambda j=j: ffn(j, 1, 0), lambda j=j: attention(j), lambda j=j: xattn(j, 1),
                   lambda j=j: ffn(j, 1, 1), lambda j=j: final_block(j)]
    for dry in (True, False):
        P.dry = dry
        bank_ptr[0] = 0
        tmp_ptr[0] = 0
        wcnt[0] = 0
        for i, st_ in enumerate(stages):
            if i >= kstop:
                break
            st_()
    P.emit("sp", lambda h: h.nop(), reads=[R_out], writes=[])

    with nc.Block() as block:
        @block.sync
        def _(e):
            P.replay("sp", e)

        @block.tensor
        def _(e):
            P.replay("pe", e)

        @block.scalar
        def _(e):
            P.replay("act", e)

        @block.vector
        def _(e):
            P.replay("dve", e)

        @block.gpsimd
        def _(e):
            P.replay("pool", e)
    return nc, es


_CACHE = {}


def _consts():
    ident = np.eye(128, dtype=np.float32)
    mask = np.zeros((128, 4, 512), np.float32)
    p = np.arange(128)[:, None]
    f = np.arange(512)[None, :]
    for dk in range(4):
        mask[:, dk, :] = (dk * 128 + p <= f).astype(np.float32)
    sel = np.zeros((NS, NS, 128), np.float32)
    for s in range(NS):
        sel[s, s, :] = 1.0
    e0 = np.zeros((128, 1), np.float32)
    e0[0, 0] = 1.0
    bmask = np.zeros((8, 8, 128), np.float32)
    for h in range(8):
        bmask[h, h, :] = 1.0
    sel2 = np.zeros((8, NS, NS), np.float32)
    for s in range(NS):
        sel2[:, s, s] = 1.0
    iota = np.arange(128, dtype=np.float32).reshape(128, 1)
    return dict(c_ident=ident, c_mask=mask.reshape(128, 2048), c_sel=sel.reshape(NS, NS * 128), c_e0=e0,
                c_bmask=bmask.reshape(8, D), c_sel2=sel2.reshape(8, NS * NS), c_iota=iota)


def kernel(**inp):
    if "nc" not in _CACHE:
        _CACHE["nc"] = build_program()
    nc, _es = _CACHE["nc"]
    f = lambda a: np.ascontiguousarray(np.asarray(a))
    consts = _consts()
    wnames = ["ffn1_g", "ffn1_w_in", "ffn1_w_out", "ffn2_g", "ffn2_w_in", "ffn2_w_out", "mix_g", "even_w_in", "conv_a_w",
              "conv_a_b", "conv_a_ln_g", "conv_a_ln_b", "conv_b_w", "conv_b_b", "lru_w_a", "lru_b_a", "lru_w_x", "lru_b_x",
              "lru_lambda", "even_w_out", "attn_w_in", "lam_q1", "lam_k1", "lam_q2", "lam_k2", "attn_subln_g", "attn_w_out",
              "xattn_g", "xattn_w_q", "xattn_w_kv", "xattn_w_out"]
    shared = {n: f(inp[n]) for n in wnames}
    shared["final_g"] = f(inp["final_g"]).reshape(1, D)
    shared["ck"] = f(inp["cache_k"]).reshape(2560 * 128, D)
    shared["cv"] = f(inp["cache_v"]).reshape(2560 * 128, D)
    shared.update(consts)
    in_maps = []
    for c in range(8):
        b = c % 4
        m = dict(shared)
        m["xp"] = f(inp["x_prompt"][b])
        m["xs"] = f(inp["x_sample"][4 * c:4 * c + 4, 0])
        m["sca"] = f(inp["state_conv_a"][0, 4 * c:4 * c + 4])
        m["scb"] = f(inp["state_conv_b"][0, 4 * c:4 * c + 4])
        m["slru"] = f(inp["state_lru"][0, 4 * c:4 * c + 4])
        m["cmk"] = f(inp["cache_mem_k"][:, 4 * c:4 * c + 4]).reshape(2, NS, 256, D)
        m["cmv"] = f(inp["cache_mem_v"][:, 4 * c:4 * c + 4]).reshape(2, NS, 256, D)
        m["pt"] = f(inp["page_table"][4 * c:4 * c + 4]).astype(np.int32)
        m["mp"] = f(inp["mem_prompt"][b])
        in_maps.append(m)
    res = run_bass_kernel_spmd(nc, in_maps, core_ids=list(range(8))).results
    R = lambda name, cs: np.stack([res[c][name] for c in cs])
    p4 = range(4)
    a8 = range(8)
    y_p = R("o_y_p", p4)
    y_s = np.concatenate([res[c]["o_y_s"] for c in a8])[:, None, :]
    p_conv_a = R("o_p_conv_a", p4)[None]
    p_conv_b = R("o_p_conv_b", p4)[None]
    p_lru = R("o_p_lru", p4).reshape(1, 4, 512)
    p_k = R("o_p_k", p4).reshape(1, 4, SEQ_FULL, 8, 128)
    p_v = R("o_p_v", p4).reshape(1, 4, SEQ_FULL, 8, 128)
    p_mem_k = np.stack([res[c]["o_p_mem_k"] for c in p4], axis=1).reshape(2, 4, 256, 4, 256)
    p_mem_v = np.stack([res[c]["o_p_mem_v"] for c in p4], axis=1).reshape(2, 4, 256, 4, 256)
    s_conv_a = np.concatenate([res[c]["o_s_conv_a"] for c in a8])[None]
    s_conv_b = np.concatenate([res[c]["o_s_conv_b"] for c in a8])[None]
    s_lru = np.concatenate([res[c]["o_s_lru"] for c in a8])[None]
    s_k = np.concatenate([res[c]["o_s_k"] for c in a8]).reshape(1, 32, 1, 8, 128)
    s_v = np.concatenate([res[c]["o_s_v"] for c in a8]).reshape(1, 32, 1, 8, 128)
    outs = (y_p, y_s, p_conv_a, p_conv_b, p_lru, p_k, p_v, p_mem_k, p_mem_v, s_conv_a, s_conv_b, s_lru, s_k, s_v)
    return tuple(np.ascontiguousarray(o, dtype=np.float32) for o in outs)
```

```python
import math
import os
from contextlib import ExitStack

import numpy as np
import concourse.bass as bass
import concourse.mybir as mybir
from concourse.bass_utils import run_bass_kernel_spmd

F32 = mybir.dt.float32
BF16 = mybir.dt.bfloat16
I32 = mybir.dt.int32
U32 = mybir.dt.uint32
AF = mybir.ActivationFunctionType
ALU = mybir.AluOpType
AX = mybir.AxisListType

D = 1024
DFF = 2816
NHC = 22
TB = 512
NBLK = 8
NS = 4
NCOL = TB + NS
EPS = 1e-6
SLOT = 2192
NSLOT = 9
LAM_INIT = 0.8 - 0.6 * math.exp(-0.3 * 1)
NPAGE = 64
SEQ_FULL = NBLK * TB


class Res:
    __slots__ = ("name", "w", "r", "dsem", "dcnt", "excl", "nowaw")

    def __init__(self, name, excl=False, nowaw=False):
        self.name = name
        self.excl = excl
        self.nowaw = nowaw
        self.w = {}
        self.r = {}
        self.dsem = None
        self.dcnt = 0


class Prog:
    def __init__(self, nc, es):
        self.nc = nc
        self.es = es
        self.sems = {}
        self.eng = {}
        for name, h in (("pe", nc.tensor), ("act", nc.scalar), ("dve", nc.vector),
                        ("pool", nc.gpsimd), ("sp", nc.sync)):
            sem = self.new_sem("e_" + name)
            self.eng[name] = dict(h=h, sem=sem, n=0, ops=[], seen={}, pend=False)
        self.all_dma = []
        self.dry = False

    def new_sem(self, name):
        s = self.es.enter_context(self.nc.semaphore(name))
        self.sems[name] = s
        return name

    def res_sem(self, r, sw=False):
        if r.dsem is None:
            r.dsem = {}
            r.dcnt = {}
        if sw not in r.dsem:
            r.dsem[sw] = self.new_sem(("q_" if sw else "d_") + r.name)
            r.dcnt[sw] = 0
        return r.dsem[sw]

    @staticmethod
    def flat(xs):
        out = []
        for x in xs:
            if isinstance(x, (list, tuple)):
                out.extend(Prog.flat(x))
            else:
                out.append(x)
        return out

    def emit(self, e, fn, reads=(), writes=(), inc=True, dma=None, guards=()):
        if self.dry:
            return None
        reads = self.flat(reads)
        writes = self.flat(writes)
        E = self.eng[e]
        deps = {}
        for g in guards:
            for s, v in list(g.w.items()) + list(g.r.items()):
                deps[s] = max(deps.get(s, 0), v)
        sw = (e == "pool")
        own_dma = self.res_sem(dma, sw) if dma is not None else None
        for r in reads:
            for s, v in r.w.items():
                deps[s] = max(deps.get(s, 0), v)
            if r.excl:
                for s, v in r.r.items():
                    if s != E["sem"]:
                        deps[s] = max(deps.get(s, 0), v)
        for w in writes:
            for s, v in w.w.items():
                if s == own_dma or w.nowaw:
                    continue
                deps[s] = max(deps.get(s, 0), v)
            for s, v in w.r.items():
                deps[s] = max(deps.get(s, 0), v)
        for s, v in deps.items():
            if e == "pe" and s == E["sem"]:
                continue
            if E["seen"].get(s, 0) >= v:
                continue
            E["ops"].append(("w", s, v))
            E["seen"][s] = v
        if dma is not None:
            dma.dcnt[sw] += 16
            tick = (own_dma, dma.dcnt[sw])
            E["ops"].append(("d", fn, own_dma))
        else:
            tick = (E["sem"], E["n"] + 1)
            E["ops"].append(("o", fn, inc))
            if inc:
                E["n"] += 1
                E["pend"] = False
            else:
                E["pend"] = True
        for r in reads:
            r.r[tick[0]] = max(r.r.get(tick[0], 0), tick[1])
        for w in writes:
            w.w[tick[0]] = max(w.w.get(tick[0], 0), tick[1])
        return tick

    def replay(self, e, h):
        E = self.eng[e]
        assert not E["pend"], e
        for op in E["ops"]:
            if op[0] == "w":
                h.wait_ge(self.sems[op[1]], op[2])
            elif op[0] == "d":
                op[1](h).then_inc(self.sems[op[2]], 16)
            else:
                ins = op[1](h)
                if op[2]:
                    ins.then_inc(self.sems[E["sem"]], 1)


def build_program(nblk=NBLK, npage=NPAGE, npool=2560):
    SEQ = nblk * TB
    NBLK = nblk
    NPAGE = npage
    nc = bass.Bass("TRN2", target_bir_lowering=False)
    es = ExitStack()
    P = Prog(nc, es)

    def din(name, shape, dt=F32):
        return nc.dram_tensor(name, list(shape), dt, kind="ExternalInput").ap()

    def dout(name, shape):
        return nc.dram_tensor(name, list(shape), F32, kind="ExternalOutput").ap()

    xp = din("xp", (SEQ, D))
    xs = din("xs", (NS, D))
    sca = din("sca", (NS, 30, 512))
    scb = din("scb", (NS, 3, 512))
    slru = din("slru", (NS, 512))
    ck = din("ck", (npool * 128, D))
    cv = din("cv", (npool * 128, D))
    cmk = din("cmk", (2, NS, 256, D))
    cmv = din("cmv", (2, NS, 256, D))
    pt = din("pt", (NS, NPAGE), I32)
    mp = din("mp", (256, D))
    ffn_g = [din("ffn1_g", (2, D)), din("ffn2_g", (2, D))]
    ffn_wi = [din("ffn1_w_in", (2, D, 2 * DFF)), din("ffn2_w_in", (2, D, 2 * DFF))]
    ffn_wo = [din("ffn1_w_out", (2, DFF, D)), din("ffn2_w_out", (2, DFF, D))]
    mix_g = din("mix_g", (2, D))
    even_w_in = din("even_w_in", (1, D, 2048))
    conv_a_w = din("conv_a_w", (1, 31, 512))
    conv_a_b = din("conv_a_b", (1, 512))
    conv_a_ln_g = din("conv_a_ln_g", (1, 512))
    conv_a_ln_b = din("conv_a_ln_b", (1, 512))
    conv_b_w = din("conv_b_w", (1, 4, 512))
    conv_b_b = din("conv_b_b", (1, 512))
    lru_w_a = din("lru_w_a", (1, 8, 64, 64))
    lru_b_a = din("lru_b_a", (1, 512))
    lru_w_x = din("lru_w_x", (1, 8, 64, 64))
    lru_b_x = din("lru_b_x", (1, 512))
    lru_lambda = din("lru_lambda", (1, 512))
    even_w_out = din("even_w_out", (1, D, D))
    attn_w_in = din("attn_w_in", (1, D, 3072))
    lam_q1 = din("lam_q1", (1, 64))
    lam_k1 = din("lam_k1", (1, 64))
    lam_q2 = din("lam_q2", (1, 64))
    lam_k2 = din("lam_k2", (1, 64))
    attn_subln_g = din("attn_subln_g", (1, 128))
    attn_w_out = din("attn_w_out", (1, D, D))
    xattn_g = din("xattn_g", (2, D))
    xattn_w_q = din("xattn_w_q", (2, D, D))
    xattn_w_kv = din("xattn_w_kv", (2, D, 2 * D))
    xattn_w_out = din("xattn_w_out", (2, D, D))
    final_g = din("final_g", (1, D))
    c_ident = din("c_ident", (128, 128))
    c_mask = din("c_mask", (128, 4 * 512))
    c_sel = din("c_sel", (NS, NS * 128))
    c_e0 = din("c_e0", (128, 1))
    c_bmask = din("c_bmask", (16, D))
    c_sel2 = din("c_sel2", (16, NS * NS))
    c_par = din("c_par", (16, 2))
    c_iota = din("c_iota", (128, 1))

    o_y_p = dout("o_y_p", (SEQ, D))
    o_y_s = dout("o_y_s", (NS, D))
    o_p_conv_a = dout("o_p_conv_a", (30, 512))
    o_p_conv_b = dout("o_p_conv_b", (3, 512))
    o_p_lru = dout("o_p_lru", (1, 512))
    o_p_k = dout("o_p_k", (SEQ, D))
    o_p_v = dout("o_p_v", (SEQ, D))
    o_p_mem_k = dout("o_p_mem_k", (2, 256, D))
    o_p_mem_v = dout("o_p_mem_v", (2, 256, D))
    o_s_conv_a = dout("o_s_conv_a", (NS, 30, 512))
    o_s_conv_b = dout("o_s_conv_b", (NS, 3, 512))
    o_s_lru = dout("o_s_lru", (NS, 512))
    o_s_k = dout("o_s_k", (NS, D))
    o_s_v = dout("o_s_v", (NS, D))

    ktscr = nc.dram_tensor("ktscr", [8, 128, SEQ], BF16).ap()
    vscr = nc.dram_tensor("vscr", [SEQ, D], BF16).ap()
    R_ktscr = Res("ktscr")
    R_vscr = Res("vscr")
    R_out = Res("outs", nowaw=True)

    def sb(name, shape, dt=F32):
        return es.enter_context(nc.sbuf_tensor(name, list(shape), dt))

    x_t = sb("x", [128, 8, NCOL]);          R_x = Res("x")
    xn_t = sb("xn", [128, 8, NCOL], BF16);  R_xn = Res("xn")
    rs_t = sb("rs", [128, NCOL]);           R_rs = Res("rs")
    arena = sb("arena", [128, NSLOT * SLOT])
    R_sl = [Res("sl%d" % i) for i in range(NSLOT)]
    wst = [sb("wst%d" % i, [128, 4096]) for i in range(2)]
    R_wst = [Res("wst%d" % i) for i in range(2)]
    wbf = [sb("wbf%d" % i, [128, 4096], BF16) for i in range(2)]
    R_wbf = [[Res("wbf%da" % i), Res("wbf%db" % i)] for i in range(2)]
    ts_t = sb("ts", [128, 4, D]);           R_ts = Res("ts")
    memKT = sb("memKT", [128, 2, 8, 256], BF16); R_memKT = Res("memKT")
    memV = sb("memV", [128, 2, 2, D], BF16);     R_memV = Res("memV")
    ident = sb("ident", [128, 128]);        R_c = Res("consts")
    ones_bf = sb("ones_bf", [128, 128], BF16)
    ident_bf = sb("ident_bf", [128, 128], BF16)
    R_dg = [Res("dg%d" % i) for i in range(4)]
    ones_f = sb("ones_f", [128, 128])
    mask_bf = sb("mask_bf", [128, 4, 512], BF16)
    gains = sb("gains", [128, 9, 8])
    cw_a = sb("cw_a", [128, 4, 32])
    cw_b = sb("cw_b", [128, 4, 4])
    vecs = sb("vecs", [128, 10, 4])
    wabd = sb("wabd", [128, 4, 128], BF16)
    wxbd = sb("wxbd", [128, 4, 128], BF16)
    ucarry = sb("ucarry", [128, 4, 30]);    R_carry = Res("carry")
    rcarry = sb("rcarry", [128, 4, 3])
    hcarry = sb("hcarry", [128, 4])
    tmp_t = [sb("tmp%d" % i, [128, 512]) for i in range(2)]
    R_tmp = [Res("tmp%d" % i) for i in range(2)]
    small = sb("small", [128, 64]);         R_small = Res("small")
    lamc = sb("lamc", [128, 4]);            R_lam = Res("lam")
    subg = sb("subg", [128, 1])
    sel_t = sb("sel", [NS, NS * 128])
    e0_t = sb("e0", [128, 1])
    bmask_t = sb("bmask", [16, D])
    sel2_t = sb("sel2", [16, NS * NS])
    par_t = sb("par", [16, 4])
    iota_t = sb("iota", [128, 1])
    idxf_t = sb("idxf", [128, NS * NPAGE])
    idx_t = sb("idx", [128, NS * NPAGE], I32); R_idx = Res("idx")
    samp = sb("samp", [128, 4, NS, 32]);    R_samp = Res("samp")
    sampb = sb("sampb", [128, 4, NS, 4])
    samph = sb("samph", [128, 4, NS])

    banks = [es.enter_context(nc.psum_tensor("bank%d" % i, [128, 512], F32)) for i in range(8)]
    R_bank = [Res("bank%d" % i, excl=True) for i in range(8)]
    bank_ptr = [0]

    def bank():
        i = bank_ptr[0] % 8
        bank_ptr[0] += 1
        return i

    def slot(i, n=1):
        return arena[:, i * SLOT:(i + n) * SLOT]

    def slot_bf(i, n=1):
        return slot(i, n).bitcast(BF16)

    def v3(ap, c, t):
        return ap[:, 0:c * t].rearrange("p (c t) -> p c t", c=c)

    tmp_ptr = [0]

    def tmp():
        i = tmp_ptr[0] % 2
        tmp_ptr[0] += 1
        return tmp_t[i], R_tmp[i]

    def mm(out, lhsT, rhs, start, stop, reads, writes, inc=None):
        if inc is None:
            inc = stop
        P.emit("pe", lambda h: h.matmul(out, lhsT, rhs, start=start, stop=stop),
               reads=reads, writes=writes, inc=inc)

    def tr(out, in_, reads, writes, n=128):
        P.emit("pe", lambda h: h.transpose(out, in_, ident[0:n, 0:n]), reads=list(reads) + [R_c], writes=writes)

    def act(out, in_, func, reads, writes, bias=0.0, scale=1.0):
        P.emit("act", lambda h: h.activation(out, in_, func, bias=bias, scale=scale), reads=reads, writes=writes)

    def tt(out, a, b, op, reads, writes, e="dve"):
        P.emit(e, lambda h: h.tensor_tensor(out, a, b, op), reads=reads, writes=writes)

    def tsc(out, a, s1, s2, op0, op1, reads, writes, e="dve"):
        if s2 is None:
            P.emit(e, lambda h: h.tensor_scalar(out, a, s1, None, op0), reads=reads, writes=writes)
        else:
            P.emit(e, lambda h: h.tensor_scalar(out, a, s1, s2, op0, op1), reads=reads, writes=writes)

    def stt(out, a, s, b, op0, op1, reads, writes):
        P.emit("dve", lambda h: h.scalar_tensor_tensor(out, a, s, b, op0, op1), reads=reads, writes=writes)

    def cp(out, in_, reads, writes, e="act"):
        if e == "act":
            P.emit("act", lambda h: h.copy(out, in_), reads=reads, writes=writes)
        else:
            P.emit(e, lambda h: h.tensor_copy(out, in_), reads=reads, writes=writes)

    def dma(out, in_, reads, writes, sem_res, e="sp"):
        P.emit(e, lambda h: h.dma_start(out=out, in_=in_), reads=reads, writes=writes, dma=sem_res)

    def store(out, in_, src_res):
        srcs = src_res if isinstance(src_res, (list, tuple)) else [src_res]
        dma(out, in_, list(srcs), [R_out], srcs[0], e="pool")

    wcnt = [0]
    wlist = []
    wdma = [0]
    wcast = [0]
    NST = len(wst)
    NBF = len(wbf)

    def w_issue_dma(upto):
        while wdma[0] <= upto and wdma[0] < len(wlist):
            i = wdma[0]
            wdma[0] += 1
            for dst_fn, src in wlist[i][0]:
                dma(dst_fn(wst[i % NST]), src, [], [R_wst[i % NST]], R_wst[i % NST])

    cast_act_only = [False]

    def w_issue_cast(upto):
        while wcast[0] <= upto and wcast[0] < len(wlist):
            i = wcast[0]
            wcast[0] += 1
            n_el = wlist[i][1]
            hsz = n_el if cast_act_only[0] else (n_el // 2 + 127) // 128 * 128
            s_, b_ = i % NST, i % NBF
            P.emit("act", lambda h, s_=s_, b_=b_, hsz=hsz: h.copy(wbf[b_][:, 0:hsz], wst[s_][:, 0:hsz]),
                   reads=[R_wst[s_]], writes=[R_wbf[b_][0]])
            if hsz < n_el:
                P.emit("dve", lambda h, s_=s_, b_=b_, hsz=hsz, n_el=n_el: h.tensor_copy(wbf[b_][:, hsz:n_el], wst[s_][:, hsz:n_el]),
                       reads=[R_wst[s_]], writes=[R_wbf[b_][1]])

    def wload(pieces, n_el):
        k = wcnt[0]
        wcnt[0] += 1
        if P.dry:
            wlist.append((pieces, n_el))
            return wbf[0], R_wbf[0]
        w_issue_dma(k + 1)
        w_issue_cast(k + 1)
        w_issue_dma(k + NST)
        return wbf[k % NBF], R_wbf[k % NBF]

    def wview(w2d):
        return w2d.rearrange("(kc p) n -> p kc n", p=128)

    def std_tile(w2d, c0, nk=8, ncol=512):
        src = wview(w2d)[:, 0:nk, c0:c0 + ncol]
        return [(lambda t: t[:, 0:nk * ncol].rearrange("p (k n) -> p k n", k=nk), src)], nk * ncol

    def coltiles(j):
        return [(0, TB)] + ([(TB, NCOL)] if j == 0 else [])

    def ncols(j):
        return NCOL if j == 0 else TB

    def rmsnorm(j, gi):
        nc_ = ncols(j)
        sq = v3(slot_bf(8), 8, NCOL)
        act(sq[:, :, 0:nc_], x_t[:, :, 0:nc_], AF.Square, [R_x], [R_sl[8]])
        for (c0, c1) in coltiles(j):
            b = bank()
            for kc in range(8):
                mm(banks[b][:, 0:c1 - c0], ones_bf[:, :], sq[:, kc, c0:c1], kc == 0, kc == 7,
                   [R_sl[8], R_c], [R_bank[b]])
            act(rs_t[:, c0:c1], banks[b][:, 0:c1 - c0], AF.Ln, [R_bank[b]], [R_rs], bias=EPS, scale=1.0 / D)
            act(rs_t[:, c0:c1], rs_t[:, c0:c1], AF.Exp, [R_rs], [R_rs], scale=-0.5)
            for kc in range(8):
                stt(xn_t[:, kc, c0:c1], x_t[:, kc, c0:c1], gains[:, gi, kc:kc + 1], rs_t[:, c0:c1],
                    ALU.mult, ALU.mult, [R_x, R_rs, R_c], [R_xn])

    def linear_fm(j, w2d, n_out_chunks, rhs3, rhs_res, evac, nk=8, oc0=0):
        ntile = (n_out_chunks + 3) // 4
        for t in range(ntile):
            wt, rw = wload(*std_tile(w2d, oc0 * 128 + t * 512, nk))
            w3 = wt[:, 0:nk * 512].rearrange("p (k n) -> p k n", k=nk)
            for oo in range(4):
                oc = oc0 + t * 4 + oo
                if oc >= oc0 + n_out_chunks:
                    break
                for (c0, c1) in coltiles(j):
                    b = bank()
                    for kc in range(nk):
                        mm(banks[b][:, 0:c1 - c0], w3[:, kc, oo * 128:(oo + 1) * 128], rhs3[:, kc, c0:c1],
                           kc == 0, kc == nk - 1, [rw, rhs_res], [R_bank[b]])
                    evac(oc, c0, c1, b)

    def ffn(j, l, which):
        gi = (0 if which == 0 else 2) + l
        rmsnorm(j, gi)
        wi = ffn_wi[which][l]
        wo = ffn_wo[which][l]
        hb = v3(slot_bf(0, 3), NHC, NCOL)
        R_h = R_sl[0:3]

        def in_tile(t):
            wv = wview(wi)
            pieces = [
                (lambda tl: tl[:, 0:4096].rearrange("p (k n) -> p k n", k=8)[:, :, 0:256], wv[:, :, t * 256:t * 256 + 256]),
                (lambda tl: tl[:, 0:4096].rearrange("p (k n) -> p k n", k=8)[:, :, 256:512],
                 wv[:, :, DFF + t * 256:DFF + t * 256 + 256]),
            ]
            return pieces, 4096

        for t in range(11):
            wt, rw = wload(*in_tile(t))
            w3 = wt[:, 0:4096].rearrange("p (k n) -> p k n", k=8)
            for hh in range(2):
                hc = 2 * t + hh
                for (c0, c1) in coltiles(j):
                    n = c1 - c0
                    bg = bank()
                    bu = bank()
                    for kc in range(8):
                        mm(banks[bg][:, 0:n], w3[:, kc, hh * 128:(hh + 1) * 128], xn_t[:, kc, c0:c1],
                           kc == 0, kc == 7, [rw, R_xn], [R_bank[bg]])
                    for kc in range(8):
                        mm(banks[bu][:, 0:n], w3[:, kc, 256 + hh * 128:256 + (hh + 1) * 128], xn_t[:, kc, c0:c1],
                           kc == 0, kc == 7, [rw, R_xn], [R_bank[bu]])
                    tm, rt = tmp()
                    act(tm[:, 0:n], banks[bg][:, 0:n], AF.Silu, [R_bank[bg]], [rt])
                    tt(hb[:, hc, c0:c1], tm[:, 0:n], banks[bu][:, 0:n], ALU.mult, [rt, R_bank[bu]], R_h)

        def out_tile(p, half):
            src = wview(wo)[:, half * 11:(half + 1) * 11, p * 256:(p + 1) * 256]
            return [(lambda tl: tl[:, 0:11 * 256].rearrange("p (k n) -> p k n", k=11), src)], 11 * 256

        seq = [(p, half) for p in range(4) for half in range(2)]
        bs = {}
        for i, (p, half) in enumerate(seq):
            wt, rw = wload(*out_tile(p, half))
            w3 = wt[:, 0:11 * 256].rearrange("p (k n) -> p k n", k=11)
            for fo in range(2):
                for (c0, c1) in coltiles(j):
                    n = c1 - c0
                    if half == 0:
                        bs[(fo, c0)] = bank()
                    b = bs[(fo, c0)]
                    for kk in range(11):
                        mm(banks[b][:, 0:n], w3[:, kk, fo * 128:(fo + 1) * 128], hb[:, half * 11 + kk, c0:c1],
                           half == 0 and kk == 0, half == 1 and kk == 10, [rw] + R_h, [R_bank[b]],
                           inc=(kk == 10))
                    if half == 1:
                        oc = 2 * p + fo
                        stt(x_t[:, oc, c0:c1], banks[b][:, 0:n], 0.5, x_t[:, oc, c0:c1], ALU.mult, ALU.add,
                            [R_bank[b], R_x], [R_x])

    def add_to_x(oc, c0, c1, b):
        tt(x_t[:, oc, c0:c1], banks[b][:, 0:c1 - c0], x_t[:, oc, c0:c1], ALU.add, [R_bank[b], R_x], [R_x])

    def store_T(src_fn, n, nchunk, dram2d, reads, stage_ap, stage_res):
        for g in range((nchunk + 3) // 4):
            b = bank()
            cs = range(g * 4, min(nchunk, g * 4 + 4))
            for c in cs:
                tr(banks[b][0:n, (c - g * 4) * 128:(c - g * 4 + 1) * 128], src_fn(c), reads, [R_bank[b]])
            w = len(cs) * 128
            cp(stage_ap[0:n, g * 512:g * 512 + w], banks[b][0:n, 0:w], [R_bank[b]], [stage_res])
        store(dram2d, stage_ap[0:n, 0:nchunk * 128], stage_res)

    def load_T(dst_fn, n, nchunk, dram2d, dst_res, stage_ap, stage_res):
        dma(stage_ap[0:n, 0:nchunk * 128], dram2d, [], [stage_res], stage_res)
        for c in range(nchunk):
            b = bank()
            tr(banks[b][:, 0:n], stage_ap[0:n, c * 128:(c + 1) * 128], [stage_res], [R_bank[b]], n=n)
            cp(dst_fn(c), banks[b][:, 0:n], [R_bank[b]], [dst_res])

    def setup():
        dma(ident[:, :], c_ident, [], [R_c], R_c)
        P.emit("pool", lambda h: h.memset(ones_f[:, :], 1.0), reads=[], writes=[R_c])
        cp(ident_bf[:, :], ident[:, :], [R_c], [R_c], e="dve")
        P.emit("pool", lambda h: h.memset(ones_bf[:, :], 1.0), reads=[], writes=[R_c])
        dma(ts_t[:, 0, :], c_mask[:, 0:1024], [], [R_ts], R_ts)
        dma(ts_t[:, 1, :], c_mask[:, 1024:2048], [], [R_ts], R_ts)
        cp(mask_bf[:, 0:2, :], ts_t[:, 0, :].rearrange("p (a b) -> p a b", a=2), [R_ts], [R_c], e="dve")
        cp(mask_bf[:, 2:4, :], ts_t[:, 1, :].rearrange("p (a b) -> p a b", a=2), [R_ts], [R_c], e="dve")
        dma(sel_t[:, :], c_sel, [], [R_c], R_c)
        dma(e0_t[:, :], c_e0, [], [R_c], R_c)
        dma(bmask_t[:, :], c_bmask, [], [R_c], R_c)
        dma(sel2_t[:, :], c_sel2, [], [R_c], R_c)
        dma(par_t[:, 0:2], c_par, [], [R_c], R_c)
        dma(iota_t[:, :], c_iota, [], [R_c], R_c)
        st = slot(0)
        glist = [ffn_g[0][0:1, :], ffn_g[0][1:2, :], ffn_g[1][0:1, :], ffn_g[1][1:2, :], mix_g[0:1, :], mix_g[1:2, :],
                 xattn_g[0:1, :], xattn_g[1:2, :], final_g[0:1, :]]
        for i, g in enumerate(glist):
            dma(st[i:i + 1, 0:D], g, [], [R_sl[0]], R_sl[0], e="pool")
        for c in range(8):
            b = bank()
            tr(banks[b][:, 0:9], st[0:9, c * 128:(c + 1) * 128], [R_sl[0]], [R_bank[b]], n=9)
            cp(gains[:, :, c], banks[b][:, 0:9], [R_bank[b]], [R_c])
        st1 = slot(1)
        vl = [conv_a_b, conv_a_ln_g, conv_a_ln_b, conv_b_b, lru_b_a, lru_b_x, lru_lambda]
        for i, v in enumerate(vl):
            dma(st1[i:i + 1, 0:512], v, [], [R_sl[1]], R_sl[1], e="pool")
        for c in range(4):
            b = bank()
            tr(banks[b][:, 0:7], st1[0:7, c * 128:(c + 1) * 128], [R_sl[1]], [R_bank[b]], n=7)
            cp(vecs[:, 0:7, c], banks[b][:, 0:7], [R_bank[b]], [R_c])
        st2 = slot(2)
        dma(st2[0:31, 0:512], conv_a_w[0], [], [R_sl[2]], R_sl[2])
        for c in range(4):
            b = bank()
            tr(banks[b][:, 0:31], st2[0:31, c * 128:(c + 1) * 128], [R_sl[2]], [R_bank[b]], n=31)
            cp(cw_a[:, c, 0:31], banks[b][:, 0:31], [R_bank[b]], [R_c])
        st3 = slot(3)
        dma(st3[0:4, 0:512], conv_b_w[0], [], [R_sl[3]], R_sl[3])
        for c in range(4):
            b = bank()
            tr(banks[b][:, 0:4], st3[0:4, c * 128:(c + 1) * 128], [R_sl[3]], [R_bank[b]], n=4)
            cp(cw_b[:, c, 0:4], banks[b][:, 0:4], [R_bank[b]], [R_c])
        st4 = v3(slot(4), 4, 128)
        st5 = v3(slot(5), 4, 128)
        P.emit("pool", lambda h: h.memset(slot(4)[:, 0:512], 0.0), reads=[], writes=[R_sl[4]])
        P.emit("pool", lambda h: h.memset(slot(5)[:, 0:512], 0.0), reads=[], writes=[R_sl[5]])
        for c in range(4):
            for hh in range(2):
                dma(st4[hh * 64:(hh + 1) * 64, c, hh * 64:(hh + 1) * 64], lru_w_a[0, 2 * c + hh], [], [R_sl[4]], R_sl[4])
                dma(st5[hh * 64:(hh + 1) * 64, c, hh * 64:(hh + 1) * 64], lru_w_x[0, 2 * c + hh], [], [R_sl[5]], R_sl[5])
        cp(wabd[:, :, :], st4, [R_sl[4]], [R_c], e="dve")
        cp(wxbd[:, :, :], st5, [R_sl[5]], [R_c], e="dve")
        lam_ = vecs[:, 6, :]
        a_ = small[:, 0:4]; e_ = small[:, 4:8]; z_ = small[:, 8:12]; z2 = small[:, 12:16]
        pl = small[:, 16:20]; t_ = small[:, 20:24]
        RS_ = [R_small, R_c]
        act(a_, lam_, AF.Abs, RS_, [R_small])
        act(e_, a_, AF.Exp, RS_, [R_small], scale=-1.0)
        tsc(t_, e_, 2.0, None, ALU.add, None, RS_, [R_small])
        P.emit("dve", lambda h: h.reciprocal(t_, t_), reads=RS_, writes=[R_small])
        tt(z_, e_, t_, ALU.mult, RS_, [R_small])
        tt(z2, z_, z_, ALU.mult, RS_, [R_small])
        tsc(pl, z2, 1.0 / 11.0, 1.0 / 9.0, ALU.mult, ALU.add, RS_, [R_small])
        for cf in (1.0 / 7.0, 1.0 / 5.0, 1.0 / 3.0, 1.0):
            tt(pl, pl, z2, ALU.mult, RS_, [R_small])
            tsc(pl, pl, cf, None, ALU.add, None, RS_, [R_small])
        tt(pl, pl, z_, ALU.mult, RS_, [R_small])
        tsc(t_, lam_, -1.0, 0.0, ALU.mult, ALU.max, RS_, [R_small])
        stt(pl, pl, 2.0, t_, ALU.mult, ALU.add, RS_, [R_small])
        tsc(vecs[:, 7, :], pl, -8.0, None, ALU.mult, None, RS_, [R_c])
        for i, v in enumerate((lam_q1, lam_k1, lam_q2, lam_k2)):
            dma(st[32:33, i * 64:(i + 1) * 64], v, [], [R_sl[0]], R_sl[0], e="pool")
        tt(st[32:33, 256:320], st[32:33, 0:64], st[32:33, 64:128], ALU.mult, [R_sl[0]], [R_sl[0]])
        tt(st[32:33, 320:384], st[32:33, 128:192], st[32:33, 192:256], ALU.mult, [R_sl[0]], [R_sl[0]])
        P.emit("dve", lambda h: h.reduce_sum(st[32:33, 384:386], st[32:33, 256:384].rearrange("p (a b) -> p a b", a=2), AX.X),
               reads=[R_sl[0]], writes=[R_sl[0]])
        act(st[32:33, 384:386], st[32:33, 384:386], AF.Exp, [R_sl[0]], [R_sl[0]])
        tt(st[32:33, 386:387], st[32:33, 384:385], st[32:33, 385:386], ALU.subtract, [R_sl[0]], [R_sl[0]])
        tsc(st[32:33, 387:388], st[32:33, 386:387], LAM_INIT, None, ALU.add, None, [R_sl[0]], [R_sl[0]])
        tsc(st[32:33, 388:389], st[32:33, 387:388], -1.0, None, ALU.mult, None, [R_sl[0]], [R_sl[0]])
        b = bank()
        mm(banks[b][:, 0:2], ones_f[32:33, :], st[32:33, 387:389], True, True, [R_sl[0], R_c], [R_bank[b]])
        cp(lamc[:, 0:2], banks[b][:, 0:2], [R_bank[b]], [R_lam])
        stt(par_t[:, 2:3], par_t[:, 1:2], lamc[0:16, 1:2], par_t[:, 0:1], ALU.mult, ALU.add, [R_c, R_lam], [R_c])
        tsc(sel2_t[:, :], sel2_t[:, :], par_t[:, 2:3], None, ALU.mult, None, [R_c], [R_c])
        dma(st[64:65, 0:128], attn_subln_g, [], [R_sl[0]], R_sl[0], e="pool")
        b = bank()
        mm(banks[b][:, 0:1], st[64:65, 0:128], ones_f[64:65, 0:1], True, True, [R_sl[0], R_c], [R_bank[b]])
        act(subg[:, :], banks[b][:, 0:1], AF.Copy, [R_bank[b]], [R_lam], scale=1.0 - LAM_INIT)
        P.emit("pool", lambda h: h.memset(ucarry[:, :, :], 0.0), reads=[], writes=[R_carry])
        P.emit("pool", lambda h: h.memset(rcarry[:, :, :], 0.0), reads=[], writes=[R_carry])
        P.emit("pool", lambda h: h.memset(hcarry[:, :], 0.0), reads=[], writes=[R_carry])
        dma(idx_t[:, :], pt.rearrange("s j -> (s j)").partition_broadcast(128), [], [R_idx], R_idx)
        cp(idxf_t[:, :], idx_t[:, :], [R_idx], [R_idx], e="dve")
        tsc(idxf_t[:, :], idxf_t[:, :], 128.0, iota_t[:, 0:1], ALU.mult, ALU.add, [R_idx, R_c], [R_idx])
        cp(idx_t[:, :], idxf_t[:, :], [R_idx], [R_idx], e="dve")

    def mem_setup():
        mt = v3(slot(0), 8, 256)
        mtb = v3(slot_bf(1), 8, 256)
        for r in range(2):
            load_T(lambda c, r=r: mt[:, c, r * 128:(r + 1) * 128], 128, 8, mp[r * 128:(r + 1) * 128, :], R_sl[0],
                   ts_t[:, r, :], R_ts)
        cp(mtb, mt, [R_sl[0]], [R_sl[1]], e="dve")
        ksub = int(os.environ.get("KSUB", "99"))
        if ksub < 1:
            return
        for l in range(2):
            w2d = xattn_w_kv[l]
            for t in range(4):
                wt, rw = wload(*std_tile(w2d, t * 512))
                if ksub < 2:
                    continue
                w3 = wt[:, 0:4096].rearrange("p (k n) -> p k n", k=8)
                if t < 2:
                    for oo in range(4):
                        b = bank()
                        for kc in range(8):
                            mm(banks[b][:, 0:256], w3[:, kc, oo * 128:(oo + 1) * 128], mtb[:, kc, :], kc == 0, kc == 7,
                               [rw, R_sl[1]], [R_bank[b]])
                        cp(memKT[:, l, t * 4 + oo, :], banks[b][:, 0:256], [R_bank[b]], [R_memKT])
                for r in range(2):
                    b = bank()
                    for kc in range(8):
                        mm(banks[b][:, :], mtb[:, kc, r * 128:(r + 1) * 128], w3[:, kc, :], kc == 0, kc == 7,
                           [rw, R_sl[1]], [R_bank[b]])
                    stg = v3(slot(2 + (t % 2)), 2, 512)
                    cp(stg[:, r, :], banks[b][:, :], [R_bank[b]], [R_sl[2 + (t % 2)]])
                    if ksub < 3:
                        continue
                    if t >= 2:
                        cp(memV[:, l, r, (t - 2) * 512:(t - 1) * 512], banks[b][:, :], [R_bank[b]], [R_memV])
                    dst = (o_p_mem_k if t < 2 else o_p_mem_v)[l, r * 128:(r + 1) * 128, (t % 2) * 512:(t % 2 + 1) * 512]
                    if ksub < 4:
                        continue
                    store(dst, stg[:, r, :], R_sl[2 + (t % 2)])

    def load_block(j):
        dma(ts_t[:, :, :], xp[j * TB:(j + 1) * TB, :].rearrange("(s p) f -> p s f", p=128), [], [R_ts], R_ts)
        for c in range(8):
            b = bank()
            for s in range(4):
                tr(banks[b][:, s * 128:(s + 1) * 128], ts_t[:, s, c * 128:(c + 1) * 128], [R_ts], [R_bank[b]])
            cp(x_t[:, c, 0:TB], banks[b][:, :], [R_bank[b]], [R_x], e=("act" if c % 2 else "dve"))
        if j == 0:
            load_T(lambda c: x_t[:, c, TB:NCOL], NS, 8, xs, R_x, slot(0), R_sl[0])

    def final_block(j):
        rmsnorm(j, 8)
        nc_ = ncols(j)
        yt = v3(slot(0, 2), 8, NCOL)
        for kc in range(8):
            stt(yt[:, kc, 0:nc_], x_t[:, kc, 0:nc_], gains[:, 8, kc:kc + 1], rs_t[:, 0:nc_], ALU.mult, ALU.mult,
                [R_x, R_rs, R_c], [R_sl[0], R_sl[1]])
        for s in range(4):
            for g in range(2):
                b = bank()
                for c in range(4):
                    tr(banks[b][:, c * 128:(c + 1) * 128], yt[:, g * 4 + c, s * 128:(s + 1) * 128], [R_sl[0], R_sl[1]],
                       [R_bank[b]])
                cp(ts_t[:, s, g * 512:(g + 1) * 512], banks[b][:, :], [R_bank[b]], [R_ts], e=("act" if g else "dve"))
        store(o_y_p[j * TB:(j + 1) * TB, :].rearrange("(s p) f -> p s f", p=128), ts_t[:, :, :], R_ts)
        if j == 0:
            store_T(lambda c: yt[:, c, TB:NCOL], NS, 8, o_y_s, [R_sl[0], R_sl[1]], slot(2), R_sl[2])

    def even_mixer(j):
        rmsnorm(j, 4)
        nc_ = ncols(j)
        W = TB + 30
        aval = v3(slot(0), 4, NCOL);  R_av = R_sl[0]
        uext = v3(slot(1), 4, W);     R_u = R_sl[1]
        rext = v3(slot(2), 4, TB + 3); R_r = R_sl[2]
        bg = v3(slot(3), 4, NCOL);    R_bg = R_sl[3]
        xr = v3(slot(4), 4, NCOL);    R_xr = R_sl[4]
        m5 = v3(slot(5), 4, NCOL);    R_5 = R_sl[5]
        m6 = v3(slot(6), 4, NCOL);    R_6 = R_sl[6]
        m7 = v3(slot(7), 4, NCOL);    R_7 = R_sl[7]
        mixin = v3(slot_bf(8), 8, NCOL); R_mi = R_sl[8]
        cp(uext[:, :, 0:30], ucarry[:, :, :], [R_carry], [R_u], e="pool")
        cp(rext[:, :, 0:3], rcarry[:, :, :], [R_carry], [R_r], e="pool")
        if j == 0:
            for s in range(NS):
                load_T(lambda c, s=s: samp[:, c, s, 0:30], 30, 4, sca[s], R_samp, slot(5), R_sl[5])
                load_T(lambda c, s=s: sampb[:, c, s, 0:3], 3, 4, scb[s], R_samp, slot(6), R_sl[6])
                if "d2d" not in os.environ.get("KSKIP", ""):
                    dma(o_s_conv_a[s, 0:29, :], sca[s, 1:30, :], [], [R_out], R_out)
                    dma(o_s_conv_b[s, 0:2, :], scb[s, 1:3, :], [], [R_out], R_out)
            load_T(lambda c: samph[:, c, :], NS, 4, slru, R_samp, slot(7), R_sl[7])

        def evac(oc, c0, c1, b):
            q, c = oc // 4, oc % 4
            n = c1 - c0
            samp_tile = c0 >= TB
            if q == 0:
                cp(aval[:, c, c0:c1], banks[b][:, 0:n], [R_bank[b]], [R_av])
            elif q == 1:
                tm, rt = tmp()
                act(tm[:, 0:n], banks[b][:, 0:n], AF.Sigmoid, [R_bank[b]], [rt])
                if samp_tile:
                    tt(samp[:, c, :, 30], tm[:, 0:n], aval[:, c, c0:c1], ALU.mult, [rt, R_av], [R_samp])
                else:
                    tt(uext[:, c, 30:30 + TB], tm[:, 0:n], aval[:, c, c0:c1], ALU.mult, [rt, R_av], [R_u])
            elif q == 2:
                if samp_tile:
                    cp(sampb[:, c, :, 3], banks[b][:, 0:n], [R_bank[b]], [R_samp])
                else:
                    cp(rext[:, c, 3:3 + TB], banks[b][:, 0:n], [R_bank[b]], [R_r])
            else:
                cp(bg[:, c, c0:c1], banks[b][:, 0:n], [R_bank[b]], [R_bg])

        linear_fm(j, even_w_in[0], 8, xn_t, R_xn, evac)
        acc = aval
        ubf = v3(slot_bf(5), 4, W)
        cp(ubf, uext, [R_u], [R_5])
        dgs = slot_bf(6)
        di = 0
        for c in range(4):
            b = bank()
            for w in range(31):
                dg = dgs[:, (di % 4) * 128:(di % 4 + 1) * 128]
                rd = R_dg[di % 4]
                di += 1
                tsc(dg, ident_bf[:, :], cw_a[:, c, w:w + 1], None, ALU.mult, None, [R_c], [rd, R_6])
                mm(banks[b][:, :], dg, ubf[:, c, w:w + TB], w == 0, w == 30, [rd, R_5], [R_bank[b]], inc=True)
            tsc(acc[:, c, 0:TB], banks[b][:, :], vecs[:, 0, c:c + 1], None, ALU.add, None, [R_bank[b], R_c], [R_av])
        if j == 0:
            for c in range(4):
                pr = m7[:, c, 0:NS * 31].rearrange("p (s w) -> p s w", s=NS)
                tt(pr, samp[:, c, :, 0:31], cw_a[:, c:c + 1, 0:31].to_broadcast([128, NS, 31]), ALU.mult,
                   [R_samp, R_c], [R_7])
                P.emit("dve", lambda h, c=c, pr=pr: h.reduce_sum(acc[:, c, TB:NCOL], pr, AX.X), reads=[R_7], writes=[R_av])
                tsc(acc[:, c, TB:NCOL], acc[:, c, TB:NCOL], vecs[:, 0, c:c + 1], None, ALU.add, None, [R_av, R_c], [R_av])
        cast_act_only[0] = True
        linear_fm(j, even_w_in[0], 8, xn_t, R_xn, evac, oc0=8)
        cast_act_only[0] = False
        sqf = m7
        act(sqf[:, :, 0:nc_], acc[:, :, 0:nc_], AF.Square, [R_av], [R_7])
        for (c0, c1) in coltiles(j):
            n = c1 - c0
            b1 = bank(); b2 = bank()
            for c in range(4):
                mm(banks[b1][:, 0:n], ones_f[:, :], acc[:, c, c0:c1], c == 0, c == 3, [R_av, R_c], [R_bank[b1]])
            for c in range(4):
                mm(banks[b2][:, 0:n], ones_f[:, :], sqf[:, c, c0:c1], c == 0, c == 3, [R_7, R_c], [R_bank[b2]])
            mu = m5[:, 0, c0:c1]; var = m5[:, 1, c0:c1]; t1 = m5[:, 2, c0:c1]
            act(mu, banks[b1][:, 0:n], AF.Copy, [R_bank[b1]], [R_5], scale=1.0 / 512)
            tt(t1, mu, mu, ALU.mult, [R_5], [R_5])
            stt(var, banks[b2][:, 0:n], 1.0 / 512, t1, ALU.mult, ALU.subtract, [R_bank[b2], R_5], [R_5])
            act(var, var, AF.Ln, [R_5], [R_5], bias=EPS)
            act(var, var, AF.Exp, [R_5], [R_5], scale=-0.5)
            for c in range(4):
                tt(m6[:, c, c0:c1], acc[:, c, c0:c1], mu, ALU.subtract, [R_av, R_5], [R_6])
                tt(m6[:, c, c0:c1], m6[:, c, c0:c1], var, ALU.mult, [R_6, R_5], [R_6])
                P.emit("act", lambda h, c=c, c0=c0, c1=c1: h.activation(
                    mixin[:, c, c0:c1], m6[:, c, c0:c1], AF.Silu, bias=vecs[:, 2, c:c + 1], scale=vecs[:, 1, c:c + 1]),
                    reads=[R_6, R_c], writes=[R_mi])
        for c in range(4):
            tsc(xr[:, c, 0:TB], rext[:, c, 0:TB], cw_b[:, c, 0:1], vecs[:, 3, c:c + 1], ALU.mult, ALU.add, [R_r, R_c], [R_xr])
            for w in range(1, 4):
                stt(xr[:, c, 0:TB], rext[:, c, w:w + TB], cw_b[:, c, w:w + 1], xr[:, c, 0:TB], ALU.mult, ALU.add,
                    [R_r, R_c, R_xr], [R_xr])
        if j == 0:
            for c in range(4):
                pr = m7[:, c, 0:NS * 4].rearrange("p (s w) -> p s w", s=NS)
                tt(pr, sampb[:, c, :, 0:4], cw_b[:, c:c + 1, 0:4].to_broadcast([128, NS, 4]), ALU.mult, [R_samp, R_c], [R_7])
                P.emit("dve", lambda h, c=c, pr=pr: h.reduce_sum(xr[:, c, TB:NCOL], pr, AX.X), reads=[R_7], writes=[R_xr])
                tsc(xr[:, c, TB:NCOL], xr[:, c, TB:NCOL], vecs[:, 3, c:c + 1], None, ALU.add, None, [R_xr, R_c], [R_xr])
        xrb = v3(slot_bf(7), 4, NCOL)
        cp(xrb[:, :, 0:nc_], xr[:, :, 0:nc_], [R_xr], [R_7])
        for c in range(4):
            for (c0, c1) in coltiles(j):
                n = c1 - c0
                b1 = bank(); b2 = bank()
                mm(banks[b1][:, 0:n], wabd[:, c, :], xrb[:, c, c0:c1], True, True, [R_7, R_c], [R_bank[b1]])
                mm(banks[b2][:, 0:n], wxbd[:, c, :], xrb[:, c, c0:c1], True, True, [R_7, R_c], [R_bank[b2]])
                P.emit("act", lambda h, c=c, c0=c0, c1=c1, b1=b1, n=n: h.activation(
                    m5[:, c, c0:c1], banks[b1][:, 0:n], AF.Sigmoid, bias=vecs[:, 4, c:c + 1]),
                    reads=[R_bank[b1], R_c], writes=[R_5])
                P.emit("act", lambda h, c=c, c0=c0, c1=c1, b2=b2, n=n: h.activation(
                    m6[:, c, c0:c1], banks[b2][:, 0:n], AF.Sigmoid, bias=vecs[:, 5, c:c + 1]),
                    reads=[R_bank[b2], R_c], writes=[R_6])
        for c in range(4):
            P.emit("act", lambda h, c=c: h.activation(m5[:, c, 0:nc_], m5[:, c, 0:nc_], AF.Exp, scale=vecs[:, 7, c:c + 1]),
                   reads=[R_5, R_c], writes=[R_5])
        bt = v3(slot(7), 4, NCOL)
        tt(bt[:, :, 0:nc_], m5[:, :, 0:nc_], m5[:, :, 0:nc_], ALU.mult, [R_5], [R_7])
        tsc(bt[:, :, 0:nc_], bt[:, :, 0:nc_], -1.0, 1.0, ALU.mult, ALU.add, [R_7], [R_7])
        tsc(bt[:, :, 0:nc_], bt[:, :, 0:nc_], 1e-30, None, ALU.max, None, [R_7], [R_7])
        act(bt[:, :, 0:nc_], bt[:, :, 0:nc_], AF.Ln, [R_7], [R_7])
        act(bt[:, :, 0:nc_], bt[:, :, 0:nc_], AF.Exp, [R_7], [R_7], scale=0.5)
        tt(m6[:, :, 0:nc_], m6[:, :, 0:nc_], bt[:, :, 0:nc_], ALU.mult, [R_6, R_7], [R_6])
        tt(m6[:, :, 0:nc_], m6[:, :, 0:nc_], xr[:, :, 0:nc_], ALU.mult, [R_6, R_xr], [R_6])
        hl = bt
        for c in range(4):
            P.emit("dve", lambda h, c=c: h.tensor_tensor_scan(hl[:, c, 0:TB], m5[:, c, 0:TB], m6[:, c, 0:TB],
                                                            hcarry[:, c:c + 1], ALU.mult, ALU.add),
                   reads=[R_5, R_6, R_carry], writes=[R_7])
        if j == 0:
            for c in range(4):
                tt(hl[:, c, TB:NCOL], m5[:, c, TB:NCOL], samph[:, c, :], ALU.mult, [R_5, R_samp], [R_7])
                tt(hl[:, c, TB:NCOL], hl[:, c, TB:NCOL], m6[:, c, TB:NCOL], ALU.add, [R_7, R_6], [R_7])
        cp(hcarry[:, :], hl[:, :, TB - 1], [R_7], [R_carry], e="pool")
        cp(ucarry[:, :, :], uext[:, :, TB:TB + 30], [R_u], [R_carry], e="pool")
        cp(rcarry[:, :, :], rext[:, :, TB:TB + 3], [R_r], [R_carry], e="pool")
        g5 = m5
        tt(g5[:, :, 0:nc_], bg[:, :, 0:nc_], bg[:, :, 0:nc_], ALU.mult, [R_bg], [R_5])
        tsc(g5[:, :, 0:nc_], g5[:, :, 0:nc_], 0.044715, 1.0, ALU.mult, ALU.add, [R_5], [R_5])
        tt(g5[:, :, 0:nc_], g5[:, :, 0:nc_], bg[:, :, 0:nc_], ALU.mult, [R_5, R_bg], [R_5])
        act(g5[:, :, 0:nc_], g5[:, :, 0:nc_], AF.Sigmoid, [R_5], [R_5], scale=1.5957691216057308)
        tt(g5[:, :, 0:nc_], g5[:, :, 0:nc_], bg[:, :, 0:nc_], ALU.mult, [R_5, R_bg], [R_5])
        tt(mixin[:, 4:8, 0:nc_], hl[:, :, 0:nc_], g5[:, :, 0:nc_], ALU.mult, [R_7, R_5], [R_mi])
        if j == NBLK - 1:
            store_T(lambda c: uext[:, c, TB:TB + 30], 30, 4, o_p_conv_a, [R_u], slot(3), R_sl[3])
            store_T(lambda c: rext[:, c, TB:TB + 3], 3, 4, o_p_conv_b, [R_r], slot(4), R_sl[4])
            store_T(lambda c: hl[:, c, TB - 1:TB], 1, 4, o_p_lru, [R_7], slot(6), R_sl[6])
        if j == 0:
            store_T(lambda c: samp[:, c, :, 30], NS, 4, o_s_conv_a[:, 29, :], [R_samp], slot(3), R_sl[3])
            store_T(lambda c: sampb[:, c, :, 3], NS, 4, o_s_conv_b[:, 2, :], [R_samp], slot(4), R_sl[4])
            store_T(lambda c: hl[:, c, TB:NCOL], NS, 4, o_s_lru, [R_7], slot(6), R_sl[6])
        linear_fm(j, even_w_out[0], 8, mixin, R_mi, add_to_x)

    def xattn_core(q3, R_q, c0, c1, kT, R_k, vv, R_v, o3, R_o, pslot):
        n = c1 - c0
        pT = slot_bf(pslot)
        for hd in range(4):
            R_p = R_xp[hd]
            pts = []
            bl = bank()
            for mt_ in range(2):
                b = bank()
                for dc in range(2):
                    mm(banks[b][:, 0:n], kT[:, hd * 2 + dc, mt_ * 128:(mt_ + 1) * 128], q3[:, hd * 2 + dc, c0:c1],
                       dc == 0, dc == 1, [R_k, R_q], [R_bank[b]])
                pt_ = pT[:, hd * 1024 + mt_ * 512:hd * 1024 + mt_ * 512 + n]
                act(pt_, banks[b][:, 0:n], AF.Exp, [R_bank[b]], [R_p, R_sl[pslot]], scale=1.0 / 16.0)
                pts.append(pt_)
            for mt_ in range(2):
                mm(banks[bl][:, 0:n], ones_bf[:, :], pts[mt_], mt_ == 0, mt_ == 1, [R_p, R_c], [R_bank[bl]])
            tm, rt = tmp()
            act(tm[:, 0:n], banks[bl][:, 0:n], AF.Ln, [R_bank[bl]], [rt])
            act(tm[:, 0:n], tm[:, 0:n], AF.Exp, [rt], [rt], scale=-1.0)
            for ec in range(2):
                b = bank()
                for mt_ in range(2):
                    mm(banks[b][:, 0:n], vv[:, mt_, hd * 256 + ec * 128:hd * 256 + (ec + 1) * 128], pts[mt_],
                       mt_ == 0, mt_ == 1, [R_v, R_p], [R_bank[b]])
                tt(o3[:, hd * 2 + ec, c0:c1], banks[b][:, 0:n], tm[:, 0:n], ALU.mult, [R_bank[b], rt], [R_o])

    def xattn(j, l):
        rmsnorm(j, 6 + l)
        q3 = v3(slot_bf(0), 8, NCOL); R_q = R_sl[0]
        o3 = v3(slot_bf(1), 8, NCOL); R_o = R_sl[1]

        def evq(oc, c0, c1, b):
            cp(q3[:, oc, c0:c1], banks[b][:, 0:c1 - c0], [R_bank[b]], [R_q])

        linear_fm(j, xattn_w_q[l], 8, xn_t, R_xn, evq)
        xattn_core(q3, R_q, 0, TB, memKT[:, l], R_memKT, memV[:, l], R_memV, o3, R_o, 2)
        if j == 0:
            for s in range(NS):
                kTs = v3(slot_bf(3), 8, 256)
                vvs = v3(slot_bf(4), 2, D)
                kst = v3(slot(5, 2), 2, D)
                vst = v3(slot(7), 2, D)
                dma(kst, cmk[l, s].rearrange("(r p) f -> p r f", p=128), [], [R_sl[5], R_sl[6]], R_sl[5])
                dma(vst, cmv[l, s].rearrange("(r p) f -> p r f", p=128), [], [R_sl[7]], R_sl[7])
                for c in range(8):
                    b = bank()
                    for r in range(2):
                        tr(banks[b][:, r * 128:(r + 1) * 128], kst[:, r, c * 128:(c + 1) * 128], [R_sl[5], R_sl[6]], [R_bank[b]])
                    cp(kTs[:, c, :], banks[b][:, 0:256], [R_bank[b]], [R_sl[3]], e=("act" if c % 2 else "dve"))
                cp(vvs, vst, [R_sl[7]], [R_sl[4]], e="pool")
                xattn_core(q3, R_q, TB + s, TB + s + 1, kTs, R_sl[3], vvs, R_sl[4], o3, R_o, 2)
        linear_fm(j, xattn_w_out[l], 8, o3, R_o, add_to_x)

    R_pT = [Res("pT%d" % i) for i in range(4)]
    R_xp = [Res("xp%d" % i) for i in range(4)]

    def attention(j):
        rmsnorm(j, 5)
        nc_ = ncols(j)
        qT = v3(slot_bf(0), 8, NCOL); R_q = R_sl[0]
        att = v3(slot_bf(1), 8, NCOL); R_att = R_sl[1]
        kTn = v3(slot_bf(2), 8, TB); R_kn = R_sl[2]
        wi = attn_w_in[0]
        nkt = 4 * (j + 1)

        def evq(oc, c0, c1, b):
            cp(qT[:, oc, c0:c1], banks[b][:, 0:c1 - c0], [R_bank[b]], [R_q])

        linear_fm(j, wi[:, 0:D], 8, xn_t, R_xn, evq)
        ktok = ts_t; vtok = v3(slot(5, 2), 4, D); vbf = v3(slot_bf(7), 4, D)
        def stok(col, n=512):
            if col < 2 * D:
                return slot(3)[0:NS, col:col + n], R_sl[3]
            return slot(4)[0:NS, col - 2 * D:col - 2 * D + n], R_sl[4]
        for part in range(2):
            for t in range(2):
                col = D + part * D + t * 512
                wt, rw = wload(*std_tile(wi, col))
                w3 = wt[:, 0:4096].rearrange("p (k n) -> p k n", k=8)
                if part == 0:
                    for oo in range(4):
                        b = bank()
                        for kc in range(8):
                            mm(banks[b][:, :], w3[:, kc, oo * 128:(oo + 1) * 128], xn_t[:, kc, 0:TB], kc == 0, kc == 7,
                               [rw, R_xn], [R_bank[b]])
                        cp(kTn[:, t * 4 + oo, :], banks[b][:, :], [R_bank[b]], [R_kn])
                for s in range(4):
                    b = bank()
                    for kc in range(8):
                        mm(banks[b][:, :], xn_t[:, kc, s * 128:(s + 1) * 128], w3[:, kc, :], kc == 0, kc == 7,
                           [rw, R_xn], [R_bank[b]])
                    if part == 0:
                        cp(ktok[:, s, t * 512:(t + 1) * 512], banks[b][:, :], [R_bank[b]], [R_ts])
                    else:
                        cp(vtok[:, s, t * 512:(t + 1) * 512], banks[b][:, :], [R_bank[b]], [R_sl[5], R_sl[6]])
                        cp(vbf[:, s, t * 512:(t + 1) * 512], banks[b][:, :], [R_bank[b]], [R_sl[7]], e="dve")
                if j == 0:
                    b = bank()
                    for kc in range(8):
                        mm(banks[b][0:NS, :], xn_t[:, kc, TB:NCOL], w3[:, kc, :], kc == 0, kc == 7, [rw, R_xn], [R_bank[b]])
                    sap, sres = stok(col)
                    cp(sap, banks[b][0:NS, :], [R_bank[b]], [sres])
        if j == 0:
            for t in range(2):
                wt, rw = wload(*std_tile(wi, t * 512))
                w3 = wt[:, 0:4096].rearrange("p (k n) -> p k n", k=8)
                b = bank()
                for kc in range(8):
                    mm(banks[b][0:NS, :], xn_t[:, kc, TB:NCOL], w3[:, kc, :], kc == 0, kc == 7, [rw, R_xn], [R_bank[b]])
                sap, sres = stok(t * 512)
                cp(sap, banks[b][0:NS, :], [R_bank[b]], [sres])
            store(o_s_k, stok(D, D)[0], R_sl[3])
            store(o_s_v, stok(2 * D, D)[0], R_sl[4])
        rows = slice(j * TB, (j + 1) * TB)
        store(o_p_k[rows, :].rearrange("(s p) f -> p s f", p=128), ktok[:, :, :], R_ts)
        store(o_p_v[rows, :].rearrange("(s p) f -> p s f", p=128), vtok, [R_sl[5], R_sl[6]])
        dma(vscr[rows, :].rearrange("(s p) f -> p s f", p=128), vbf, [R_sl[7]], [R_vscr], R_sl[7], e="pool")
        dma(ktscr[:, :, rows].rearrange("h p t -> p h t"), kTn, [R_kn], [R_ktscr], R_kn, e="pool")
        nk = nkt * 128

        def kv_bufs(hd):
            if j == 0:
                return (slot_bf(8)[:, 2048:2560], R_sl[8], v3(slot_bf(8), 4, 128), R_sl[8])
            if hd % 2 == 0:
                return (slot_bf(3)[:, 0:SEQ], R_sl[3], v3(slot_bf(8), 32, 128), R_sl[8])
            return (slot_bf(2)[:, 0:SEQ], R_sl[2], v3(slot_bf(4), 32, 128), R_sl[4])

        def kv_load(hd):
            kh, R_kh, vh, R_vh = kv_bufs(hd)
            dma(kh[:, 0:nk], ktscr[hd, :, 0:nk], [R_ktscr], [R_kh], R_kh)
            dma(vh[:, 0:nkt, :], vscr[0:nk, hd * 128:(hd + 1) * 128].rearrange("(t p) e -> p t e", p=128),
                [R_vscr], [R_vh], R_vh)

        kv_load(0)
        for hd in range(8):
            kh, R_kh, vh, R_vh = kv_bufs(hd)
            if j == 0 and hd > 0:
                kv_load(hd)
            if hd + 1 < 8 and j > 0:
                kv_load(hd + 1)
            A = [None, None]
            steps = [(m, kt) for m in range(2) for kt in range(nkt)]

            def s_issue(i):
                m, kt = steps[i]
                bsn = i % 4
                mm(banks[bsn][:, :], kh[m * 64:(m + 1) * 64, kt * 128:(kt + 1) * 128], qT[m * 64:(m + 1) * 64, hd, 0:TB],
                   True, True, [R_kh, R_q], [R_bank[bsn]])

            LOOK = 2
            for i in range(min(LOOK, len(steps))):
                s_issue(i)
            for i, (m, kt) in enumerate(steps):
                if i + LOOK < len(steps):
                    s_issue(i + LOOK)
                bo = 4 + 2 * m
                bl = 5 + 2 * m
                bsn = i % 4
                pT = slot_bf(6)[:, (i % 4) * 512:(i % 4 + 1) * 512]
                R_p = R_pT[i % 4]
                act(pT, banks[bsn][:, :], AF.Exp, [R_bank[bsn]], [R_p, R_sl[6]], scale=0.125)
                dk = kt - 4 * j
                if dk >= 0:
                    tt(pT, pT, mask_bf[:, dk, :], ALU.mult, [R_p, R_c], [R_p], e="pool")
                mm(banks[bo][:, :], vh[:, kt, :], pT, kt == 0, kt == nkt - 1, [R_vh, R_p], [R_bank[bo]])
                mm(banks[bl][:, :], ones_bf[:, :], pT, kt == 0, kt == nkt - 1, [R_p, R_c], [R_bank[bl]])
                if kt == nkt - 1:
                    tm, rt = tmp()
                    act(tm[:, :], banks[bl][:, :], AF.Ln, [R_bank[bl]], [rt])
                    act(tm[:, :], tm[:, :], AF.Exp, [rt], [rt], scale=-1.0)
                    am = slot(5)[:, m * 512:(m + 1) * 512]
                    tt(am, banks[bo][:, :], tm[:, :], ALU.mult, [R_bank[bo], rt], [R_sl[5]])
                    A[m] = am
            dd = slot(5)[:, 1024:1536]
            stt(dd, A[1], lamc[:, 1:2], A[0], ALU.mult, ALU.add, [R_sl[5], R_lam], [R_sl[5]])
            sqd = slot_bf(5)[:, 3072:3584]
            act(sqd, dd, AF.Square, [R_sl[5]], [R_sl[5]])
            b = bank_ptr[0] % 4
            bank_ptr[0] += 1
            mm(banks[b][:, :], ones_bf[:, :], sqd, True, True, [R_sl[5], R_c], [R_bank[b]])
            tm, rt = tmp()
            act(tm[:, :], banks[b][:, :], AF.Ln, [R_bank[b]], [rt], bias=EPS, scale=1.0 / 128)
            act(tm[:, :], tm[:, :], AF.Exp, [rt], [rt], scale=-0.5)
            stt(att[:, hd, 0:TB], dd, subg[:, 0:1], tm[:, :], ALU.mult, ALU.mult, [R_sl[5], R_lam, rt], [R_att])
        if j == 0 and "satt" not in os.environ.get("KSKIP", ""):
            sample_attention(stok, att, R_att)
        linear_fm(j, attn_w_out[0], 8, att, R_att, add_to_x)

    R_pg = [Res("pg%d" % i) for i in range(4)]

    def sample_attention(stok, att, R_att):
        ck_rows = ck
        cv_rows = cv
        S_all = v3(slot(4)[:, 1100:2192], NPAGE + 1, 16)
        R_S = R_sl[4]
        attok = slot(2)
        R_at = R_sl[2]
        bat = [6, 7]
        for s in range(NS):
            qkv = slot(5, 2)
            R_qkv = [R_sl[5], R_sl[6]]
            for t in range(6):
                b = bank_ptr[0] % 4
                bank_ptr[0] += 1
                sap, sres = stok(t * 512)
                mm(banks[b][:, :], sel_t[0:NS, s * 128:(s + 1) * 128], sap, True, True,
                   [sres, R_c], [R_bank[b]])
                cp(qkv[:, t * 512:(t + 1) * 512], banks[b][:, :], [R_bank[b]], R_qkv, e=("act" if t % 2 else "dve"))
            qbc = qkv[:, 0:D]
            pgk = [slot(7)[:, 0:D], slot(8)[:, 0:D], slot(7)[:, D:2 * D], slot(8)[:, D:2 * D]]
            R_pgk = R_pg
            prod = slot(0)[:, 0:D]
            for pg in range(NPAGE + 1):
                if pg < NPAGE:
                    kb = pgk[pg % 4]
                    rk = R_pgk[pg % 4]
                    col = s * NPAGE + pg
                    P.emit("pool", lambda h, kb=kb, col=col: h.indirect_dma_start(
                        out=kb, out_offset=None, in_=ck_rows,
                        in_offset=bass.IndirectOffsetOnAxis(ap=idx_t[:, col:col + 1].bitcast(U32), axis=0)),
                        reads=[R_idx], writes=[rk], dma=rk, guards=[R_sl[7], R_sl[8]])
                    src, rsrc = kb, [rk]
                else:
                    src, rsrc = qkv[:, D:2 * D], R_qkv
                tt(prod, src, qbc, ALU.mult, rsrc + R_qkv, [R_sl[0]])
                P.emit("dve", lambda h, pg=pg: h.reduce_sum(S_all[:, pg, :], prod.rearrange("p (a d) -> p a d", d=64), AX.X),
                       reads=[R_sl[0]], writes=[R_S])
            mx = small[:, 32:48]
            P.emit("dve", lambda h: h.reduce_max(mx, S_all[:, :, :].rearrange("p g a -> p a g"), AX.X),
                   reads=[R_S], writes=[R_small])
            b = bank_ptr[0] % 4
            bank_ptr[0] += 1
            tr(banks[b][0:16, 0:128], mx, [R_small], [R_bank[b]])
            m16 = small[0:16, 48:49]
            P.emit("dve", lambda h, b=b: h.reduce_max(m16, banks[b][0:16, 0:128], AX.X), reads=[R_bank[b]], writes=[R_small])
            dg = slot(0)[0:16, 0:16]
            tsc(dg, ident[0:16, 0:16], m16, None, ALU.mult, None, [R_small, R_c], [R_sl[0]])
            b = bank_ptr[0] % 4
            bank_ptr[0] += 1
            mm(banks[b][:, 0:16], ones_f[0:16, :], dg, True, True, [R_sl[0], R_c], [R_bank[b]])
            mbc = small[:, 16:32]
            cp(mbc, banks[b][:, 0:16], [R_bank[b]], [R_small])
            tt(S_all[:, :, :], S_all[:, :, :], mbc.unsqueeze(1).to_broadcast([128, NPAGE + 1, 16]), ALU.subtract,
               [R_S, R_small], [R_S])
            act(S_all[:, :, :], S_all[:, :, :], AF.Exp, [R_S], [R_S], scale=0.125)
            tsc(S_all[:, NPAGE, :], S_all[:, NPAGE, :], e0_t[:, 0:1], None, ALU.mult, None, [R_S, R_c], [R_S])
            ps = small[:, 0:16]
            P.emit("dve", lambda h: h.reduce_sum(ps, S_all[:, :, :].rearrange("p g a -> p a g"), AX.X),
                   reads=[R_S], writes=[R_small])
            lq = small[0:16, 50:51]
            b = bank_ptr[0] % 4
            bank_ptr[0] += 1
            mm(banks[b][0:16, 0:1], ps, ones_f[:, 0:1], True, True, [R_small, R_c], [R_bank[b]])
            cp(lq, banks[b][0:16, 0:1], [R_bank[b]], [R_small])
            P.emit("dve", lambda h: h.reciprocal(lq, lq), reads=[R_small], writes=[R_small])
            Sv = S_all[:, :, :].rearrange("p g (h m) -> p g m h", m=2)
            for pg in range(NPAGE + 1):
                if pg < NPAGE:
                    vb = pgk[(pg + 1) % 4]
                    rv = R_pgk[(pg + 1) % 4]
                    col = s * NPAGE + pg
                    P.emit("pool", lambda h, vb=vb, col=col: h.indirect_dma_start(
                        out=vb, out_offset=None, in_=cv_rows,
                        in_offset=bass.IndirectOffsetOnAxis(ap=idx_t[:, col:col + 1].bitcast(U32), axis=0)),
                        reads=[R_idx], writes=[rv], dma=rv, guards=[R_sl[7], R_sl[8]])
                    src, rsrc = vb, [rv]
                else:
                    src, rsrc = qkv[:, 2 * D:3 * D], R_qkv
                for hf in range(2):
                    bb = 4 + hf
                    mm(banks[bb][0:16, :], S_all[:, pg, :], src[:, hf * 512:(hf + 1) * 512], pg == 0, pg == NPAGE,
                       [R_S] + rsrc, [R_bank[bb]], inc=(pg == NPAGE or hf == 1))
            dmat = slot(0)[0:16, 0:D]
            for hf in range(2):
                cs = slice(hf * 512, (hf + 1) * 512)
                stt(dmat[:, cs], banks[4 + hf][0:16, :], lq[:, 0:1], bmask_t[:, cs], ALU.mult, ALU.mult,
                    [R_bank[4 + hf], R_small, R_c], [R_sl[0]])
            for hf in range(2):
                b = bank_ptr[0] % 4
                bank_ptr[0] += 1
                mm(banks[b][0:NS, :], sel2_t[0:16, s * NS:(s + 1) * NS], dmat[:, hf * 512:(hf + 1) * 512], True, True,
                   [R_sl[0], R_c], [R_bank[b]])
                if s == 0:
                    cp(attok[0:NS, hf * 512:(hf + 1) * 512], banks[b][0:NS, :], [R_bank[b]], [R_at])
                else:
                    tt(attok[0:NS, hf * 512:(hf + 1) * 512], banks[b][0:NS, :], attok[0:NS, hf * 512:(hf + 1) * 512], ALU.add,
                       [R_bank[b], R_at], [R_at])
        P.emit("pool", lambda h: h.memset(slot(7)[:, 0:1], 0.0), reads=[], writes=[R_sl[7]] + R_pg)
        P.emit("pool", lambda h: h.memset(slot(8)[:, 0:1], 0.0), reads=[], writes=[R_sl[8]] + R_pg)
        a3 = attok[0:NS, 0:D].rearrange("p (h e) -> p h e", h=8)
        sq3 = slot(0)[0:NS, 0:D].rearrange("p (h e) -> p h e", h=8)
        tt(sq3, a3, a3, ALU.mult, [R_at], [R_sl[0]])
        ssum = small[0:NS, 52:60]
        P.emit("dve", lambda h: h.reduce_sum(ssum, sq3, AX.X), reads=[R_sl[0]], writes=[R_small])
        act(ssum, ssum, AF.Sqrt, [R_small], [R_small], bias=EPS, scale=1.0 / 128)
        P.emit("dve", lambda h: h.reciprocal(ssum, ssum), reads=[R_small], writes=[R_small])
        tt(a3, a3, ssum.unsqueeze(2).to_broadcast([NS, 8, 128]), ALU.mult, [R_at, R_small], [R_at])
        for hd in range(8):
            b = bank_ptr[0] % 4
            bank_ptr[0] += 1
            tr(banks[b][:, 0:NS], attok[0:NS, hd * 128:(hd + 1) * 128], [R_at], [R_bank[b]], n=NS)
            tsc(att[:, hd, TB:NCOL], banks[b][:, 0:NS], subg[:, 0:1], None, ALU.mult, None, [R_bank[b], R_lam], [R_att])


    kstop = int(os.environ.get("KSTOP", "1000000"))
    stages = [setup, mem_setup]
    for j in range(NBLK):
        stages += [lambda j=j: load_block(j), lambda j=j: ffn(j, 0, 0), lambda j=j: even_mixer(j), lambda j=j: xattn(j, 0),
                   lambda j=j: ffn(j, 0, 1), lambda j=j: ffn(j, 1, 0), lambda j=j: attention(j), lambda j=j: xattn(j, 1),
                   lambda j=j: ffn(j, 1, 1), lambda j=j: final_block(j)]
    for dry in (True, False):
        P.dry = dry
        bank_ptr[0] = 0
        tmp_ptr[0] = 0
        wcnt[0] = 0
        for i, st_ in enumerate(stages):
            if i >= kstop:
                break
            st_()
    P.emit("sp", lambda h: h.nop(), reads=[R_out], writes=[])

    with nc.Block() as block:
        @block.sync
        def _(e):
            P.replay("sp", e)

        @block.tensor
        def _(e):
            P.replay("pe", e)

        @block.scalar
        def _(e):
            P.replay("act", e)

        @block.vector
        def _(e):
            P.replay("dve", e)

        @block.gpsimd
        def _(e):
            P.replay("pool", e)
    return nc, es


_CACHE = {}


def _consts():
    ident = np.eye(128, dtype=np.float32)
    mask = np.zeros((128, 4, 512), np.float32)
    p = np.arange(128)[:, None]
    f = np.arange(512)[None, :]
    for dk in range(4):
        mask[:, dk, :] = (dk * 128 + p <= f).astype(np.float32)
    sel = np.zeros((NS, NS, 128), np.float32)
    for s in range(NS):
        sel[s, s, :] = 1.0
    e0 = np.zeros((128, 1), np.float32)
    e0[0, 0] = 1.0
    bmask = np.zeros((16, 8, 128), np.float32)
    for a in range(16):
        bmask[a, a // 2, :] = 1.0
    sel2 = np.zeros((16, NS, NS), np.float32)
    for s in range(NS):
        sel2[:, s, s] = 1.0
    par = np.zeros((16, 2), np.float32)
    par[0::2, 0] = 1.0
    par[1::2, 1] = 1.0
    iota = np.arange(128, dtype=np.float32).reshape(128, 1)
    return dict(c_ident=ident, c_mask=mask.reshape(128, 2048), c_sel=sel.reshape(NS, NS * 128), c_e0=e0,
                c_bmask=bmask.reshape(16, D), c_sel2=sel2.reshape(16, NS * NS), c_iota=iota, c_par=par)


def kernel(**inp):
    if "nc" not in _CACHE:
        _CACHE["nc"] = build_program()
    nc, _es = _CACHE["nc"]
    f = lambda a: np.ascontiguousarray(np.asarray(a))
    consts = _consts()
    wnames = ["ffn1_g", "ffn1_w_in", "ffn1_w_out", "ffn2_g", "ffn2_w_in", "ffn2_w_out", "mix_g", "even_w_in", "conv_a_w",
              "conv_a_b", "conv_a_ln_g", "conv_a_ln_b", "conv_b_w", "conv_b_b", "lru_w_a", "lru_b_a", "lru_w_x", "lru_b_x",
              "lru_lambda", "even_w_out", "attn_w_in", "lam_q1", "lam_k1", "lam_q2", "lam_k2", "attn_subln_g", "attn_w_out",
              "xattn_g", "xattn_w_q", "xattn_w_kv", "xattn_w_out"]
    shared = {n: f(inp[n]) for n in wnames}
    shared["final_g"] = f(inp["final_g"]).reshape(1, D)
    shared["ck"] = f(inp["cache_k"]).reshape(2560 * 128, D)
    shared["cv"] = f(inp["cache_v"]).reshape(2560 * 128, D)
    shared.update(consts)
    in_maps = []
    for c in range(8):
        b = c % 4
        m = dict(shared)
        m["xp"] = f(inp["x_prompt"][b])
        m["xs"] = f(inp["x_sample"][4 * c:4 * c + 4, 0])
        m["sca"] = f(inp["state_conv_a"][0, 4 * c:4 * c + 4])
        m["scb"] = f(inp["state_conv_b"][0, 4 * c:4 * c + 4])
        m["slru"] = f(inp["state_lru"][0, 4 * c:4 * c + 4])
        m["cmk"] = f(inp["cache_mem_k"][:, 4 * c:4 * c + 4]).reshape(2, NS, 256, D)
        m["cmv"] = f(inp["cache_mem_v"][:, 4 * c:4 * c + 4]).reshape(2, NS, 256, D)
        m["pt"] = f(inp["page_table"][4 * c:4 * c + 4]).astype(np.int32)
        m["mp"] = f(inp["mem_prompt"][b])
        in_maps.append(m)
    res = run_bass_kernel_spmd(nc, in_maps, core_ids=list(range(8))).results
    R = lambda name, cs: np.stack([res[c][name] for c in cs])
    p4 = range(4)
    a8 = range(8)
    y_p = R("o_y_p", p4)
    y_s = np.concatenate([res[c]["o_y_s"] for c in a8])[:, None, :]
    p_conv_a = R("o_p_conv_a", p4)[None]
    p_conv_b = R("o_p_conv_b", p4)[None]
    p_lru = R("o_p_lru", p4).reshape(1, 4, 512)
    p_k = R("o_p_k", p4).reshape(1, 4, SEQ_FULL, 8, 128)
    p_v = R("o_p_v", p4).reshape(1, 4, SEQ_FULL, 8, 128)
    p_mem_k = np.stack([res[c]["o_p_mem_k"] for c in p4], axis=1).reshape(2, 4, 256, 4, 256)
    p_mem_v = np.stack([res[c]["o_p_mem_v"] for c in p4], axis=1).reshape(2, 4, 256, 4, 256)
    s_conv_a = np.concatenate([res[c]["o_s_conv_a"] for c in a8])[None]
    s_conv_b = np.concatenate([res[c]["o_s_conv_b"] for c in a8])[None]
    s_lru = np.concatenate([res[c]["o_s_lru"] for c in a8])[None]
    s_k = np.concatenate([res[c]["o_s_k"] for c in a8]).reshape(1, 32, 1, 8, 128)
    s_v = np.concatenate([res[c]["o_s_v"] for c in a8]).reshape(1, 32, 1, 8, 128)
    outs = (y_p, y_s, p_conv_a, p_conv_b, p_lru, p_k, p_v, p_mem_k, p_mem_v, s_conv_a, s_conv_b, s_lru, s_k, s_v)
    return tuple(np.ascontiguousarray(o, dtype=np.float32) for o in outs)
```

```python
import math
import os
from contextlib import ExitStack

import numpy as np
import concourse.bass as bass
import concourse.mybir as mybir
from concourse.bass_utils import run_bass_kernel_spmd

F32 = mybir.dt.float32
BF16 = mybir.dt.bfloat16
I32 = mybir.dt.int32
U32 = mybir.dt.uint32
AF = mybir.ActivationFunctionType
ALU = mybir.AluOpType
AX = mybir.AxisListType

D = 1024
DFF = 2816
NHC = 22
TB = 512
NBLK = 8
NS = 4
NCOL = TB + NS
EPS = 1e-6
SLOT = 2192
NSLOT = 9
LAM_INIT = 0.8 - 0.6 * math.exp(-0.3 * 1)
NPAGE = 64
SEQ_FULL = NBLK * TB


class Res:
    __slots__ = ("name", "w", "r", "dsem", "dcnt", "excl", "nowaw")

    def __init__(self, name, excl=False, nowaw=False):
        self.name = name
        self.excl = excl
        self.nowaw = nowaw
        self.w = {}
        self.r = {}
        self.dsem = None
        self.dcnt = 0


class Prog:
    def __init__(self, nc, es):
        self.nc = nc
        self.es = es
        self.sems = {}
        self.eng = {}
        for name, h in (("pe", nc.tensor), ("act", nc.scalar), ("dve", nc.vector),
                        ("pool", nc.gpsimd), ("sp", nc.sync)):
            sem = self.new_sem("e_" + name)
            self.eng[name] = dict(h=h, sem=sem, n=0, ops=[], seen={}, pend=False)
        self.all_dma = []
        self.dry = False

    def new_sem(self, name):
        s = self.es.enter_context(self.nc.semaphore(name))
        self.sems[name] = s
        return name

    def res_sem(self, r, sw=False):
        if r.dsem is None:
            r.dsem = {}
            r.dcnt = {}
        if sw not in r.dsem:
            r.dsem[sw] = self.new_sem(("q_" if sw else "d_") + r.name)
            r.dcnt[sw] = 0
        return r.dsem[sw]

    @staticmethod
    def flat(xs):
        out = []
        for x in xs:
            if isinstance(x, (list, tuple)):
                out.extend(Prog.flat(x))
            else:
                out.append(x)
        return out

    def emit(self, e, fn, reads=(), writes=(), inc=True, dma=None, guards=()):
        if self.dry:
            return None
        reads = self.flat(reads)
        writes = self.flat(writes)
        E = self.eng[e]
        deps = {}
        for g in guards:
            for s, v in list(g.w.items()) + list(g.r.items()):
                deps[s] = max(deps.get(s, 0), v)
        sw = (e == "pool")
        own_dma = self.res_sem(dma, sw) if dma is not None else None
        for r in reads:
            for s, v in r.w.items():
                deps[s] = max(deps.get(s, 0), v)
            if r.excl:
                for s, v in r.r.items():
                    if s != E["sem"]:
                        deps[s] = max(deps.get(s, 0), v)
        for w in writes:
            for s, v in w.w.items():
                if s == own_dma or w.nowaw:
                    continue
                deps[s] = max(deps.get(s, 0), v)
            for s, v in w.r.items():
                deps[s] = max(deps.get(s, 0), v)
        for s, v in deps.items():
            if e == "pe" and s == E["sem"]:
                continue
            if E["seen"].get(s, 0) >= v:
                continue
            E["ops"].append(("w", s, v))
            E["seen"][s] = v
        if dma is not None:
            dma.dcnt[sw] += 16
            tick = (own_dma, dma.dcnt[sw])
            E["ops"].append(("d", fn, own_dma))
        else:
            tick = (E["sem"], E["n"] + 1)
            E["ops"].append(("o", fn, inc))
            if inc:
                E["n"] += 1
                E["pend"] = False
            else:
                E["pend"] = True
        for r in reads:
            r.r[tick[0]] = max(r.r.get(tick[0], 0), tick[1])
        for w in writes:
            w.w[tick[0]] = max(w.w.get(tick[0], 0), tick[1])
        return tick

    def replay(self, e, h):
        E = self.eng[e]
        assert not E["pend"], e
        for op in E["ops"]:
            if op[0] == "w":
                h.wait_ge(self.sems[op[1]], op[2])
            elif op[0] == "d":
                op[1](h).then_inc(self.sems[op[2]], 16)
            else:
                ins = op[1](h)
                if op[2]:
                    ins.then_inc(self.sems[E["sem"]], 1)


def build_program(nblk=NBLK, npage=NPAGE, npool=2560):
    SEQ = nblk * TB
    NBLK = nblk
    NPAGE = npage
    nc = bass.Bass("TRN2", target_bir_lowering=False)
    es = ExitStack()
    P = Prog(nc, es)

    def din(name, shape, dt=F32):
        return nc.dram_tensor(name, list(shape), dt, kind="ExternalInput").ap()

    def dout(name, shape):
        return nc.dram_tensor(name, list(shape), F32, kind="ExternalOutput").ap()

    xp = din("xp", (SEQ, D))
    xs = din("xs", (NS, D))
    sca = din("sca", (NS, 30, 512))
    scb = din("scb", (NS, 3, 512))
    slru = din("slru", (NS, 512))
    ck = din("ck", (npool * 128, D))
    cv = din("cv", (npool * 128, D))
    cmk = din("cmk", (2, NS, 256, D))
    cmv = din("cmv", (2, NS, 256, D))
    pt = din("pt", (NS, NPAGE), I32)
    mp = din("mp", (256, D))
    ffn_g = [din("ffn1_g", (2, D)), din("ffn2_g", (2, D))]
    ffn_wi = [din("ffn1_w_in", (2, D, 2 * DFF)), din("ffn2_w_in", (2, D, 2 * DFF))]
    ffn_wo = [din("ffn1_w_out", (2, DFF, D)), din("ffn2_w_out", (2, DFF, D))]
    mix_g = din("mix_g", (2, D))
    even_w_in = din("even_w_in", (1, D, 2048))
    conv_a_w = din("conv_a_w", (1, 31, 512))
    conv_a_b = din("conv_a_b", (1, 512))
    conv_a_ln_g = din("conv_a_ln_g", (1, 512))
    conv_a_ln_b = din("conv_a_ln_b", (1, 512))
    conv_b_w = din("conv_b_w", (1, 4, 512))
    conv_b_b = din("conv_b_b", (1, 512))
    lru_w_a = din("lru_w_a", (1, 8, 64, 64))
    lru_b_a = din("lru_b_a", (1, 512))
    lru_w_x = din("lru_w_x", (1, 8, 64, 64))
    lru_b_x = din("lru_b_x", (1, 512))
    lru_lambda = din("lru_lambda", (1, 512))
    even_w_out = din("even_w_out", (1, D, D))
    attn_w_in = din("attn_w_in", (1, D, 3072))
    lam_q1 = din("lam_q1", (1, 64))
    lam_k1 = din("lam_k1", (1, 64))
    lam_q2 = din("lam_q2", (1, 64))
    lam_k2 = din("lam_k2", (1, 64))
    attn_subln_g = din("attn_subln_g", (1, 128))
    attn_w_out = din("attn_w_out", (1, D, D))
    xattn_g = din("xattn_g", (2, D))
    xattn_w_q = din("xattn_w_q", (2, D, D))
    xattn_w_kv = din("xattn_w_kv", (2, D, 2 * D))
    xattn_w_out = din("xattn_w_out", (2, D, D))
    final_g = din("final_g", (1, D))
    c_ident = din("c_ident", (128, 128))
    c_mask = din("c_mask", (128, 4 * 512))
    c_sel = din("c_sel", (NS, NS * 128))
    c_e0 = din("c_e0", (128, 1))
    c_bmask = din("c_bmask", (16, D))
    c_sel2 = din("c_sel2", (16, NS * NS))
    c_par = din("c_par", (16, 2))
    c_iota = din("c_iota", (128, 1))

    o_y_p = dout("o_y_p", (SEQ, D))
    o_y_s = dout("o_y_s", (NS, D))
    o_p_conv_a = dout("o_p_conv_a", (30, 512))
    o_p_conv_b = dout("o_p_conv_b", (3, 512))
    o_p_lru = dout("o_p_lru", (1, 512))
    o_p_k = dout("o_p_k", (SEQ, D))
    o_p_v = dout("o_p_v", (SEQ, D))
    o_p_mem_k = dout("o_p_mem_k", (2, 256, D))
    o_p_mem_v = dout("o_p_mem_v", (2, 256, D))
    o_s_conv_a = dout("o_s_conv_a", (NS, 30, 512))
    o_s_conv_b = dout("o_s_conv_b", (NS, 3, 512))
    o_s_lru = dout("o_s_lru", (NS, 512))
    o_s_k = dout("o_s_k", (NS, D))
    o_s_v = dout("o_s_v", (NS, D))

    ktscr = nc.dram_tensor("ktscr", [8, 128, SEQ], BF16).ap()
    vscr = nc.dram_tensor("vscr", [SEQ, D], BF16).ap()
    R_ktscr = Res("ktscr")
    R_vscr = Res("vscr")
    R_out = Res("outs", nowaw=True)

    def sb(name, shape, dt=F32):
        return es.enter_context(nc.sbuf_tensor(name, list(shape), dt))

    x_t = sb("x", [128, 8, NCOL]);          R_x = Res("x")
    xn_t = sb("xn", [128, 8, NCOL], BF16);  R_xn = Res("xn")
    rs_t = sb("rs", [128, NCOL]);           R_rs = Res("rs")
    arena = sb("arena", [128, NSLOT * SLOT])
    R_sl = [Res("sl%d" % i) for i in range(NSLOT)]
    wst = [sb("wst%d" % i, [128, 4096]) for i in range(2)]
    R_wst = [Res("wst%d" % i) for i in range(2)]
    wbf = [sb("wbf%d" % i, [128, 4096], BF16) for i in range(2)]
    R_wbf = [[Res("wbf%da" % i), Res("wbf%db" % i)] for i in range(2)]
    ts_t = sb("ts", [128, 4, D]);           R_ts = Res("ts")
    memKT = sb("memKT", [128, 2, 8, 256], BF16); R_memKT = Res("memKT")
    memV = sb("memV", [128, 2, 2, D], BF16);     R_memV = Res("memV")
    ident = sb("ident", [128, 128]);        R_c = Res("consts")
    ones_bf = sb("ones_bf", [128, 128], BF16)
    ident_bf = sb("ident_bf", [128, 128], BF16)
    R_dg = [Res("dg%d" % i) for i in range(4)]
    ones_f = sb("ones_f", [128, 128])
    mask_bf = sb("mask_bf", [128, 4, 512], BF16)
    gains = sb("gains", [128, 9, 8])
    cw_a = sb("cw_a", [128, 4, 32])
    cw_b = sb("cw_b", [128, 4, 4])
    vecs = sb("vecs", [128, 10, 4])
    wabd = sb("wabd", [128, 4, 128], BF16)
    wxbd = sb("wxbd", [128, 4, 128], BF16)
    ucarry = sb("ucarry", [128, 4, 30]);    R_carry = Res("carry")
    rcarry = sb("rcarry", [128, 4, 3])
    hcarry = sb("hcarry", [128, 4])
    tmp_t = [sb("tmp%d" % i, [128, 512]) for i in range(2)]
    R_tmp = [Res("tmp%d" % i) for i in range(2)]
    small = sb("small", [128, 64]);         R_small = Res("small")
    lamc = sb("lamc", [128, 4]);            R_lam = Res("lam")
    subg = sb("subg", [128, 1])
    sel_t = sb("sel", [NS, NS * 128])
    e0_t = sb("e0", [128, 1])
    bmask_t = sb("bmask", [16, D])
    sel2_t = sb("sel2", [16, NS * NS])
    par_t = sb("par", [16, 4])
    iota_t = sb("iota", [128, 1])
    idxf_t = sb("idxf", [128, NS * NPAGE])
    idx_t = sb("idx", [128, NS * NPAGE], I32); R_idx = Res("idx")
    samp = sb("samp", [128, 4, NS, 32]);    R_samp = Res("samp")
    sampb = sb("sampb", [128, 4, NS, 4])
    samph = sb("samph", [128, 4, NS])

    banks = [es.enter_context(nc.psum_tensor("bank%d" % i, [128, 512], F32)) for i in range(8)]
    R_bank = [Res("bank%d" % i, excl=True) for i in range(8)]
    bank_ptr = [0]

    def bank():
        i = bank_ptr[0] % 8
        bank_ptr[0] += 1
        return i

    def slot(i, n=1):
        return arena[:, i * SLOT:(i + n) * SLOT]

    def slot_bf(i, n=1):
        return slot(i, n).bitcast(BF16)

    def v3(ap, c, t):
        return ap[:, 0:c * t].rearrange("p (c t) -> p c t", c=c)

    tmp_ptr = [0]

    def tmp():
        i = tmp_ptr[0] % 2
        tmp_ptr[0] += 1
        return tmp_t[i], R_tmp[i]

    def mm(out, lhsT, rhs, start, stop, reads, writes, inc=None):
        if inc is None:
            inc = stop
        P.emit("pe", lambda h: h.matmul(out, lhsT, rhs, start=start, stop=stop),
               reads=reads, writes=writes, inc=inc)

    def tr(out, in_, reads, writes, n=128):
        P.emit("pe", lambda h: h.transpose(out, in_, ident[0:n, 0:n]), reads=list(reads) + [R_c], writes=writes)

    def act(out, in_, func, reads, writes, bias=0.0, scale=1.0):
        P.emit("act", lambda h: h.activation(out, in_, func, bias=bias, scale=scale), reads=reads, writes=writes)

    def tt(out, a, b, op, reads, writes, e="dve"):
        P.emit(e, lambda h: h.tensor_tensor(out, a, b, op), reads=reads, writes=writes)

    def tsc(out, a, s1, s2, op0, op1, reads, writes, e="dve"):
        if s2 is None:
            P.emit(e, lambda h: h.tensor_scalar(out, a, s1, None, op0), reads=reads, writes=writes)
        else:
            P.emit(e, lambda h: h.tensor_scalar(out, a, s1, s2, op0, op1), reads=reads, writes=writes)

    def stt(out, a, s, b, op0, op1, reads, writes):
        P.emit("dve", lambda h: h.scalar_tensor_tensor(out, a, s, b, op0, op1), reads=reads, writes=writes)

    def cp(out, in_, reads, writes, e="act"):
        if e == "act":
            P.emit("act", lambda h: h.copy(out, in_), reads=reads, writes=writes)
        else:
            P.emit(e, lambda h: h.tensor_copy(out, in_), reads=reads, writes=writes)

    def dma(out, in_, reads, writes, sem_res, e="sp"):
        P.emit(e, lambda h: h.dma_start(out=out, in_=in_), reads=reads, writes=writes, dma=sem_res)

    def store(out, in_, src_res):
        srcs = src_res if isinstance(src_res, (list, tuple)) else [src_res]
        dma(out, in_, list(srcs), [R_out], srcs[0], e="pool")

    wcnt = [0]
    wlist = []
    wdma = [0]
    wcast = [0]
    NST = len(wst)
    NBF = len(wbf)

    def w_issue_dma(upto):
        while wdma[0] <= upto and wdma[0] < len(wlist):
            i = wdma[0]
            wdma[0] += 1
            for dst_fn, src in wlist[i][0]:
                dma(dst_fn(wst[i % NST]), src, [], [R_wst[i % NST]], R_wst[i % NST])

    cast_act_only = [False]

    def w_issue_cast(upto):
        while wcast[0] <= upto and wcast[0] < len(wlist):
            i = wcast[0]
            wcast[0] += 1
            n_el = wlist[i][1]
            hsz = n_el if cast_act_only[0] else (n_el // 2 + 127) // 128 * 128
            s_, b_ = i % NST, i % NBF
            P.emit("act", lambda h, s_=s_, b_=b_, hsz=hsz: h.copy(wbf[b_][:, 0:hsz], wst[s_][:, 0:hsz]),
                   reads=[R_wst[s_]], writes=[R_wbf[b_][0]])
            if hsz < n_el:
                P.emit("dve", lambda h, s_=s_, b_=b_, hsz=hsz, n_el=n_el: h.tensor_copy(wbf[b_][:, hsz:n_el], wst[s_][:, hsz:n_el]),
                       reads=[R_wst[s_]], writes=[R_wbf[b_][1]])

    def wload(pieces, n_el):
        k = wcnt[0]
        wcnt[0] += 1
        if P.dry:
            wlist.append((pieces, n_el))
            return wbf[0], R_wbf[0]
        w_issue_dma(k + 1)
        w_issue_cast(k + 1)
        w_issue_dma(k + NST)
        return wbf[k % NBF], R_wbf[k % NBF]

    def wview(w2d):
        return w2d.rearrange("(kc p) n -> p kc n", p=128)

    def std_tile(w2d, c0, nk=8, ncol=512):
        src = wview(w2d)[:, 0:nk, c0:c0 + ncol]
        return [(lambda t: t[:, 0:nk * ncol].rearrange("p (k n) -> p k n", k=nk), src)], nk * ncol

    def coltiles(j):
        return [(0, TB)] + ([(TB, NCOL)] if j == 0 else [])

    def ncols(j):
        return NCOL if j == 0 else TB

    def rmsnorm(j, gi, want_xn=True):
        nc_ = ncols(j)
        sq = v3(slot_bf(8), 8, NCOL)
        act(sq[:, :, 0:nc_], x_t[:, :, 0:nc_], AF.Square, [R_x], [R_sl[8]])
        for (c0, c1) in coltiles(j):
            b = bank()
            for kc in range(8):
                mm(banks[b][:, 0:c1 - c0], ones_bf[:, :], sq[:, kc, c0:c1], kc == 0, kc == 7,
                   [R_sl[8], R_c], [R_bank[b]])
            act(rs_t[:, c0:c1], banks[b][:, 0:c1 - c0], AF.Ln, [R_bank[b]], [R_rs], bias=EPS, scale=1.0 / D)
            act(rs_t[:, c0:c1], rs_t[:, c0:c1], AF.Exp, [R_rs], [R_rs], scale=-0.5)
            for kc in range(8 if want_xn else 0):
                stt(xn_t[:, kc, c0:c1], x_t[:, kc, c0:c1], gains[:, gi, kc:kc + 1], rs_t[:, c0:c1],
                    ALU.mult, ALU.mult, [R_x, R_rs, R_c], [R_xn])

    def linear_fm(j, w2d, n_out_chunks, rhs3, rhs_res, evac, nk=8, oc0=0):
        ntile = (n_out_chunks + 3) // 4
        for t in range(ntile):
            wt, rw = wload(*std_tile(w2d, oc0 * 128 + t * 512, nk))
            w3 = wt[:, 0:nk * 512].rearrange("p (k n) -> p k n", k=nk)
            for oo in range(4):
                oc = oc0 + t * 4 + oo
                if oc >= oc0 + n_out_chunks:
                    break
                for (c0, c1) in coltiles(j):
                    b = bank()
                    for kc in range(nk):
                        mm(banks[b][:, 0:c1 - c0], w3[:, kc, oo * 128:(oo + 1) * 128], rhs3[:, kc, c0:c1],
                           kc == 0, kc == nk - 1, [rw, rhs_res], [R_bank[b]])
                    evac(oc, c0, c1, b)

    def ffn(j, l, which):
        gi = (0 if which == 0 else 2) + l
        rmsnorm(j, gi)
        wi = ffn_wi[which][l]
        wo = ffn_wo[which][l]
        hb = v3(slot_bf(0, 3), NHC, NCOL)
        R_h = R_sl[0:3]

        def in_tile(t):
            wv = wview(wi)
            pieces = [
                (lambda tl: tl[:, 0:4096].rearrange("p (k n) -> p k n", k=8)[:, :, 0:256], wv[:, :, t * 256:t * 256 + 256]),
                (lambda tl: tl[:, 0:4096].rearrange("p (k n) -> p k n", k=8)[:, :, 256:512],
                 wv[:, :, DFF + t * 256:DFF + t * 256 + 256]),
            ]
            return pieces, 4096

        for t in range(11):
            wt, rw = wload(*in_tile(t))
            w3 = wt[:, 0:4096].rearrange("p (k n) -> p k n", k=8)
            for hh in range(2):
                hc = 2 * t + hh
                for (c0, c1) in coltiles(j):
                    n = c1 - c0
                    bg = bank()
                    bu = bank()
                    for kc in range(8):
                        mm(banks[bg][:, 0:n], w3[:, kc, hh * 128:(hh + 1) * 128], xn_t[:, kc, c0:c1],
                           kc == 0, kc == 7, [rw, R_xn], [R_bank[bg]])
                    for kc in range(8):
                        mm(banks[bu][:, 0:n], w3[:, kc, 256 + hh * 128:256 + (hh + 1) * 128], xn_t[:, kc, c0:c1],
                           kc == 0, kc == 7, [rw, R_xn], [R_bank[bu]])
                    tm, rt = tmp()
                    act(tm[:, 0:n], banks[bg][:, 0:n], AF.Silu, [R_bank[bg]], [rt])
                    tt(hb[:, hc, c0:c1], tm[:, 0:n], banks[bu][:, 0:n], ALU.mult, [rt, R_bank[bu]], R_h)

        def out_tile(p, half):
            src = wview(wo)[:, half * 11:(half + 1) * 11, p * 256:(p + 1) * 256]
            return [(lambda tl: tl[:, 0:11 * 256].rearrange("p (k n) -> p k n", k=11), src)], 11 * 256

        seq = [(p, half) for p in range(4) for half in range(2)]
        bs = {}
        for i, (p, half) in enumerate(seq):
            wt, rw = wload(*out_tile(p, half))
            w3 = wt[:, 0:11 * 256].rearrange("p (k n) -> p k n", k=11)
            for fo in range(2):
                for (c0, c1) in coltiles(j):
                    n = c1 - c0
                    if half == 0:
                        bs[(fo, c0)] = bank()
                    b = bs[(fo, c0)]
                    for kk in range(11):
                        mm(banks[b][:, 0:n], w3[:, kk, fo * 128:(fo + 1) * 128], hb[:, half * 11 + kk, c0:c1],
                           half == 0 and kk == 0, half == 1 and kk == 10, [rw] + R_h, [R_bank[b]],
                           inc=(kk == 10))
                    if half == 1:
                        oc = 2 * p + fo
                        stt(x_t[:, oc, c0:c1], banks[b][:, 0:n], 0.5, x_t[:, oc, c0:c1], ALU.mult, ALU.add,
                            [R_bank[b], R_x], [R_x])

    def add_to_x(oc, c0, c1, b):
        tt(x_t[:, oc, c0:c1], banks[b][:, 0:c1 - c0], x_t[:, oc, c0:c1], ALU.add, [R_bank[b], R_x], [R_x])

    def store_T(src_fn, n, nchunk, dram2d, reads, stage_ap, stage_res):
        for g in range((nchunk + 3) // 4):
            b = bank()
            cs = range(g * 4, min(nchunk, g * 4 + 4))
            for c in cs:
                tr(banks[b][0:n, (c - g * 4) * 128:(c - g * 4 + 1) * 128], src_fn(c), reads, [R_bank[b]])
            w = len(cs) * 128
            cp(stage_ap[0:n, g * 512:g * 512 + w], banks[b][0:n, 0:w], [R_bank[b]], [stage_res])
        store(dram2d, stage_ap[0:n, 0:nchunk * 128], stage_res)

    def load_T(dst_fn, n, nchunk, dram2d, dst_res, stage_ap, stage_res):
        dma(stage_ap[0:n, 0:nchunk * 128], dram2d, [], [stage_res], stage_res)
        for c in range(nchunk):
            b = bank()
            tr(banks[b][:, 0:n], stage_ap[0:n, c * 128:(c + 1) * 128], [stage_res], [R_bank[b]], n=n)
            cp(dst_fn(c), banks[b][:, 0:n], [R_bank[b]], [dst_res])

    def setup():
        dma(ident[:, :], c_ident, [], [R_c], R_c)
        P.emit("pool", lambda h: h.memset(ones_f[:, :], 1.0), reads=[], writes=[R_c])
        cp(ident_bf[:, :], ident[:, :], [R_c], [R_c], e="dve")
        P.emit("pool", lambda h: h.memset(ones_bf[:, :], 1.0), reads=[], writes=[R_c])
        dma(ts_t[:, 0, :], c_mask[:, 0:1024], [], [R_ts], R_ts)
        dma(ts_t[:, 1, :], c_mask[:, 1024:2048], [], [R_ts], R_ts)
        cp(mask_bf[:, 0:2, :], ts_t[:, 0, :].rearrange("p (a b) -> p a b", a=2), [R_ts], [R_c], e="dve")
        cp(mask_bf[:, 2:4, :], ts_t[:, 1, :].rearrange("p (a b) -> p a b", a=2), [R_ts], [R_c], e="dve")
        dma(sel_t[:, :], c_sel, [], [R_c], R_c)
        dma(e0_t[:, :], c_e0, [], [R_c], R_c)
        dma(bmask_t[:, :], c_bmask, [], [R_c], R_c)
        dma(sel2_t[:, :], c_sel2, [], [R_c], R_c)
        dma(par_t[:, 0:2], c_par, [], [R_c], R_c)
        dma(iota_t[:, :], c_iota, [], [R_c], R_c)
        st = slot(0)
        glist = [ffn_g[0][0:1, :], ffn_g[0][1:2, :], ffn_g[1][0:1, :], ffn_g[1][1:2, :], mix_g[0:1, :], mix_g[1:2, :],
                 xattn_g[0:1, :], xattn_g[1:2, :], final_g[0:1, :]]
        for i, g in enumerate(glist):
            dma(st[i:i + 1, 0:D], g, [], [R_sl[0]], R_sl[0], e="pool")
        for c in range(8):
            b = bank()
            tr(banks[b][:, 0:9], st[0:9, c * 128:(c + 1) * 128], [R_sl[0]], [R_bank[b]], n=9)
            cp(gains[:, :, c], banks[b][:, 0:9], [R_bank[b]], [R_c])
        st1 = slot(1)
        vl = [conv_a_b, conv_a_ln_g, conv_a_ln_b, conv_b_b, lru_b_a, lru_b_x, lru_lambda]
        for i, v in enumerate(vl):
            dma(st1[i:i + 1, 0:512], v, [], [R_sl[1]], R_sl[1], e="pool")
        for c in range(4):
            b = bank()
            tr(banks[b][:, 0:7], st1[0:7, c * 128:(c + 1) * 128], [R_sl[1]], [R_bank[b]], n=7)
            cp(vecs[:, 0:7, c], banks[b][:, 0:7], [R_bank[b]], [R_c])
        st2 = slot(2)
        dma(st2[0:31, 0:512], conv_a_w[0], [], [R_sl[2]], R_sl[2])
        for c in range(4):
            b = bank()
            tr(banks[b][:, 0:31], st2[0:31, c * 128:(c + 1) * 128], [R_sl[2]], [R_bank[b]], n=31)
            cp(cw_a[:, c, 0:31], banks[b][:, 0:31], [R_bank[b]], [R_c])
        st3 = slot(3)
        dma(st3[0:4, 0:512], conv_b_w[0], [], [R_sl[3]], R_sl[3])
        for c in range(4):
            b = bank()
            tr(banks[b][:, 0:4], st3[0:4, c * 128:(c + 1) * 128], [R_sl[3]], [R_bank[b]], n=4)
            cp(cw_b[:, c, 0:4], banks[b][:, 0:4], [R_bank[b]], [R_c])
        st4 = v3(slot(4), 4, 128)
        st5 = v3(slot(5), 4, 128)
        P.emit("pool", lambda h: h.memset(slot(4)[:, 0:512], 0.0), reads=[], writes=[R_sl[4]])
        P.emit("pool", lambda h: h.memset(slot(5)[:, 0:512], 0.0), reads=[], writes=[R_sl[5]])
        for c in range(4):
            for hh in range(2):
                dma(st4[hh * 64:(hh + 1) * 64, c, hh * 64:(hh + 1) * 64], lru_w_a[0, 2 * c + hh], [], [R_sl[4]], R_sl[4])
                dma(st5[hh * 64:(hh + 1) * 64, c, hh * 64:(hh + 1) * 64], lru_w_x[0, 2 * c + hh], [], [R_sl[5]], R_sl[5])
        cp(wabd[:, :, :], st4, [R_sl[4]], [R_c], e="dve")
        cp(wxbd[:, :, :], st5, [R_sl[5]], [R_c], e="dve")
        lam_ = vecs[:, 6, :]
        a_ = small[:, 0:4]; e_ = small[:, 4:8]; z_ = small[:, 8:12]; z2 = small[:, 12:16]
        pl = small[:, 16:20]; t_ = small[:, 20:24]
        RS_ = [R_small, R_c]
        act(a_, lam_, AF.Abs, RS_, [R_small])
        act(e_, a_, AF.Exp, RS_, [R_small], scale=-1.0)
        tsc(t_, e_, 2.0, None, ALU.add, None, RS_, [R_small])
        P.emit("dve", lambda h: h.reciprocal(t_, t_), reads=RS_, writes=[R_small])
        tt(z_, e_, t_, ALU.mult, RS_, [R_small])
        tt(z2, z_, z_, ALU.mult, RS_, [R_small])
        tsc(pl, z2, 1.0 / 11.0, 1.0 / 9.0, ALU.mult, ALU.add, RS_, [R_small])
        for cf in (1.0 / 7.0, 1.0 / 5.0, 1.0 / 3.0, 1.0):
            tt(pl, pl, z2, ALU.mult, RS_, [R_small])
            tsc(pl, pl, cf, None, ALU.add, None, RS_, [R_small])
        tt(pl, pl, z_, ALU.mult, RS_, [R_small])
        tsc(t_, lam_, -1.0, 0.0, ALU.mult, ALU.max, RS_, [R_small])
        stt(pl, pl, 2.0, t_, ALU.mult, ALU.add, RS_, [R_small])
        tsc(vecs[:, 7, :], pl, -8.0, None, ALU.mult, None, RS_, [R_c])
        for i, v in enumerate((lam_q1, lam_k1, lam_q2, lam_k2)):
            dma(st[32:33, i * 64:(i + 1) * 64], v, [], [R_sl[0]], R_sl[0], e="pool")
        tt(st[32:33, 256:320], st[32:33, 0:64], st[32:33, 64:128], ALU.mult, [R_sl[0]], [R_sl[0]])
        tt(st[32:33, 320:384], st[32:33, 128:192], st[32:33, 192:256], ALU.mult, [R_sl[0]], [R_sl[0]])
        P.emit("dve", lambda h: h.reduce_sum(st[32:33, 384:386], st[32:33, 256:384].rearrange("p (a b) -> p a b", a=2), AX.X),
               reads=[R_sl[0]], writes=[R_sl[0]])
        act(st[32:33, 384:386], st[32:33, 384:386], AF.Exp, [R_sl[0]], [R_sl[0]])
        tt(st[32:33, 386:387], st[32:33, 384:385], st[32:33, 385:386], ALU.subtract, [R_sl[0]], [R_sl[0]])
        tsc(st[32:33, 387:388], st[32:33, 386:387], LAM_INIT, None, ALU.add, None, [R_sl[0]], [R_sl[0]])
        tsc(st[32:33, 388:389], st[32:33, 387:388], -1.0, None, ALU.mult, None, [R_sl[0]], [R_sl[0]])
        b = bank()
        mm(banks[b][:, 0:2], ones_f[32:33, :], st[32:33, 387:389], True, True, [R_sl[0], R_c], [R_bank[b]])
        cp(lamc[:, 0:2], banks[b][:, 0:2], [R_bank[b]], [R_lam])
        stt(par_t[:, 2:3], par_t[:, 1:2], lamc[0:16, 1:2], par_t[:, 0:1], ALU.mult, ALU.add, [R_c, R_lam], [R_c])
        tsc(sel2_t[:, :], sel2_t[:, :], par_t[:, 2:3], None, ALU.mult, None, [R_c], [R_c])
        dma(st[64:65, 0:128], attn_subln_g, [], [R_sl[0]], R_sl[0], e="pool")
        b = bank()
        mm(banks[b][:, 0:1], st[64:65, 0:128], ones_f[64:65, 0:1], True, True, [R_sl[0], R_c], [R_bank[b]])
        act(subg[:, :], banks[b][:, 0:1], AF.Copy, [R_bank[b]], [R_lam], scale=1.0 - LAM_INIT)
        P.emit("pool", lambda h: h.memset(ucarry[:, :, :], 0.0), reads=[], writes=[R_carry])
        P.emit("pool", lambda h: h.memset(rcarry[:, :, :], 0.0), reads=[], writes=[R_carry])
        P.emit("pool", lambda h: h.memset(hcarry[:, :], 0.0), reads=[], writes=[R_carry])
        dma(idx_t[:, :], pt.rearrange("s j -> (s j)").partition_broadcast(128), [], [R_idx], R_idx)
        cp(idxf_t[:, :], idx_t[:, :], [R_idx], [R_idx], e="dve")
        tsc(idxf_t[:, :], idxf_t[:, :], 128.0, iota_t[:, 0:1], ALU.mult, ALU.add, [R_idx, R_c], [R_idx])
        cp(idx_t[:, :], idxf_t[:, :], [R_idx], [R_idx], e="dve")

    def mem_setup():
        mt = v3(slot(0), 8, 256)
        mtb = v3(slot_bf(1), 8, 256)
        for r in range(2):
            load_T(lambda c, r=r: mt[:, c, r * 128:(r + 1) * 128], 128, 8, mp[r * 128:(r + 1) * 128, :], R_sl[0],
                   ts_t[:, r, :], R_ts)
        cp(mtb, mt, [R_sl[0]], [R_sl[1]], e="dve")
        ksub = int(os.environ.get("KSUB", "99"))
        if ksub < 1:
            return
        for l in range(2):
            w2d = xattn_w_kv[l]
            for t in range(4):
                wt, rw = wload(*std_tile(w2d, t * 512))
                if ksub < 2:
                    continue
                w3 = wt[:, 0:4096].rearrange("p (k n) -> p k n", k=8)
                if t < 2:
                    for oo in range(4):
                        b = bank()
                        for kc in range(8):
                            mm(banks[b][:, 0:256], w3[:, kc, oo * 128:(oo + 1) * 128], mtb[:, kc, :], kc == 0, kc == 7,
                               [rw, R_sl[1]], [R_bank[b]])
                        cp(memKT[:, l, t * 4 + oo, :], banks[b][:, 0:256], [R_bank[b]], [R_memKT])
                for r in range(2):
                    b = bank()
                    for kc in range(8):
                        mm(banks[b][:, :], mtb[:, kc, r * 128:(r + 1) * 128], w3[:, kc, :], kc == 0, kc == 7,
                           [rw, R_sl[1]], [R_bank[b]])
                    stg = v3(slot(2 + (t % 2)), 2, 512)
                    cp(stg[:, r, :], banks[b][:, :], [R_bank[b]], [R_sl[2 + (t % 2)]])
                    if ksub < 3:
                        continue
                    if t >= 2:
                        cp(memV[:, l, r, (t - 2) * 512:(t - 1) * 512], banks[b][:, :], [R_bank[b]], [R_memV])
                    dst = (o_p_mem_k if t < 2 else o_p_mem_v)[l, r * 128:(r + 1) * 128, (t % 2) * 512:(t % 2 + 1) * 512]
                    if ksub < 4:
                        continue
                    store(dst, stg[:, r, :], R_sl[2 + (t % 2)])

    def load_block(j):
        dma(ts_t[:, :, :], xp[j * TB:(j + 1) * TB, :].rearrange("(s p) f -> p s f", p=128), [], [R_ts], R_ts)
        for c in range(8):
            b = bank()
            for s in range(4):
                tr(banks[b][:, s * 128:(s + 1) * 128], ts_t[:, s, c * 128:(c + 1) * 128], [R_ts], [R_bank[b]])
            cp(x_t[:, c, 0:TB], banks[b][:, :], [R_bank[b]], [R_x], e=("act" if c % 2 else "dve"))
        if j == 0:
            load_T(lambda c: x_t[:, c, TB:NCOL], NS, 8, xs, R_x, slot(0), R_sl[0])

    def final_block(j):
        rmsnorm(j, 8, want_xn=False)
        nc_ = ncols(j)
        yt = v3(slot(0, 2), 8, NCOL)
        for kc in range(8):
            stt(yt[:, kc, 0:nc_], x_t[:, kc, 0:nc_], gains[:, 8, kc:kc + 1], rs_t[:, 0:nc_], ALU.mult, ALU.mult,
                [R_x, R_rs, R_c], [R_sl[0], R_sl[1]])
        for s in range(4):
            for g in range(2):
                b = bank()
                for c in range(4):
                    tr(banks[b][:, c * 128:(c + 1) * 128], yt[:, g * 4 + c, s * 128:(s + 1) * 128], [R_sl[0], R_sl[1]],
                       [R_bank[b]])
                cp(ts_t[:, s, g * 512:(g + 1) * 512], banks[b][:, :], [R_bank[b]], [R_ts], e=("act" if g else "dve"))
        store(o_y_p[j * TB:(j + 1) * TB, :].rearrange("(s p) f -> p s f", p=128), ts_t[:, :, :], R_ts)
        if j == 0:
            store_T(lambda c: yt[:, c, TB:NCOL], NS, 8, o_y_s, [R_sl[0], R_sl[1]], slot(2), R_sl[2])

    def even_mixer(j):
        rmsnorm(j, 4)
        nc_ = ncols(j)
        W = TB + 30
        aval = v3(slot(0), 4, NCOL);  R_av = R_sl[0]
        uext = v3(slot(1), 4, W);     R_u = R_sl[1]
        rext = v3(slot(2), 4, TB + 3); R_r = R_sl[2]
        bg = v3(slot(3), 4, NCOL);    R_bg = R_sl[3]
        xr = v3(slot(4), 4, NCOL);    R_xr = R_sl[4]
        m5 = v3(slot(5), 4, NCOL);    R_5 = R_sl[5]
        m6 = v3(slot(6), 4, NCOL);    R_6 = R_sl[6]
        m7 = v3(slot(7), 4, NCOL);    R_7 = R_sl[7]
        mixin = v3(slot_bf(8), 8, NCOL); R_mi = R_sl[8]
        cp(uext[:, :, 0:30], ucarry[:, :, :], [R_carry], [R_u], e="pool")
        cp(rext[:, :, 0:3], rcarry[:, :, :], [R_carry], [R_r], e="pool")
        if j == 0:
            for s in range(NS):
                load_T(lambda c, s=s: samp[:, c, s, 0:30], 30, 4, sca[s], R_samp, slot(5), R_sl[5])
                load_T(lambda c, s=s: sampb[:, c, s, 0:3], 3, 4, scb[s], R_samp, slot(6), R_sl[6])
                if "d2d" not in os.environ.get("KSKIP", ""):
                    dma(o_s_conv_a[s, 0:29, :], sca[s, 1:30, :], [], [R_out], R_out)
                    dma(o_s_conv_b[s, 0:2, :], scb[s, 1:3, :], [], [R_out], R_out)
            load_T(lambda c: samph[:, c, :], NS, 4, slru, R_samp, slot(7), R_sl[7])

        def evac(oc, c0, c1, b):
            q, c = oc // 4, oc % 4
            n = c1 - c0
            samp_tile = c0 >= TB
            if q == 0:
                cp(aval[:, c, c0:c1], banks[b][:, 0:n], [R_bank[b]], [R_av])
            elif q == 1:
                tm, rt = tmp()
                act(tm[:, 0:n], banks[b][:, 0:n], AF.Sigmoid, [R_bank[b]], [rt])
                if samp_tile:
                    tt(samp[:, c, :, 30], tm[:, 0:n], aval[:, c, c0:c1], ALU.mult, [rt, R_av], [R_samp])
                else:
                    tt(uext[:, c, 30:30 + TB], tm[:, 0:n], aval[:, c, c0:c1], ALU.mult, [rt, R_av], [R_u])
            elif q == 2:
                if samp_tile:
                    cp(sampb[:, c, :, 3], banks[b][:, 0:n], [R_bank[b]], [R_samp])
                else:
                    cp(rext[:, c, 3:3 + TB], banks[b][:, 0:n], [R_bank[b]], [R_r])
            else:
                cp(bg[:, c, c0:c1], banks[b][:, 0:n], [R_bank[b]], [R_bg])

        linear_fm(j, even_w_in[0], 8, xn_t, R_xn, evac)
        acc = aval
        ubf = v3(slot_bf(5), 4, W)
        cp(ubf, uext, [R_u], [R_5])
        dgs = slot_bf(6)
        di = 0
        for c in range(4):
            b = bank()
            for w in range(31):
                dg = dgs[:, (di % 4) * 128:(di % 4 + 1) * 128]
                rd = R_dg[di % 4]
                di += 1
                tsc(dg, ident_bf[:, :], cw_a[:, c, w:w + 1], None, ALU.mult, None, [R_c], [rd, R_6])
                mm(banks[b][:, :], dg, ubf[:, c, w:w + TB], w == 0, w == 30, [rd, R_5], [R_bank[b]], inc=True)
            tsc(acc[:, c, 0:TB], banks[b][:, :], vecs[:, 0, c:c + 1], None, ALU.add, None, [R_bank[b], R_c], [R_av])
        if j == 0:
            for c in range(4):
                pr = m7[:, c, 0:NS * 31].rearrange("p (s w) -> p s w", s=NS)
                tt(pr, samp[:, c, :, 0:31], cw_a[:, c:c + 1, 0:31].to_broadcast([128, NS, 31]), ALU.mult,
                   [R_samp, R_c], [R_7])
                P.emit("dve", lambda h, c=c, pr=pr: h.reduce_sum(acc[:, c, TB:NCOL], pr, AX.X), reads=[R_7], writes=[R_av])
                tsc(acc[:, c, TB:NCOL], acc[:, c, TB:NCOL], vecs[:, 0, c:c + 1], None, ALU.add, None, [R_av, R_c], [R_av])
        cast_act_only[0] = True
        linear_fm(j, even_w_in[0], 8, xn_t, R_xn, evac, oc0=8)
        cast_act_only[0] = False
        sqf = m7
        act(sqf[:, :, 0:nc_], acc[:, :, 0:nc_], AF.Square, [R_av], [R_7])
        for (c0, c1) in coltiles(j):
            n = c1 - c0
            b1 = bank(); b2 = bank()
            for c in range(4):
                mm(banks[b1][:, 0:n], ones_f[:, :], acc[:, c, c0:c1], c == 0, c == 3, [R_av, R_c], [R_bank[b1]])
            for c in range(4):
                mm(banks[b2][:, 0:n], ones_f[:, :], sqf[:, c, c0:c1], c == 0, c == 3, [R_7, R_c], [R_bank[b2]])
            mu = m5[:, 0, c0:c1]; var = m5[:, 1, c0:c1]; t1 = m5[:, 2, c0:c1]
            act(mu, banks[b1][:, 0:n], AF.Copy, [R_bank[b1]], [R_5], scale=1.0 / 512)
            tt(t1, mu, mu, ALU.mult, [R_5], [R_5])
            stt(var, banks[b2][:, 0:n], 1.0 / 512, t1, ALU.mult, ALU.subtract, [R_bank[b2], R_5], [R_5])
            act(var, var, AF.Ln, [R_5], [R_5], bias=EPS)
            act(var, var, AF.Exp, [R_5], [R_5], scale=-0.5)
            for c in range(4):
                tt(m6[:, c, c0:c1], acc[:, c, c0:c1], mu, ALU.subtract, [R_av, R_5], [R_6])
                tt(m6[:, c, c0:c1], m6[:, c, c0:c1], var, ALU.mult, [R_6, R_5], [R_6])
                P.emit("act", lambda h, c=c, c0=c0, c1=c1: h.activation(
                    mixin[:, c, c0:c1], m6[:, c, c0:c1], AF.Silu, bias=vecs[:, 2, c:c + 1], scale=vecs[:, 1, c:c + 1]),
                    reads=[R_6, R_c], writes=[R_mi])
        for c in range(4):
            tsc(xr[:, c, 0:TB], rext[:, c, 0:TB], cw_b[:, c, 0:1], vecs[:, 3, c:c + 1], ALU.mult, ALU.add, [R_r, R_c], [R_xr])
            for w in range(1, 4):
                stt(xr[:, c, 0:TB], rext[:, c, w:w + TB], cw_b[:, c, w:w + 1], xr[:, c, 0:TB], ALU.mult, ALU.add,
                    [R_r, R_c, R_xr], [R_xr])
        if j == 0:
            for c in range(4):
                pr = m7[:, c, 0:NS * 4].rearrange("p (s w) -> p s w", s=NS)
                tt(pr, sampb[:, c, :, 0:4], cw_b[:, c:c + 1, 0:4].to_broadcast([128, NS, 4]), ALU.mult, [R_samp, R_c], [R_7])
                P.emit("dve", lambda h, c=c, pr=pr: h.reduce_sum(xr[:, c, TB:NCOL], pr, AX.X), reads=[R_7], writes=[R_xr])
                tsc(xr[:, c, TB:NCOL], xr[:, c, TB:NCOL], vecs[:, 3, c:c + 1], None, ALU.add, None, [R_xr, R_c], [R_xr])
        xrb = v3(slot_bf(7), 4, NCOL)
        cp(xrb[:, :, 0:nc_], xr[:, :, 0:nc_], [R_xr], [R_7])
        for c in range(4):
            for (c0, c1) in coltiles(j):
                n = c1 - c0
                b1 = bank(); b2 = bank()
                mm(banks[b1][:, 0:n], wabd[:, c, :], xrb[:, c, c0:c1], True, True, [R_7, R_c], [R_bank[b1]])
                mm(banks[b2][:, 0:n], wxbd[:, c, :], xrb[:, c, c0:c1], True, True, [R_7, R_c], [R_bank[b2]])
                P.emit("act", lambda h, c=c, c0=c0, c1=c1, b1=b1, n=n: h.activation(
                    m5[:, c, c0:c1], banks[b1][:, 0:n], AF.Sigmoid, bias=vecs[:, 4, c:c + 1]),
                    reads=[R_bank[b1], R_c], writes=[R_5])
                P.emit("act", lambda h, c=c, c0=c0, c1=c1, b2=b2, n=n: h.activation(
                    m6[:, c, c0:c1], banks[b2][:, 0:n], AF.Sigmoid, bias=vecs[:, 5, c:c + 1]),
                    reads=[R_bank[b2], R_c], writes=[R_6])
        for c in range(4):
            P.emit("act", lambda h, c=c: h.activation(m5[:, c, 0:nc_], m5[:, c, 0:nc_], AF.Exp, scale=vecs[:, 7, c:c + 1]),
                   reads=[R_5, R_c], writes=[R_5])
        bt = v3(slot(7), 4, NCOL)
        tt(bt[:, :, 0:nc_], m5[:, :, 0:nc_], m5[:, :, 0:nc_], ALU.mult, [R_5], [R_7])
        tsc(bt[:, :, 0:nc_], bt[:, :, 0:nc_], -1.0, 1.0, ALU.mult, ALU.add, [R_7], [R_7])
        tsc(bt[:, :, 0:nc_], bt[:, :, 0:nc_], 1e-30, None, ALU.max, None, [R_7], [R_7])
        act(bt[:, :, 0:nc_], bt[:, :, 0:nc_], AF.Ln, [R_7], [R_7])
        act(bt[:, :, 0:nc_], bt[:, :, 0:nc_], AF.Exp, [R_7], [R_7], scale=0.5)
        tt(m6[:, :, 0:nc_], m6[:, :, 0:nc_], bt[:, :, 0:nc_], ALU.mult, [R_6, R_7], [R_6])
        tt(m6[:, :, 0:nc_], m6[:, :, 0:nc_], xr[:, :, 0:nc_], ALU.mult, [R_6, R_xr], [R_6])
        hl = bt
        for c in range(4):
            P.emit("dve", lambda h, c=c: h.tensor_tensor_scan(hl[:, c, 0:TB], m5[:, c, 0:TB], m6[:, c, 0:TB],
                                                            hcarry[:, c:c + 1], ALU.mult, ALU.add),
                   reads=[R_5, R_6, R_carry], writes=[R_7])
        if j == 0:
            for c in range(4):
                tt(hl[:, c, TB:NCOL], m5[:, c, TB:NCOL], samph[:, c, :], ALU.mult, [R_5, R_samp], [R_7])
                tt(hl[:, c, TB:NCOL], hl[:, c, TB:NCOL], m6[:, c, TB:NCOL], ALU.add, [R_7, R_6], [R_7])
        cp(hcarry[:, :], hl[:, :, TB - 1], [R_7], [R_carry], e="pool")
        cp(ucarry[:, :, :], uext[:, :, TB:TB + 30], [R_u], [R_carry], e="pool")
        cp(rcarry[:, :, :], rext[:, :, TB:TB + 3], [R_r], [R_carry], e="pool")
        g5 = m5
        tt(g5[:, :, 0:nc_], bg[:, :, 0:nc_], bg[:, :, 0:nc_], ALU.mult, [R_bg], [R_5])
        tsc(g5[:, :, 0:nc_], g5[:, :, 0:nc_], 0.044715, 1.0, ALU.mult, ALU.add, [R_5], [R_5])
        tt(g5[:, :, 0:nc_], g5[:, :, 0:nc_], bg[:, :, 0:nc_], ALU.mult, [R_5, R_bg], [R_5])
        act(g5[:, :, 0:nc_], g5[:, :, 0:nc_], AF.Sigmoid, [R_5], [R_5], scale=1.5957691216057308)
        tt(g5[:, :, 0:nc_], g5[:, :, 0:nc_], bg[:, :, 0:nc_], ALU.mult, [R_5, R_bg], [R_5])
        tt(mixin[:, 4:8, 0:nc_], hl[:, :, 0:nc_], g5[:, :, 0:nc_], ALU.mult, [R_7, R_5], [R_mi])
        if j == NBLK - 1:
            store_T(lambda c: uext[:, c, TB:TB + 30], 30, 4, o_p_conv_a, [R_u], slot(3), R_sl[3])
            store_T(lambda c: rext[:, c, TB:TB + 3], 3, 4, o_p_conv_b, [R_r], slot(4), R_sl[4])
            store_T(lambda c: hl[:, c, TB - 1:TB], 1, 4, o_p_lru, [R_7], slot(6), R_sl[6])
        if j == 0:
            store_T(lambda c: samp[:, c, :, 30], NS, 4, o_s_conv_a[:, 29, :], [R_samp], slot(3), R_sl[3])
            store_T(lambda c: sampb[:, c, :, 3], NS, 4, o_s_conv_b[:, 2, :], [R_samp], slot(4), R_sl[4])
            store_T(lambda c: hl[:, c, TB:NCOL], NS, 4, o_s_lru, [R_7], slot(6), R_sl[6])
        linear_fm(j, even_w_out[0], 8, mixin, R_mi, add_to_x)

    def xattn_core(q3, R_q, c0, c1, kT, R_k, vv, R_v, o3, R_o, pslot):
        n = c1 - c0
        pT = slot_bf(pslot)
        for hd in range(4):
            R_p = R_xp[hd]
            pts = []
            bl = bank()
            for mt_ in range(2):
                b = bank()
                for dc in range(2):
                    mm(banks[b][:, 0:n], kT[:, hd * 2 + dc, mt_ * 128:(mt_ + 1) * 128], q3[:, hd * 2 + dc, c0:c1],
                       dc == 0, dc == 1, [R_k, R_q], [R_bank[b]])
                pt_ = pT[:, hd * 1024 + mt_ * 512:hd * 1024 + mt_ * 512 + n]
                act(pt_, banks[b][:, 0:n], AF.Exp, [R_bank[b]], [R_p, R_sl[pslot]], scale=1.0 / 16.0)
                pts.append(pt_)
            for mt_ in range(2):
                mm(banks[bl][:, 0:n], ones_bf[:, :], pts[mt_], mt_ == 0, mt_ == 1, [R_p, R_c], [R_bank[bl]])
            tm, rt = tmp()
            act(tm[:, 0:n], banks[bl][:, 0:n], AF.Ln, [R_bank[bl]], [rt])
            act(tm[:, 0:n], tm[:, 0:n], AF.Exp, [rt], [rt], scale=-1.0)
            for ec in range(2):
                b = bank()
                for mt_ in range(2):
                    mm(banks[b][:, 0:n], vv[:, mt_, hd * 256 + ec * 128:hd * 256 + (ec + 1) * 128], pts[mt_],
                       mt_ == 0, mt_ == 1, [R_v, R_p], [R_bank[b]])
                tt(o3[:, hd * 2 + ec, c0:c1], banks[b][:, 0:n], tm[:, 0:n], ALU.mult, [R_bank[b], rt], [R_o])

    def xattn(j, l):
        rmsnorm(j, 6 + l)
        q3 = v3(slot_bf(0), 8, NCOL); R_q = R_sl[0]
        o3 = v3(slot_bf(1), 8, NCOL); R_o = R_sl[1]

        def evq(oc, c0, c1, b):
            cp(q3[:, oc, c0:c1], banks[b][:, 0:c1 - c0], [R_bank[b]], [R_q])

        linear_fm(j, xattn_w_q[l], 8, xn_t, R_xn, evq)
        xattn_core(q3, R_q, 0, TB, memKT[:, l], R_memKT, memV[:, l], R_memV, o3, R_o, 2)
        if j == 0:
            for s in range(NS):
                kTs = v3(slot_bf(3), 8, 256)
                vvs = v3(slot_bf(4), 2, D)
                kst = v3(slot(5, 2), 2, D)
                vst = v3(slot(7), 2, D)
                dma(kst, cmk[l, s].rearrange("(r p) f -> p r f", p=128), [], [R_sl[5], R_sl[6]], R_sl[5])
                dma(vst, cmv[l, s].rearrange("(r p) f -> p r f", p=128), [], [R_sl[7]], R_sl[7])
                for c in range(8):
                    b = bank()
                    for r in range(2):
                        tr(banks[b][:, r * 128:(r + 1) * 128], kst[:, r, c * 128:(c + 1) * 128], [R_sl[5], R_sl[6]], [R_bank[b]])
                    cp(kTs[:, c, :], banks[b][:, 0:256], [R_bank[b]], [R_sl[3]], e=("act" if c % 2 else "dve"))
                cp(vvs, vst, [R_sl[7]], [R_sl[4]], e="pool")
                xattn_core(q3, R_q, TB + s, TB + s + 1, kTs, R_sl[3], vvs, R_sl[4], o3, R_o, 2)
        linear_fm(j, xattn_w_out[l], 8, o3, R_o, add_to_x)

    R_pT = [Res("pT%d" % i) for i in range(4)]
    R_xp = [Res("xp%d" % i) for i in range(4)]

    def attention(j):
        rmsnorm(j, 5)
        nc_ = ncols(j)
        qT = v3(slot_bf(0), 8, NCOL); R_q = R_sl[0]
        att = v3(slot_bf(1), 8, NCOL); R_att = R_sl[1]
        kTn = v3(slot_bf(2), 8, TB); R_kn = R_sl[2]
        wi = attn_w_in[0]
        nkt = 4 * (j + 1)

        def evq(oc, c0, c1, b):
            cp(qT[:, oc, c0:c1], banks[b][:, 0:c1 - c0], [R_bank[b]], [R_q])

        linear_fm(j, wi[:, 0:D], 8, xn_t, R_xn, evq)
        ktok = ts_t; vtok = v3(slot(5, 2), 4, D); vbf = v3(slot_bf(7), 4, D)
        def stok(col, n=512):
            if col < 2 * D:
                return slot(3)[0:NS, col:col + n], R_sl[3]
            return slot(4)[0:NS, col - 2 * D:col - 2 * D + n], R_sl[4]
        for part in range(2):
            for t in range(2):
                col = D + part * D + t * 512
                wt, rw = wload(*std_tile(wi, col))
                w3 = wt[:, 0:4096].rearrange("p (k n) -> p k n", k=8)
                if part == 0:
                    for oo in range(4):
                        b = bank()
                        for kc in range(8):
                            mm(banks[b][:, :], w3[:, kc, oo * 128:(oo + 1) * 128], xn_t[:, kc, 0:TB], kc == 0, kc == 7,
                               [rw, R_xn], [R_bank[b]])
                        cp(kTn[:, t * 4 + oo, :], banks[b][:, :], [R_bank[b]], [R_kn])
                for s in range(4):
                    b = bank()
                    for kc in range(8):
                        mm(banks[b][:, :], xn_t[:, kc, s * 128:(s + 1) * 128], w3[:, kc, :], kc == 0, kc == 7,
                           [rw, R_xn], [R_bank[b]])
                    if part == 0:
                        cp(ktok[:, s, t * 512:(t + 1) * 512], banks[b][:, :], [R_bank[b]], [R_ts])
                    else:
                        cp(vtok[:, s, t * 512:(t + 1) * 512], banks[b][:, :], [R_bank[b]], [R_sl[5], R_sl[6]])
                        cp(vbf[:, s, t * 512:(t + 1) * 512], banks[b][:, :], [R_bank[b]], [R_sl[7]], e="dve")
                if j == 0:
                    b = bank()
                    for kc in range(8):
                        mm(banks[b][0:NS, :], xn_t[:, kc, TB:NCOL], w3[:, kc, :], kc == 0, kc == 7, [rw, R_xn], [R_bank[b]])
                    sap, sres = stok(col)
                    cp(sap, banks[b][0:NS, :], [R_bank[b]], [sres])
        if j == 0:
            for t in range(2):
                wt, rw = wload(*std_tile(wi, t * 512))
                w3 = wt[:, 0:4096].rearrange("p (k n) -> p k n", k=8)
                b = bank()
                for kc in range(8):
                    mm(banks[b][0:NS, :], xn_t[:, kc, TB:NCOL], w3[:, kc, :], kc == 0, kc == 7, [rw, R_xn], [R_bank[b]])
                sap, sres = stok(t * 512)
                cp(sap, banks[b][0:NS, :], [R_bank[b]], [sres])
            store(o_s_k, stok(D, D)[0], R_sl[3])
            store(o_s_v, stok(2 * D, D)[0], R_sl[4])
        rows = slice(j * TB, (j + 1) * TB)
        store(o_p_k[rows, :].rearrange("(s p) f -> p s f", p=128), ktok[:, :, :], R_ts)
        store(o_p_v[rows, :].rearrange("(s p) f -> p s f", p=128), vtok, [R_sl[5], R_sl[6]])
        dma(vscr[rows, :].rearrange("(s p) f -> p s f", p=128), vbf, [R_sl[7]], [R_vscr], R_sl[7], e="pool")
        dma(ktscr[:, :, rows].rearrange("h p t -> p h t"), kTn, [R_kn], [R_ktscr], R_kn, e="pool")
        nk = nkt * 128

        def kv_bufs(hd):
            if j == 0:
                return (slot_bf(8)[:, 2048:2560], R_sl[8], v3(slot_bf(8), 4, 128), R_sl[8])
            if hd % 2 == 0:
                return (slot_bf(3)[:, 0:SEQ], R_sl[3], v3(slot_bf(8), 32, 128), R_sl[8])
            return (slot_bf(2)[:, 0:SEQ], R_sl[2], v3(slot_bf(4), 32, 128), R_sl[4])

        def kv_load(hd):
            kh, R_kh, vh, R_vh = kv_bufs(hd)
            dma(kh[:, 0:nk], ktscr[hd, :, 0:nk], [R_ktscr], [R_kh], R_kh)
            dma(vh[:, 0:nkt, :], vscr[0:nk, hd * 128:(hd + 1) * 128].rearrange("(t p) e -> p t e", p=128),
                [R_vscr], [R_vh], R_vh)

        kv_load(0)
        for hd in range(8):
            kh, R_kh, vh, R_vh = kv_bufs(hd)
            if j == 0 and hd > 0:
                kv_load(hd)
            if hd + 1 < 8 and j > 0:
                kv_load(hd + 1)
            A = [None, None]
            steps = [(m, kt) for m in range(2) for kt in range(nkt)]

            def s_issue(i):
                m, kt = steps[i]
                bsn = i % 4
                mm(banks[bsn][:, :], kh[m * 64:(m + 1) * 64, kt * 128:(kt + 1) * 128], qT[m * 64:(m + 1) * 64, hd, 0:TB],
                   True, True, [R_kh, R_q], [R_bank[bsn]])

            LOOK = 3
            for i in range(min(LOOK, len(steps))):
                s_issue(i)
            for i, (m, kt) in enumerate(steps):
                if i + LOOK < len(steps):
                    s_issue(i + LOOK)
                bo = 4 + 2 * m
                bl = 5 + 2 * m
                bsn = i % 4
                pT = slot_bf(6)[:, (i % 4) * 512:(i % 4 + 1) * 512]
                R_p = R_pT[i % 4]
                act(pT, banks[bsn][:, :], AF.Exp, [R_bank[bsn]], [R_p, R_sl[6]], scale=0.125)
                dk = kt - 4 * j
                if dk >= 0:
                    tt(pT, pT, mask_bf[:, dk, :], ALU.mult, [R_p, R_c], [R_p], e="pool")
                mm(banks[bo][:, :], vh[:, kt, :], pT, kt == 0, kt == nkt - 1, [R_vh, R_p], [R_bank[bo]])
                mm(banks[bl][:, :], ones_bf[:, :], pT, kt == 0, kt == nkt - 1, [R_p, R_c], [R_bank[bl]])
                if kt == nkt - 1:
                    tm, rt = tmp()
                    act(tm[:, :], banks[bl][:, :], AF.Ln, [R_bank[bl]], [rt])
                    act(tm[:, :], tm[:, :], AF.Exp, [rt], [rt], scale=-1.0)
                    am = slot(5)[:, m * 512:(m + 1) * 512]
                    tt(am, banks[bo][:, :], tm[:, :], ALU.mult, [R_bank[bo], rt], [R_sl[5]])
                    A[m] = am
            dd = slot(5)[:, 1024:1536]
            stt(dd, A[1], lamc[:, 1:2], A[0], ALU.mult, ALU.add, [R_sl[5], R_lam], [R_sl[5]])
            sqd = slot_bf(5)[:, 3072:3584]
            act(sqd, dd, AF.Square, [R_sl[5]], [R_sl[5]])
            b = bank_ptr[0] % 4
            bank_ptr[0] += 1
            mm(banks[b][:, :], ones_bf[:, :], sqd, True, True, [R_sl[5], R_c], [R_bank[b]])
            tm, rt = tmp()
            act(tm[:, :], banks[b][:, :], AF.Ln, [R_bank[b]], [rt], bias=EPS, scale=1.0 / 128)
            act(tm[:, :], tm[:, :], AF.Exp, [rt], [rt], scale=-0.5)
            stt(att[:, hd, 0:TB], dd, subg[:, 0:1], tm[:, :], ALU.mult, ALU.mult, [R_sl[5], R_lam, rt], [R_att])
        if j == 0 and "satt" not in os.environ.get("KSKIP", ""):
            sample_attention(stok, att, R_att)
        linear_fm(j, attn_w_out[0], 8, att, R_att, add_to_x)

    R_pg = [Res("pg%d" % i) for i in range(4)]

    def sample_attention(stok, att, R_att):
        ck_rows = ck
        cv_rows = cv
        S_all = v3(slot(4)[:, 1100:2192], NPAGE + 1, 16)
        R_S = R_sl[4]
        attok = slot(2)
        R_at = R_sl[2]
        bat = [6, 7]
        for s in range(NS):
            qkv = slot(5, 2)
            R_qkv = [R_sl[5], R_sl[6]]
            for t in range(6):
                b = bank_ptr[0] % 4
                bank_ptr[0] += 1
                sap, sres = stok(t * 512)
                mm(banks[b][:, :], sel_t[0:NS, s * 128:(s + 1) * 128], sap, True, True,
                   [sres, R_c], [R_bank[b]])
                cp(qkv[:, t * 512:(t + 1) * 512], banks[b][:, :], [R_bank[b]], R_qkv, e=("act" if t % 2 else "dve"))
            qbc = qkv[:, 0:D]
            pgk = [slot(7)[:, 0:D], slot(8)[:, 0:D], slot(7)[:, D:2 * D], slot(8)[:, D:2 * D]]
            R_pgk = R_pg
            prod = slot(0)[:, 0:D]
            for pg in range(NPAGE + 1):
                if pg < NPAGE:
                    kb = pgk[pg % 4]
                    rk = R_pgk[pg % 4]
                    col = s * NPAGE + pg
                    P.emit("pool", lambda h, kb=kb, col=col: h.indirect_dma_start(
                        out=kb, out_offset=None, in_=ck_rows,
                        in_offset=bass.IndirectOffsetOnAxis(ap=idx_t[:, col:col + 1].bitcast(U32), axis=0)),
                        reads=[R_idx], writes=[rk], dma=rk, guards=[R_sl[7], R_sl[8]])
                    src, rsrc = kb, [rk]
                else:
                    src, rsrc = qkv[:, D:2 * D], R_qkv
                tt(prod, src, qbc, ALU.mult, rsrc + R_qkv, [R_sl[0]])
                P.emit("dve", lambda h, pg=pg: h.reduce_sum(S_all[:, pg, :], prod.rearrange("p (a d) -> p a d", d=64), AX.X),
                       reads=[R_sl[0]], writes=[R_S])
            mx = small[:, 32:48]
            P.emit("dve", lambda h: h.reduce_max(mx, S_all[:, :, :].rearrange("p g a -> p a g"), AX.X),
                   reads=[R_S], writes=[R_small])
            b = bank_ptr[0] % 4
            bank_ptr[0] += 1
            tr(banks[b][0:16, 0:128], mx, [R_small], [R_bank[b]])
            m16 = small[0:16, 48:49]
            P.emit("dve", lambda h, b=b: h.reduce_max(m16, banks[b][0:16, 0:128], AX.X), reads=[R_bank[b]], writes=[R_small])
            dg = slot(0)[0:16, 0:16]
            tsc(dg, ident[0:16, 0:16], m16, None, ALU.mult, None, [R_small, R_c], [R_sl[0]])
            b = bank_ptr[0] % 4
            bank_ptr[0] += 1
            mm(banks[b][:, 0:16], ones_f[0:16, :], dg, True, True, [R_sl[0], R_c], [R_bank[b]])
            mbc = small[:, 16:32]
            cp(mbc, banks[b][:, 0:16], [R_bank[b]], [R_small])
            tt(S_all[:, :, :], S_all[:, :, :], mbc.unsqueeze(1).to_broadcast([128, NPAGE + 1, 16]), ALU.subtract,
               [R_S, R_small], [R_S])
            act(S_all[:, :, :], S_all[:, :, :], AF.Exp, [R_S], [R_S], scale=0.125)
            tsc(S_all[:, NPAGE, :], S_all[:, NPAGE, :], e0_t[:, 0:1], None, ALU.mult, None, [R_S, R_c], [R_S])
            ps = small[:, 0:16]
            P.emit("dve", lambda h: h.reduce_sum(ps, S_all[:, :, :].rearrange("p g a -> p a g"), AX.X),
                   reads=[R_S], writes=[R_small])
            lq = small[0:16, 50:51]
            b = bank_ptr[0] % 4
            bank_ptr[0] += 1
            mm(banks[b][0:16, 0:1], ps, ones_f[:, 0:1], True, True, [R_small, R_c], [R_bank[b]])
            cp(lq, banks[b][0:16, 0:1], [R_bank[b]], [R_small])
            P.emit("dve", lambda h: h.reciprocal(lq, lq), reads=[R_small], writes=[R_small])
            Sv = S_all[:, :, :].rearrange("p g (h m) -> p g m h", m=2)
            for pg in range(NPAGE + 1):
                if pg < NPAGE:
                    vb = pgk[(pg + 1) % 4]
                    rv = R_pgk[(pg + 1) % 4]
                    col = s * NPAGE + pg
                    P.emit("pool", lambda h, vb=vb, col=col: h.indirect_dma_start(
                        out=vb, out_offset=None, in_=cv_rows,
                        in_offset=bass.IndirectOffsetOnAxis(ap=idx_t[:, col:col + 1].bitcast(U32), axis=0)),
                        reads=[R_idx], writes=[rv], dma=rv, guards=[R_sl[7], R_sl[8]])
                    src, rsrc = vb, [rv]
                else:
                    src, rsrc = qkv[:, 2 * D:3 * D], R_qkv
                for hf in range(2):
                    bb = 4 + hf
                    mm(banks[bb][0:16, :], S_all[:, pg, :], src[:, hf * 512:(hf + 1) * 512], pg == 0, pg == NPAGE,
                       [R_S] + rsrc, [R_bank[bb]], inc=(pg == NPAGE or hf == 1))
            dmat = slot(0)[0:16, 0:D]
            for hf in range(2):
                cs = slice(hf * 512, (hf + 1) * 512)
                stt(dmat[:, cs], banks[4 + hf][0:16, :], lq[:, 0:1], bmask_t[:, cs], ALU.mult, ALU.mult,
                    [R_bank[4 + hf], R_small, R_c], [R_sl[0]])
            for hf in range(2):
                b = bank_ptr[0] % 4
                bank_ptr[0] += 1
                mm(banks[b][0:NS, :], sel2_t[0:16, s * NS:(s + 1) * NS], dmat[:, hf * 512:(hf + 1) * 512], True, True,
                   [R_sl[0], R_c], [R_bank[b]])
                if s == 0:
                    cp(attok[0:NS, hf * 512:(hf + 1) * 512], banks[b][0:NS, :], [R_bank[b]], [R_at])
                else:
                    tt(attok[0:NS, hf * 512:(hf + 1) * 512], banks[b][0:NS, :], attok[0:NS, hf * 512:(hf + 1) * 512], ALU.add,
                       [R_bank[b], R_at], [R_at])
        P.emit("pool", lambda h: h.memset(slot(7)[:, 0:1], 0.0), reads=[], writes=[R_sl[7]] + R_pg)
        P.emit("pool", lambda h: h.memset(slot(8)[:, 0:1], 0.0), reads=[], writes=[R_sl[8]] + R_pg)
        a3 = attok[0:NS, 0:D].rearrange("p (h e) -> p h e", h=8)
        sq3 = slot(0)[0:NS, 0:D].rearrange("p (h e) -> p h e", h=8)
        tt(sq3, a3, a3, ALU.mult, [R_at], [R_sl[0]])
        ssum = small[0:NS, 52:60]
        P.emit("dve", lambda h: h.reduce_sum(ssum, sq3, AX.X), reads=[R_sl[0]], writes=[R_small])
        act(ssum, ssum, AF.Sqrt, [R_small], [R_small], bias=EPS, scale=1.0 / 128)
        P.emit("dve", lambda h: h.reciprocal(ssum, ssum), reads=[R_small], writes=[R_small])
        tt(a3, a3, ssum.unsqueeze(2).to_broadcast([NS, 8, 128]), ALU.mult, [R_at, R_small], [R_at])
        for hd in range(8):
            b = bank_ptr[0] % 4
            bank_ptr[0] += 1
            tr(banks[b][:, 0:NS], attok[0:NS, hd * 128:(hd + 1) * 128], [R_at], [R_bank[b]], n=NS)
            tsc(att[:, hd, TB:NCOL], banks[b][:, 0:NS], subg[:, 0:1], None, ALU.mult, None, [R_bank[b], R_lam], [R_att])


    kstop = int(os.environ.get("KSTOP", "1000000"))
    stages = [setup, mem_setup]
    for j in range(NBLK):
        stages += [lambda j=j: load_block(j), lambda j=j: ffn(j, 0, 0), lambda j=j: even_mixer(j), lambda j=j: xattn(j, 0),
                   lambda j=j: ffn(j, 0, 1), lambda j=j: ffn(j, 1, 0), lambda j=j: attention(j), lambda j=j: xattn(j, 1),
                   lambda j=j: ffn(j, 1, 1), lambda j=j: final_block(j)]
    for dry in (True, False):
        P.dry = dry
        bank_ptr[0] = 0
        tmp_ptr[0] = 0
        wcnt[0] = 0
        for i, st_ in enumerate(stages):
            if i >= kstop:
                break
            st_()
    P.emit("sp", lambda h: h.nop(), reads=[R_out], writes=[])

    with nc.Block() as block:
        @block.sync
        def _(e):
            P.replay("sp", e)

        @block.tensor
        def _(e):
            P.replay("pe", e)

        @block.scalar
        def _(e):
            P.replay("act", e)

        @block.vector
        def _(e):
            P.replay("dve", e)

        @block.gpsimd
        def _(e):
            P.replay("pool", e)
    return nc, es


_CACHE = {}


def _consts():
    ident = np.eye(128, dtype=np.float32)
    mask = np.zeros((128, 4, 512), np.float32)
    p = np.arange(128)[:, None]
    f = np.arange(512)[None, :]
    for dk in range(4):
        mask[:, dk, :] = (dk * 128 + p <= f).astype(np.float32)
    sel = np.zeros((NS, NS, 128), np.float32)
    for s in range(NS):
        sel[s, s, :] = 1.0
    e0 = np.zeros((128, 1), np.float32)
    e0[0, 0] = 1.0
    bmask = np.zeros((16, 8, 128), np.float32)
    for a in range(16):
        bmask[a, a // 2, :] = 1.0
    sel2 = np.zeros((16, NS, NS), np.float32)
    for s in range(NS):
        sel2[:, s, s] = 1.0
    par = np.zeros((16, 2), np.float32)
    par[0::2, 0] = 1.0
    par[1::2, 1] = 1.0
    iota = np.arange(128, dtype=np.float32).reshape(128, 1)
    return dict(c_ident=ident, c_mask=mask.reshape(128, 2048), c_sel=sel.reshape(NS, NS * 128), c_e0=e0,
                c_bmask=bmask.reshape(16, D), c_sel2=sel2.reshape(16, NS * NS), c_iota=iota, c_par=par)


def kernel(**inp):
    if "nc" not in _CACHE:
        _CACHE["nc"] = build_program()
    nc, _es = _CACHE["nc"]
    f = lambda a: np.ascontiguousarray(np.asarray(a))
    consts = _consts()
    wnames = ["ffn1_g", "ffn1_w_in", "ffn1_w_out", "ffn2_g", "ffn2_w_in", "ffn2_w_out", "mix_g", "even_w_in", "conv_a_w",
              "conv_a_b", "conv_a_ln_g", "conv_a_ln_b", "conv_b_w", "conv_b_b", "lru_w_a", "lru_b_a", "lru_w_x", "lru_b_x",
              "lru_lambda", "even_w_out", "attn_w_in", "lam_q1", "lam_k1", "lam_q2", "lam_k2", "attn_subln_g", "attn_w_out",
              "xattn_g", "xattn_w_q", "xattn_w_kv", "xattn_w_out"]
    shared = {n: f(inp[n]) for n in wnames}
    shared["final_g"] = f(inp["final_g"]).reshape(1, D)
    shared["ck"] = f(inp["cache_k"]).reshape(2560 * 128, D)
    shared["cv"] = f(inp["cache_v"]).reshape(2560 * 128, D)
    shared.update(consts)
    in_maps = []
    for c in range(8):
        b = c % 4
        m = dict(shared)
        m["xp"] = f(inp["x_prompt"][b])
        m["xs"] = f(inp["x_sample"][4 * c:4 * c + 4, 0])
        m["sca"] = f(inp["state_conv_a"][0, 4 * c:4 * c + 4])
        m["scb"] = f(inp["state_conv_b"][0, 4 * c:4 * c + 4])
        m["slru"] = f(inp["state_lru"][0, 4 * c:4 * c + 4])
        m["cmk"] = f(inp["cache_mem_k"][:, 4 * c:4 * c + 4]).reshape(2, NS, 256, D)
        m["cmv"] = f(inp["cache_mem_v"][:, 4 * c:4 * c + 4]).reshape(2, NS, 256, D)
        m["pt"] = f(inp["page_table"][4 * c:4 * c + 4]).astype(np.int32)
        m["mp"] = f(inp["mem_prompt"][b])
        in_maps.append(m)
    res = run_bass_kernel_spmd(nc, in_maps, core_ids=list(range(8))).results
    R = lambda name, cs: np.stack([res[c][name] for c in cs])
    p4 = range(4)
    a8 = range(8)
    y_p = R("o_y_p", p4)
    y_s = np.concatenate([res[c]["o_y_s"] for c in a8])[:, None, :]
    p_conv_a = R("o_p_conv_a", p4)[None]
    p_conv_b = R("o_p_conv_b", p4)[None]
    p_lru = R("o_p_lru", p4).reshape(1, 4, 512)
    p_k = R("o_p_k", p4).reshape(1, 4, SEQ_FULL, 8, 128)
    p_v = R("o_p_v", p4).reshape(1, 4, SEQ_FULL, 8, 128)
    p_mem_k = np.stack([res[c]["o_p_mem_k"] for c in p4], axis=1).reshape(2, 4, 256, 4, 256)
    p_mem_v = np.stack([res[c]["o_p_mem_v"] for c in p4], axis=1).reshape(2, 4, 256, 4, 256)
    s_conv_a = np.concatenate([res[c]["o_s_conv_a"] for c in a8])[None]
    s_conv_b = np.concatenate([res[c]["o_s_conv_b"] for c in a8])[None]
    s_lru = np.concatenate([res[c]["o_s_lru"] for c in a8])[None]
    s_k = np.concatenate([res[c]["o_s_k"] for c in a8]).reshape(1, 32, 1, 8, 128)
    s_v = np.concatenate([res[c]["o_s_v"] for c in a8]).reshape(1, 32, 1, 8, 128)
    outs = (y_p, y_s, p_conv_a, p_conv_b, p_lru, p_k, p_v, p_mem_k, p_mem_v, s_conv_a, s_conv_b, s_lru, s_k, s_v)
    return tuple(np.ascontiguousarray(o, dtype=np.float32) for o in outs)
```
